# Optimizing a Trainium2 kernel written in Bass

```python
import functools
import jax, jax.numpy as jnp
from jax import lax
import numpy as np

D_MODEL = 2048
BATCH = 4
SEQ = 2048
DEPTH = 1
DEC_BATCH = 32
DEC_SEQ = 4
PAST_LEN = 16384
PAGE_SIZE = 128

N_META = 16
D_CONV = D_MODEL // 2
CONV_WIDTH = 3
HEAD_DIM = 64
N_HEADS = (D_MODEL // 2) // HEAD_DIM
N_KV_HEADS = N_HEADS // 4
GROUP = N_HEADS // N_KV_HEADS
Q_DIM = N_HEADS * HEAD_DIM
KV_DIM = N_KV_HEADS * HEAD_DIM
WINDOW = 128
BLOCK = 128
D_FF = ((8 * D_MODEL // 3 + 127) // 128) * 128
N_BRANCH = 2
IN_SPLITS = (D_CONV, 2 * D_CONV, 3 * D_CONV, 3 * D_CONV + Q_DIM,
             3 * D_CONV + Q_DIM + KV_DIM, 3 * D_CONV + Q_DIM + 2 * KV_DIM,
             3 * D_CONV + Q_DIM + 2 * KV_DIM + D_MODEL)
D_IN = 3 * D_CONV + Q_DIM + 2 * KV_DIM + N_BRANCH * D_MODEL
EPS = 1e-6
NEG_INF = -1e30

kernel_name = "hybrid_conv_swa_macaron_step"


def rms_norm(x, g):
    xf = x.astype(jnp.float32)
    y = xf * lax.rsqrt(jnp.mean(xf * xf, axis=-1, keepdims=True) + EPS)
    return (y * g.astype(jnp.float32)).astype(x.dtype)


def half_ffn(x, g, w_up, w_down):
    h = rms_norm(x, g)
    gate, up = jnp.split(h @ w_up, 2, axis=-1)
    return x + 0.5 * ((jax.nn.silu(gate) * up) @ w_down)


def alibi_slopes():
    h = jnp.arange(1, N_HEADS + 1, dtype=jnp.float32)
    return jnp.exp2(-8.0 * h / N_HEADS).reshape(N_KV_HEADS, GROUP)


def project(h, w_in, q_g, k_g):
    z = h @ w_in
    xc, bg, cg, q, k, v, gc, ga = jnp.split(z, IN_SPLITS, axis=-1)
    lead = h.shape[:-1]
    q = rms_norm(q.reshape(*lead, N_KV_HEADS, GROUP, HEAD_DIM), q_g) * (HEAD_DIM ** -0.5)
    k = rms_norm(k.reshape(*lead, N_KV_HEADS, HEAD_DIM), k_g)
    v = v.reshape(*lead, N_KV_HEADS, HEAD_DIM)
    return xc, bg, cg, q, k, v, jax.nn.sigmoid(gc), jax.nn.sigmoid(ga)


def short_conv(xc, bg, cg, prefix, conv_w, w_out):
    u = cg * xc
    L = u.shape[1]
    up = jnp.concatenate([prefix.astype(u.dtype), u], axis=1)
    y = conv_w[0] * up[:, 0:L]
    for j in range(1, CONV_WIDTH):
        y = y + conv_w[j] * up[:, j:j + L]
    return (bg * y) @ w_out, up[:, L:]


def window_attention(q, k, v, pos_q, pos_k, sinks):
    s = jnp.einsum('...qkgd,...skd->...kgqs', q, k, preferred_element_type=jnp.float32)
    dist = (pos_q[..., :, None] - pos_k[..., None, :])[..., None, None, :, :]
    mask = (dist >= 0) & (dist <= WINDOW) & (pos_k[..., None, None, None, :] >= 0)
    s = jnp.where(mask, s - alibi_slopes()[:, :, None, None] * dist.astype(jnp.float32), NEG_INF)
    sink = jnp.broadcast_to(sinks.astype(jnp.float32).reshape(N_KV_HEADS, GROUP, 1, 1),
                            s.shape[:-1] + (1,))
    p = jax.nn.softmax(jnp.concatenate([s, sink], axis=-1), axis=-1)[..., :-1]
    return jnp.einsum('...kgqs,...skd->...qkgd', p.astype(v.dtype), v)


def prompt_attention(q, k, v, sinks):
    B, L = q.shape[0], q.shape[1]
    pad = BLOCK - N_META
    Lp = L + pad
    nb = Lp // BLOCK
    padf = lambda a: jnp.pad(a, ((0, 0), (pad, 0)) + ((0, 0),) * (a.ndim - 2))
    qb = padf(q).reshape(B, nb, BLOCK, N_KV_HEADS, GROUP, HEAD_DIM)
    kb = padf(k).reshape(B, nb, BLOCK, N_KV_HEADS, HEAD_DIM)
    vb = padf(v).reshape(B, nb, BLOCK, N_KV_HEADS, HEAD_DIM)
    shift = lambda a: jnp.concatenate([jnp.zeros_like(a[:, :1]), a[:, :-1]], axis=1)
    kk = jnp.concatenate([shift(kb), kb], axis=2)
    vv = jnp.concatenate([shift(vb), vb], axis=2)
    pos_q = (jnp.arange(Lp, dtype=jnp.int32) - pad).reshape(nb, BLOCK)
    pos_k = jnp.concatenate([pos_q - BLOCK, pos_q], axis=1)
    o = window_attention(qb, kk, vv, pos_q, pos_k, sinks)
    n_win = min(WINDOW, L)
    return o.reshape(B, Lp, Q_DIM)[:, pad:], k[:, L - n_win:], v[:, L - n_win:]


def sample_attention(q, k, v, cache_k, cache_v, sinks):
    Bd, T = q.shape[0], q.shape[1]
    n_buf = cache_k.shape[1]
    kk = jnp.concatenate([cache_k.astype(k.dtype), k], axis=1)
    vv = jnp.concatenate([cache_v.astype(v.dtype), v], axis=1)
    pos_q = jnp.arange(T, dtype=jnp.int32) + PAST_LEN
    pos_k = jnp.arange(n_buf + T, dtype=jnp.int32) + (PAST_LEN - n_buf)
    o = window_attention(q, kk, vv, pos_q, pos_k, sinks)
    return o.reshape(Bd, T, Q_DIM), kk[:, T:], vv[:, T:]


def layer_forward(x, conv_prefix, attend, g_ffn1, w_up1, w_down1, g_mix, w_in, q_g, k_g,
                  conv_w, w_conv_out, w_attn_out, w_o, g_ffn2, w_up2, w_down2):
    x = half_ffn(x, g_ffn1, w_up1, w_down1)
    h = rms_norm(x, g_mix)
    xc, bg, cg, q, k, v, gate_c, gate_a = project(h, w_in, q_g, k_g)
    conv_out, new_conv = short_conv(xc, bg, cg, conv_prefix, conv_w, w_conv_out)
    attn, new_k, new_v = attend(q, k, v)
    attn_out = attn @ w_attn_out
    x = x + (gate_c * conv_out + gate_a * attn_out) @ w_o
    x = half_ffn(x, g_ffn2, w_up2, w_down2)
    return x, new_conv, new_k, new_v


def setup_inputs(seed: int = 0) -> dict:
    key = jax.random.key(seed)
    ks = jax.random.split(key, 24)
    n_win = min(WINDOW, PAST_LEN)
    nrm = lambda k, shape, scale: jax.random.normal(k, shape, jnp.float32) * scale
    gain = lambda k, shape: 1.0 + 0.02 * jax.random.normal(k, shape, jnp.float32)
    return {
        "x_prompt": nrm(ks[0], (BATCH, SEQ, D_MODEL), 1.0),
        "x_sample": nrm(ks[1], (DEC_BATCH, DEC_SEQ, D_MODEL), 1.0),
        "state_conv": nrm(ks[2], (DEPTH, DEC_BATCH, CONV_WIDTH - 1, D_CONV), 1.0),
        "cache_k_win": nrm(ks[3], (DEPTH, DEC_BATCH, n_win, N_KV_HEADS, HEAD_DIM), 1.0),
        "cache_v_win": nrm(ks[4], (DEPTH, DEC_BATCH, n_win, N_KV_HEADS, HEAD_DIM), 1.0),
        "meta_tokens": nrm(ks[5], (N_META, D_MODEL), 1.0),
        "ffn1_norm": gain(ks[6], (DEPTH, D_MODEL)),
        "ffn1_w_up": nrm(ks[7], (DEPTH, D_MODEL, 2 * D_FF), D_MODEL ** -0.5),
        "ffn1_w_down": nrm(ks[8], (DEPTH, D_FF, D_MODEL), D_FF ** -0.5),
        "mix_norm": gain(ks[9], (DEPTH, D_MODEL)),
        "w_in": nrm(ks[10], (DEPTH, D_MODEL, D_IN), D_MODEL ** -0.5),
        "q_norm": gain(ks[11], (DEPTH, HEAD_DIM)),
        "k_norm": gain(ks[12], (DEPTH, HEAD_DIM)),
        "conv_w": nrm(ks[13], (DEPTH, CONV_WIDTH, D_CONV), CONV_WIDTH ** -0.5),
        "w_conv_out": nrm(ks[14], (DEPTH, D_CONV, D_MODEL), D_CONV ** -0.5),
        "attn_sinks": nrm(ks[15], (DEPTH, N_HEADS), 0.5),
        "w_attn_out": nrm(ks[16], (DEPTH, Q_DIM, D_MODEL), Q_DIM ** -0.5),
        "w_o": nrm(ks[17], (DEPTH, D_MODEL, D_MODEL), D_MODEL ** -0.5),
        "ffn2_norm": gain(ks[18], (DEPTH, D_MODEL)),
        "ffn2_w_up": nrm(ks[19], (DEPTH, D_MODEL, 2 * D_FF), D_MODEL ** -0.5),
        "ffn2_w_down": nrm(ks[20], (DEPTH, D_FF, D_MODEL), D_FF ** -0.5),
    }


def reference(x_prompt, x_sample, state_conv, cache_k_win, cache_v_win, meta_tokens,
              ffn1_norm, ffn1_w_up, ffn1_w_down, mix_norm, w_in, q_norm, k_norm,
              conv_w, w_conv_out, attn_sinks, w_attn_out, w_o,
              ffn2_norm, ffn2_w_up, ffn2_w_down):
    B = x_prompt.shape[0]
    meta = jnp.broadcast_to(meta_tokens[None].astype(x_prompt.dtype), (B, N_META, D_MODEL))
    xp = jnp.concatenate([meta, x_prompt], axis=1)
    xs = x_sample
    conv_p, k_p, v_p, conv_s, k_s, v_s = [], [], [], [], [], []
    for l in range(DEPTH):
        w = (ffn1_norm[l], ffn1_w_up[l], ffn1_w_down[l], mix_norm[l], w_in[l], q_norm[l], k_norm[l],
             conv_w[l], w_conv_out[l], w_attn_out[l], w_o[l], ffn2_norm[l], ffn2_w_up[l], ffn2_w_down[l])
        zero_prefix = jnp.zeros((B, CONV_WIDTH - 1, D_CONV), xp.dtype)
        xp, cp, kp, vp = layer_forward(
            xp, zero_prefix, functools.partial(prompt_attention, sinks=attn_sinks[l]), *w)
        xs, cs, ksn, vsn = layer_forward(
            xs, state_conv[l],
            functools.partial(sample_attention, cache_k=cache_k_win[l], cache_v=cache_v_win[l],
                              sinks=attn_sinks[l]), *w)
        conv_p.append(cp); k_p.append(kp); v_p.append(vp)
        conv_s.append(cs); k_s.append(ksn); v_s.append(vsn)
    y_prompt = xp[:, N_META:]
    y_sample = xs
    return (y_prompt, y_sample, jnp.stack(conv_p), jnp.stack(k_p), jnp.stack(v_p),
            jnp.stack(conv_s), jnp.stack(k_s), jnp.stack(v_s))
```

```python
import numpy as np
import concourse.bass as bass
import concourse.mybir as mybir
from concourse.bass_utils import run_bass_kernel_spmd

F32 = mybir.dt.float32
BF16 = mybir.dt.bfloat16
AF = mybir.ActivationFunctionType
ALU = mybir.AluOpType
AX = mybir.AxisListType

D = 2048
DFF = 5504
NCH = 43
DIN = 8704
EPS = 1e-6
NCORES = 8


class Op:
    __slots__ = ("eng", "fn", "reads", "writes", "ndma", "stream", "deps", "mile", "cum")

    def __init__(self, eng, fn, reads, writes, ndma, stream):
        self.eng, self.fn, self.reads, self.writes = eng, fn, reads, writes
        self.ndma, self.stream = ndma, stream
        self.deps = set()
        self.mile = 0
        self.cum = 0


class Sched:
    ENGS = ("pe", "act", "dve", "pool", "sp")

    def __init__(self):
        self.ops = []
        self.lw = {}
        self.rd = {}
        self.after = None
        self.last = {}
        self.dmas_since = []

    def barrier(self):
        deps = set(self.last.values()) | set(self.dmas_since)
        i = self.add("dve", self._bfn, nobar=True)
        self.ops[i].deps |= deps
        self.ops[i].deps.discard(i)
        self.after = i
        self.dmas_since = []

    def add(self, eng, fn, reads=(), writes=(), ndma=0, stream=None, nobar=False):
        reads = list(reads) + ([("R1a",), ("R1b",)] if ("R", 1) in reads else [])
        writes = list(writes) + ([("R1a",), ("R1b",)] if ("R", 1) in writes else [])
        op = Op(eng, fn, tuple(reads), tuple(writes), ndma, stream)
        i = len(self.ops)
        if self.after is not None and not nobar:
            op.deps.add(self.after)
        if ndma:
            self.dmas_since.append(i)
        else:
            self.last[eng] = i
        for k in op.reads:
            w = self.lw.get(k)
            if w is not None:
                op.deps.add(w)
            if k[0] == "ps":
                for e2, r in self.rd.get(k, {}).items():
                    if e2 != eng:
                        op.deps.add(r)
        for k in op.writes:
            w = self.lw.get(k)
            if w is not None:
                op.deps.add(w)
            for r in self.rd.get(k, {}).values():
                op.deps.add(r)
        for k in op.reads:
            self.rd.setdefault(k, {})[("dma", i) if ndma else eng] = i
        for k in op.writes:
            self.lw[k] = i
            self.rd[k] = {}
        op.deps.discard(i)
        self.ops.append(op)
        return i

    def emit(self, nc, final_streams):
        ops = self.ops
        need = [False] * len(ops)
        for i, op in enumerate(ops):
            for j in op.deps:
                pj = ops[j]
                if pj.ndma:
                    continue
                if pj.eng == "pe" and op.eng == "pe" and not op.ndma:
                    continue
                need[j] = True
        cnt = {e: 0 for e in self.ENGS}
        scum = {}
        for i, op in enumerate(ops):
            if op.ndma:
                scum[op.stream] = scum.get(op.stream, 0) + 16 * op.ndma
                op.cum = scum[op.stream]
            elif need[i]:
                cnt[op.eng] += 1
                op.mile = cnt[op.eng]
        streams = sorted(scum.keys(), key=str)
        import contextlib
        with contextlib.ExitStack() as es:
            esem = {e: es.enter_context(nc.semaphore("s_" + e)) for e in self.ENGS}
            ssem = {s: es.enter_context(nc.semaphore("d_" + str(n))) for n, s in enumerate(streams)}
            block = es.enter_context(nc.Block())
            per_eng = {e: [] for e in self.ENGS}
            for i, op in enumerate(ops):
                per_eng[op.eng].append(i)

            def run(e_name, eh):
                waited = {}
                for i in per_eng[e_name]:
                    op = ops[i]
                    reqs = {}
                    for j in op.deps:
                        pj = ops[j]
                        if pj.ndma:
                            key = ("s", pj.stream)
                            val = pj.cum
                        else:
                            if pj.eng == "pe" and e_name == "pe" and not op.ndma:
                                continue
                            key = ("e", pj.eng)
                            val = pj.mile
                        if val > reqs.get(key, 0):
                            reqs[key] = val
                    for key, val in reqs.items():
                        if waited.get(key, 0) >= val:
                            continue
                        waited[key] = val
                        sem = ssem[key[1]] if key[0] == "s" else esem[key[1]]
                        eh.wait_ge(sem, val)
                    r = op.fn(eh)
                    if op.ndma:
                        for ins in r:
                            ins.then_inc(ssem[op.stream], 16)
                    elif need[i]:
                        r.then_inc(esem[e_name], 1)
                if e_name == "sp":
                    for s in streams:
                        if str(s).startswith(final_streams):
                            eh.wait_ge(ssem[s], scum[s])

            @block.tensor
            def _(eh):
                run("pe", eh)

            @block.scalar
            def _(eh):
                run("act", eh)

            @block.vector
            def _(eh):
                run("dve", eh)

            @block.gpsimd
            def _(eh):
                run("pool", eh)

            @block.sync
            def _(eh):
                run("sp", eh)


KSTOP = None


class _Stop(Exception):
    pass


def build_program():
    nc = bass.Bass("TRN2", target_bir_lowering=False)
    S = Sched()

    def stop_at(name):
        if KSTOP == name:
            raise _Stop()

    def din(name, shape, dt=F32):
        return nc.dram_tensor(name, list(shape), dt, kind="ExternalInput").ap()

    def dout(name, shape, dt=F32):
        return nc.dram_tensor(name, list(shape), dt, kind="ExternalOutput").ap()

    xh = din("xh", [128, D]); xm = din("xm", [1024, D]); xs = din("xs", [16, D])
    sconv = din("sconv", [8, 1024]); ck = din("ck", [4, 128, 256]); cv = din("cv", [4, 128, 256])
    w_upt = [din("w_up1t", [21, 128, 8192]), din("w_up2t", [21, 128, 8192])]
    w_upL = [din("w_up1L", [128, 4096]), din("w_up2L", [128, 4096])]
    w_dn = [din("w_dn1", [DFF, D]), din("w_dn2", [DFF, D])]
    w_qkv = din("w_qkv", [D, 1536]); w_convt = din("w_convt", [4, 128, 12288]); w_gatet = din("w_gatet", [8, 128, 8192])
    w_cot = din("w_cot", [8, 128, 2048]); w_aot = din("w_aot", [8, 128, 2048])
    w_o = din("w_o", [D, D])
    gtab_d = din("gtab", [128, 48]); qg_d = din("qg", [1, 64]); kg_d = din("kg", [1, 64])
    cw_d = din("cw", [128, 24]); sinks_d = din("sinks", [1, 16])
    maskP_d = din("maskP", [128, 2048]); maskC_d = din("maskC", [128, 2048])
    hv_d = din("hv", [128, 64]); ident_d = din("ident", [128, 128])

    yp = dout("yp", [1024, D]); ys = dout("ys", [16, D])
    ocp = dout("ocp", [2, 1024]); okp = dout("okp", [128, 256]); ovp = dout("ovp", [128, 256])
    ocs = dout("ocs", [8, 1024]); oks = dout("oks", [4, 128, 256]); ovs = dout("ovs", [4, 128, 256])

    import contextlib
    es = contextlib.ExitStack()
    es.enter_context(nc.allow_low_precision("bf16 matmul operands, fp32 accumulation"))

    def sb(name, shape, dt):
        return es.enter_context(nc.sbuf_tensor(name, list(shape), dt))

    x_sb = sb("x_sb", [128, 5, D], F32)
    hT = sb("hT", [128, 16, 640], BF16)
    actT = sb("actT", [128, 4, 640], BF16)
    ring = sb("ring", [128, 4, 8192], BF16)
    sg = sb("sg", [128, 4, 320], F32)
    stat = sb("stat", [128, 768], F32)
    gtab = sb("gtab_s", [128, 48], F32)
    qg_bc = sb("qg_bc", [128, 64], F32); kg_bc = sb("kg_bc", [128, 64], F32)
    cw = sb("cw_s", [128, 24], F32)
    est = sb("est", [128, 16], F32)
    hv_f = sb("hv_f", [128, 64], F32); hvones = sb("hvones", [128, 64], BF16); ones = sb("ones", [128, 64], BF16)
    identf = sb("identf", [128, 128], F32); identb = sb("identb", [128, 128], BF16)
    bscr = sb("bscr", [128, 2], F32)
    KT3 = sb("KT3", [64, 4, 3, 128], BF16)
    V3 = sb("V3", [128, 3, 256], BF16)
    AT = sb("AT", [128, 8, 528], BF16)
    YB = sb("YB", [128, 8, 528], BF16)
    Kn = sb("Kn", [128, 2, 256], F32); Vf = sb("Vf", [128, 2, 256], F32)
    stT = sb("stT", [128, 8, 8], F32)
    us = sb("us", [128, 4, 6], F32); ysm = sb("ysm", [128, 4, 4], F32)
    ucarry = sb("ucarry", [128, 8, 2], F32)
    ucat = sb("ucat", [128, 8, 10], F32)
    T = sb("T", [128, 10496], F32)
    ps = es.enter_context(nc.psum_tensor("ps", [128, 4096], F32))
    S._bfn = lambda e: e.memset(bscr[:], 0.0)

    maskv = ring[:, 3, :].bitcast(F32).rearrange("p (m n) -> p m n", n=2048)
    E = T[:, 0:2048].rearrange("p (a b c) -> p a b c", a=2, b=2)
    sqb = T[:, 2048:3072]
    dtmp = T[:, 3072:4096].rearrange("p (a c) -> p a c", a=2)
    ckf = T[:, 0:1024].rearrange("p (s c) -> p s c", s=4)
    kt1 = T[:, 4096:4352]
    PT = T[:, 4352:5376].bitcast(BF16).rearrange("p (a b c) -> p a b c", a=2, b=2)
    Qn = T[:, 5376:6400].bitcast(BF16).rearrange("p (a c) -> p a c", a=2)
    QTt = T[0:64, 6400:8448].bitcast(BF16).rearrange("p (a h c) -> p a h c", a=2, h=16)
    KTc = T[0:64, 8448:9472].bitcast(BF16).rearrange("p (s k c) -> p s k c", s=4, k=4)
    Vc = T[:, 9472:9984].bitcast(BF16).rearrange("p (s c) -> p s c", s=4)
    Vn = T[0:4, 9984:10496].bitcast(BF16).rearrange("p (s c) -> p s c", s=4)
    sct = T[0:8, 0:1024]
    ub = T[:, 1024:2112].rearrange("p (i n) -> p i n", i=2)
    bgs = T[:, 2112:3200].rearrange("p (i n) -> p i n", i=2)
    xcs = T[:, 3200:3840].rearrange("p (i n) -> p i n", i=2)
    yt = T[:, 3840:4864].rearrange("p (i n) -> p i n", i=2)
    uout = T[0:10, 3840:4864]
    wca = T[:, 4864:8960].bitcast(BF16).rearrange("p (s n) -> p s n", s=2)
    m12 = T[:, 8960:9600].rearrange("p (i n) -> p i n", i=2)

    def slot_ap(ws):
        if ws < 4:
            return ring[:, ws, :]
        return T[:, 0:4096].bitcast(BF16)

    def bank(b, rows=128, n=512, r0=0):
        return ps[r0:r0 + rows, b * 512:b * 512 + n]

    def psk(b):
        return ("ps", b)

    def ld_tables(e):
        r = []
        r.append(e.dma_start(out=gtab[:], in_=gtab_d[:, :]))
        r.append(e.dma_start(out=qg_bc[:], in_=qg_d[0:1, :].partition_broadcast(128)))
        r.append(e.dma_start(out=kg_bc[:], in_=kg_d[0:1, :].partition_broadcast(128)))
        r.append(e.dma_start(out=cw[:], in_=cw_d[:, :]))
        r.append(e.dma_start(out=est[:], in_=sinks_d[0:1, :].partition_broadcast(128)))
        r.append(e.dma_start(out=hv_f[:], in_=hv_d[:, :]))
        r.append(e.dma_start(out=identf[:], in_=ident_d[:, :]))
        return r
    S.add("sp", ld_tables, writes=[("tab",)], ndma=7, stream="tab")
    S.add("dve", lambda e: e.tensor_copy(out=identb[:], in_=identf[:]), reads=[("tab",)], writes=[("identb",)])
    S.add("dve", lambda e: e.tensor_copy(out=hvones[:], in_=hv_f[:]), reads=[("tab",)], writes=[("hvones",)])
    S.add("dve", lambda e: e.memset(ones[:], 1.0), writes=[("ones",)])
    S.add("act", lambda e: e.activation(out=est[:], in_=est[:], func=AF.Exp), reads=[("tab",)], writes=[("est",)])

    w_dn_v = [w.rearrange("(c p) n -> p c n", p=128) for w in w_dn]
    w_qkv_v = w_qkv.rearrange("(k p) c -> p k c", p=128)
    w_o_v = w_o.rearrange("(c p) n -> p c n", p=128)

    def xk(lt):
        return [("x", lt, 0), ("x", lt, 1)]

    statc = [0]

    def newstat(n=1):
        c = statc[0]
        statc[0] += n
        if statc[0] > 768:
            c = 0
            statc[0] = n
        return c

    ctr = {"xn": 0, "sg": 0, "up": 0}

    def tiles_overlapping(tiles, a, b):
        return [t for t in tiles if t["col0"] < b and t["col0"] + t["R"] > a]

    gvec_d = din("gvec", [3, D])
    g_bc = ring[:, 3, 0:4096].bitcast(F32)
    hb = ring[:, 3, 4096:8192].rearrange("p (i n) -> p i n", i=2)
    psb0 = ps[:, 0:512].bitcast(BF16)
    psb1 = ps[:, 512:1024].bitcast(BF16)

    def norm_pipe(tiles, gcol):
        info = []
        for t in tiles:
            i = ctr["xn"] % 2
            ctr["xn"] += 1
            info.append((t, i, newstat(3)))

        def begin():
            S.add("sp", lambda e, gcol=gcol: [e.dma_start(out=g_bc, in_=gvec_d[gcol:gcol + 1, :].partition_broadcast(128))],
                  writes=[("R", 3)], ndma=1, stream="gbc", nobar=True)

        def stage1(n_):
            t, i, sc = info[n_]
            lt, R = t["lt"], t["R"]
            S.add("dve", lambda e, lt=lt, R=R, sc=sc, i=i: e.scalar_tensor_tensor(
                out=hb[0:R, i, :], in0=x_sb[0:R, lt, :], scalar=1.0, in1=x_sb[0:R, lt, :],
                op0=ALU.mult, op1=ALU.mult, accum_out=stat[0:R, sc:sc + 1]),
                reads=xk(lt) + [("R", 3)], writes=[("st", sc), ("hb", i)])
            S.add("act", lambda e, R=R, sc=sc: e.activation(
                out=stat[0:R, sc + 1:sc + 2], in_=stat[0:R, sc:sc + 1], func=AF.Sqrt, scale=1.0 / D, bias=EPS),
                reads=[("st", sc)], writes=[("st", sc + 1)])
            S.add("dve", lambda e, R=R, sc=sc: e.reciprocal(out=stat[0:R, sc + 2:sc + 3], in_=stat[0:R, sc + 1:sc + 2]),
                  reads=[("st", sc + 1)], writes=[("st", sc + 2)])
            S.add("dve", lambda e, lt=lt, R=R, sc=sc, i=i: e.scalar_tensor_tensor(
                out=hb[0:R, i, :], in0=x_sb[0:R, lt, :], scalar=stat[0:R, sc + 2:sc + 3], in1=g_bc[0:R, :],
                op0=ALU.mult, op1=ALU.mult),
                reads=xk(lt) + [("st", sc + 2), ("R", 3)], writes=[("hb", i)])

        def stage2(n_):
            t, i, sc = info[n_]
            lt, R, col0 = t["lt"], t["R"], t["col0"]
            for k in range(16):
                pb, b = (psb0, 0) if k < 8 else (psb1, 1)
                S.add("pe", lambda e, R=R, i=i, k=k, pb=pb: e.transpose(
                    out=pb[:, (k % 8) * 128:(k % 8) * 128 + R], in_=hb[0:R, i, k * 128:(k + 1) * 128],
                    identity=identb[0:R, 0:R]),
                    reads=[("hb", i), ("identb",), ("R", 3)], writes=[psk(b)])
            for b, pb in ((0, psb0), (1, psb1)):
                S.add("act", lambda e, R=R, b=b, pb=pb, col0=col0: e.activation(
                    out=hT[:, b * 8:b * 8 + 8, col0:col0 + R],
                    in_=pb.rearrange("p (a c) -> p a c", c=128)[:, :, 0:R], func=AF.Copy),
                    reads=[psk(b)], writes=[("hT", lt, b * 8 + kk) for kk in range(8)])

        return begin, stage1, stage2, len(info)

    def emit_norm(tiles, gcol):
        begin, stage1, stage2, n = norm_pipe(tiles, gcol)
        begin()
        stage1(0)
        for n_ in range(n):
            if n_ + 1 < n:
                stage1(n_ + 1)
            stage2(n_)

    def norm_hooks(tiles, gcol):
        begin, stage1, stage2, n = norm_pipe(tiles, gcol)

        def after(idx):
            stage1(idx)
            if idx >= 1:
                stage2(idx - 1)
            if idx == n - 1:
                stage2(idx)
        return begin, after

    def split2(a, b):
        m = a + ((b - a + 1) // 2)
        return [(a, m), (m, b)]

    xstage = T[:, 0:10240].rearrange("p (t n) -> p t n", n=D)

    def down_accum(tiles, nloc, wslot, scale, after=None, stage=False):
        Wd = slot_ap(wslot).rearrange("p (c n) -> p c n", n=D)
        for idx, t in enumerate(tiles):
            lt, R, col0 = t["lt"], t["R"], t["col0"]
            for half in range(2):
                b0 = 4 + 2 * half
                for cl in range(nloc):
                    for n in range(2):
                        S.add("pe", lambda e, R=R, col0=col0, cl=cl, n=n, half=half, b0=b0, Wd=Wd, nloc=nloc: e.matmul(
                            bank(b0 + n, rows=R), lhsT=actT[:, cl, col0:col0 + R],
                            rhs=Wd[:, cl, (2 * half + n) * 512:(2 * half + n + 1) * 512],
                            start=(cl == 0), stop=(cl == nloc - 1)),
                            reads=[("actT", cl, lt), ("R", wslot)], writes=[psk(b0 + n)])
                dst = xstage if stage else x_sb
                wk = [("xs", lt, half), ("wca", 0), ("wca", 1), ("m12", 0), ("m12", 1)] if stage else [("x", lt, half)]
                S.add("dve", lambda e, R=R, lt=lt, half=half, b0=b0, scale=scale, dst=dst: e.scalar_tensor_tensor(
                    out=dst[0:R, lt, half * 1024:(half + 1) * 1024], in0=ps[0:R, b0 * 512:b0 * 512 + 1024],
                    scalar=scale, in1=x_sb[0:R, lt, half * 1024:(half + 1) * 1024], op0=ALU.mult, op1=ALU.add),
                    reads=[psk(b0), psk(b0 + 1), ("x", lt, half)], writes=wk)
            if after is not None:
                after(idx)

    def load_wdn(src_v, c0, ng, wslot, extra=()):
        def f(e, c0=c0, ng=ng, wslot=wslot):
            Wd = slot_ap(wslot).rearrange("p (c n) -> p c n", n=D)
            return [e.dma_start(out=Wd[:, j:j + 1, :], in_=src_v[:, c0 + j:c0 + j + 1, :]) for j in range(ng)]
        wk = [("R", wslot)]
        if wslot == 4:
            wk += [("xs", lt_, h_) for lt_ in range(5) for h_ in range(2)] + [("uout",)]
        S.add("pool", f, reads=list(extra), writes=wk, ndma=ng, stream=("R", wslot), nobar=True)

    def load_wpair(src_t, u, wslot, extra=()):
        def f(e):
            dst = ring[:, wslot, :].rearrange("p (a n) -> p a n", n=2048)
            src = src_t[u].rearrange("p (a n) -> p a n", n=2048)
            return [e.dma_start(out=dst[:, 0:2, :], in_=src[:, 0:2, :]), e.dma_start(out=dst[:, 2:4, :], in_=src[:, 2:4, :])]
        S.add("pool", f, reads=list(extra), writes=[("R", wslot)], ndma=2, stream=("R", wslot), nobar=True)

    def load_wlast(src, wslot, extra=()):
        def f(e):
            dst = ring[:, wslot, 0:4096].rearrange("p (a n) -> p a n", n=2048)
            return [e.dma_start(out=dst, in_=src.rearrange("p (a n) -> p a n", n=2048))]
        S.add("pool", f, reads=list(extra), writes=[("R", wslot)], ndma=1, stream=("R", wslot), nobar=True)

    def emit_ffn(fi, tiles, col_lo, col_hi, last_after=None, last_pre=None, stage_last=False, first_loaded=False):
        slices = split2(col_lo, col_hi)
        units = [(u, 2 if 2 * u + 1 < NCH else 1) for u in range((NCH + 1) // 2)]
        ngroups = (NCH + 3) // 4

        def ld_unit(u, extra=()):
            if units[u][1] == 1:
                load_wlast(w_upL[fi], u % 2, extra=extra)
            else:
                load_wpair(w_upt[fi], u, u % 2, extra=extra)

        gslot = lambda gi: 4 if (fi == 0 and gi == ngroups - 1) else 2 + gi % 2

        def ld_group(gi, extra=()):
            ng = min(4, NCH - 4 * gi)
            load_wdn(w_dn_v[fi], 4 * gi, ng, gslot(gi), extra=extra)

        kst = ("ffn_started", ctr["up"])
        if not first_loaded:
            ld_unit(0)
        for u, nchk in units:
            wslot = u % 2
            if nchk == 2:
                Wv = ring[:, wslot, :].rearrange("p (a k c) -> p a k c", a=2, k=16)
            else:
                Wv = ring[:, wslot, 0:4096].rearrange("p (a k c) -> p a k c", a=2, k=16)
            for cl in range(nchk):
                c = 2 * u + cl
                cg = c % 4
                for (a, b) in slices:
                    n = b - a
                    par = ctr["up"] % 2
                    ctr["up"] += 1
                    gb, ubk = (0, 1) if par == 0 else (2, 3)
                    si = ctr["sg"] % 4
                    ctr["sg"] += 1
                    tl = tiles_overlapping(tiles, a, b)
                    for part, bk in ((0, gb), (1, ubk)):
                        for k in range(16):
                            first_mm = (u == 0 and cl == 0 and part == 0 and k == 0 and a == slices[0][0])
                            S.add("pe", lambda e, part=part, bk=bk, k=k, cl=cl, a=a, b=b, n=n, Wv=Wv: e.matmul(
                                bank(bk, n=n), lhsT=Wv[:, part, k, cl * 128:(cl + 1) * 128], rhs=hT[:, k, a:b],
                                start=(k == 0), stop=(k == 15)),
                                reads=[("R", wslot)] + [("hT", t["lt"], k) for t in tl],
                                writes=[psk(bk)] + ([kst] if first_mm else []))
                            if first_mm:
                                ld_unit(1, extra=[kst]); ld_group(0, extra=[kst]); ld_group(1, extra=[kst])
                    S.add("act", lambda e, gb=gb, n=n, si=si: e.activation(out=sg[:, si, 0:n], in_=bank(gb, n=n), func=AF.Silu),
                          reads=[psk(gb)], writes=[("sg", si)])
                    S.add("dve", lambda e, ubk=ubk, n=n, si=si, cg=cg, a=a, b=b: e.tensor_tensor(
                        out=actT[:, cg, a:b], in0=sg[:, si, 0:n], in1=bank(ubk, n=n), op=ALU.mult),
                        reads=[("sg", si), psk(ubk)], writes=[("actT", cg, t["lt"]) for t in tl])
            if u + 2 < len(units):
                ld_unit(u + 2)
            c_last = 2 * u + nchk - 1
            if c_last % 4 == 3 or c_last == NCH - 1:
                gi = c_last // 4
                is_last = (c_last == NCH - 1)
                if is_last and last_pre is not None:
                    last_pre()
                down_accum(tiles, c_last % 4 + 1, gslot(gi), 0.5,
                           after=(last_after if is_last else None), stage=(stage_last and is_last))
                if gi + 2 < ngroups:
                    ld_group(gi + 2)

    kv_ring = {"i": 0}
    try:
      for P in range(2):
          if P == 0:
              tiles = [dict(lt=0, gt=0, R=128, col0=0, kind="halo")] + \
                      [dict(lt=i, gt=i, R=128, col0=128 * i, kind="main") for i in range(1, 5)]
              cb = 128
          else:
              tiles = [dict(lt=i, gt=5 + i, R=128, col0=128 * i, kind="main") for i in range(4)] + \
                      [dict(lt=4, gt=9, R=16, col0=512, kind="samp")]
              cb = 0
          full = [t for t in tiles if t["kind"] != "halo"]
          ncols = tiles[-1]["col0"] + tiles[-1]["R"]

          for t in tiles:
              lt, gt, R = t["lt"], t["gt"], t["R"]
              if t["kind"] == "halo":
                  src = xh[:, :]
              elif t["kind"] == "main":
                  src = xm[(gt - 1) * 128:gt * 128, :]
              else:
                  src = xs[:, :]
              S.add("sp", lambda e, lt=lt, R=R, src=src: [e.dma_start(out=x_sb[0:R, lt, :], in_=src)],
                    writes=xk(lt), ndma=1, stream=("x", lt), nobar=True)

          if P == 1:
              S.add("sp", lambda e: [e.dma_start(out=oks[:, 0:124, :], in_=ck[:, 4:128, :]),
                                     e.dma_start(out=ovs[:, 0:124, :], in_=cv[:, 4:128, :])],
                    ndma=2, stream="out", nobar=True)

          emit_norm(tiles, 0)
          stop_at("norm1")
          nb_, na_ = norm_hooks(tiles, 1)
          emit_ffn(0, tiles, 0, ncols, last_after=na_, last_pre=nb_)
          stop_at("ffn1")

          def ld_gate(u, extra=()):
              load_wpair(w_gatet, u, u % 2, extra=extra)
              ws = u % 2

              def f(e, u=u, ws=ws):
                  return [e.dma_start(out=wca[:, ws, 0:2048], in_=w_cot[u]),
                          e.dma_start(out=wca[:, ws, 2048:4096], in_=w_aot[u])]
              S.add("pool", f, reads=list(extra), writes=[("wca", ws)], ndma=2, stream=("wca", ws))
          Rflat = ring[:].rearrange("p s n -> p (s n)")
          Wqkv = Rflat[:, 0:24576].rearrange("p (k c) -> p k c", c=1536)

          def ld_qkv(e):
              return [e.dma_start(out=Wqkv[:, kq * 4:kq * 4 + 4, :], in_=w_qkv_v[:, kq * 4:kq * 4 + 4, :])
                      for kq in range(4)]
          S.add("pool", ld_qkv, writes=[("R", 0), ("R", 1), ("R", 2)], ndma=4, stream="qkv", nobar=True)
          rqkv = [("R", 0), ("R", 1), ("R", 2)]
          S.barrier()
          def ld_masks():
              S.add("sp", lambda e: [e.dma_start(out=maskv[:, 0, :], in_=maskP_d[:, :]), e.dma_start(out=maskv[:, 1, :], in_=maskC_d[:, :])],
                    reads=[psk(2)], writes=[("R", 3)], ndma=2, stream="mask", nobar=True)

          if P == 1:
              S.add("sp", lambda e: [e.dma_start(out=ckf, in_=ck.rearrange("s p c -> p s c"))],
                    writes=[("E", 0, 0), ("E", 0, 1)], ndma=1, stream="cache")
              S.add("pool", lambda e: [e.dma_start(out=Vc, in_=cv.rearrange("s p c -> p s c"))],
                    writes=[("Vc",)], ndma=1, stream="cachev")
              for sbi in range(4):
                  for kvh in range(4):
                      S.add("pe", lambda e, sbi=sbi, kvh=kvh: e.transpose(
                          out=ps[0:64, 5 * 512 + kvh * 128:5 * 512 + (kvh + 1) * 128],
                          in_=ckf[:, sbi, kvh * 64:(kvh + 1) * 64], identity=identf[:, :]),
                          reads=[("E", 0, 0), ("E", 0, 1), ("tab",)], writes=[psk(5)])
                  S.add("dve", lambda e, sbi=sbi: e.tensor_copy(
                      out=KTc[:, sbi, :, :], in_=ps[0:64, 5 * 512:6 * 512].rearrange("p (a c) -> p a c", c=128)),
                      reads=[psk(5)], writes=[("KTc",)])

          for t in tiles:
              t["j"] = kv_ring["i"] % 2
              kv_ring["i"] += 1
              t["g3"] = t["gt"] % 3
              t["full"] = t["kind"] != "halo"
              if t["full"]:
                  t["sc"] = newstat(48)
              t["sk"] = newstat(12)

          def A_mm(t, n):
              lt, R, col0 = t["lt"], t["R"], t["col0"]
              for k in range(16):
                  S.add("pe", lambda e, n=n, k=k, R=R, col0=col0: e.matmul(
                      bank(n, rows=R), lhsT=hT[:, k, col0:col0 + R], rhs=Wqkv[:, k, n * 512:(n + 1) * 512],
                      start=(k == 0), stop=(k == 15)),
                      reads=rqkv + [("hT", lt, k)], writes=[psk(n)])

          def A_q1(t):
              R, sc = t["R"], t["sc"]
              S.add("act", lambda e, R=R: e.activation(out=sqb[0:R, :], in_=ps[0:R, 0:1024], func=AF.Square),
                    reads=[psk(0), psk(1)], writes=[("sqb",)])
              S.add("dve", lambda e, R=R, sc=sc: e.tensor_reduce(
                  out=stat[0:R, sc:sc + 16], in_=sqb[0:R, :].rearrange("p (h d) -> p h d", d=64), op=ALU.add, axis=AX.X),
                  reads=[("sqb",)], writes=[("st", sc)])
              S.add("act", lambda e, R=R, sc=sc: e.activation(
                  out=stat[0:R, sc + 16:sc + 32], in_=stat[0:R, sc:sc + 16], func=AF.Sqrt, scale=1.0, bias=64.0 * EPS),
                  reads=[("st", sc)], writes=[("st", sc + 16)])
              S.add("dve", lambda e, R=R, sc=sc: e.reciprocal(out=stat[0:R, sc + 32:sc + 48], in_=stat[0:R, sc + 16:sc + 32]),
                    reads=[("st", sc + 16)], writes=[("st", sc + 32)])

          def A_q2(t):
              R, sc, j = t["R"], t["sc"], t["j"]
              S.add("dve", lambda e, R=R, sc=sc: e.tensor_tensor(
                  out=sqb[0:R, :].rearrange("p (h d) -> p h d", d=64),
                  in0=ps[0:R, 0:1024].rearrange("p (h d) -> p h d", d=64),
                  in1=stat[0:R, sc + 32:sc + 48].unsqueeze(2).broadcast_to([R, 16, 64]), op=ALU.mult),
                  reads=[psk(0), psk(1), ("st", sc + 32), ("sqb",)], writes=[("sqb",)])
              S.add("pool", lambda e, R=R, j=j: e.tensor_tensor(
                  out=Qn[0:R, j, :].rearrange("p (h d) -> p h d", d=64),
                  in0=sqb[0:R, :].rearrange("p (h d) -> p h d", d=64),
                  in1=qg_bc[0:R, :].unsqueeze(1).broadcast_to([R, 16, 64]), op=ALU.mult),
                  reads=[("sqb",), ("tab",)], writes=[("Qn", j)])

          def A_k(t):
              R, sk, j = t["R"], t["sk"], t["j"]
              S.add("act", lambda e, R=R: e.activation(out=kt1[0:R, :], in_=ps[0:R, 1024:1280], func=AF.Square),
                    reads=[psk(2)], writes=[("kt1",)])
              S.add("dve", lambda e, R=R, sk=sk: e.tensor_reduce(
                  out=stat[0:R, sk:sk + 4], in_=kt1[0:R, :].rearrange("p (h d) -> p h d", d=64), op=ALU.add, axis=AX.X),
                  reads=[("kt1",)], writes=[("st", sk)])
              S.add("act", lambda e, R=R, sk=sk: e.activation(
                  out=stat[0:R, sk + 4:sk + 8], in_=stat[0:R, sk:sk + 4], func=AF.Sqrt, scale=1.0 / 64, bias=EPS),
                  reads=[("st", sk)], writes=[("st", sk + 4)])
              S.add("dve", lambda e, R=R, sk=sk: e.reciprocal(out=stat[0:R, sk + 8:sk + 12], in_=stat[0:R, sk + 4:sk + 8]),
                    reads=[("st", sk + 4)], writes=[("st", sk + 8)])
              S.add("dve", lambda e, R=R, sk=sk: e.tensor_tensor(
                  out=kt1[0:R, :].rearrange("p (h d) -> p h d", d=64),
                  in0=ps[0:R, 1024:1280].rearrange("p (h d) -> p h d", d=64),
                  in1=stat[0:R, sk + 8:sk + 12].unsqueeze(2).broadcast_to([R, 4, 64]), op=ALU.mult),
                  reads=[psk(2), ("st", sk + 8), ("kt1",)], writes=[("kt1",)])
              S.add("dve", lambda e, R=R, j=j: e.tensor_tensor(
                  out=Kn[0:R, j, :].rearrange("p (h d) -> p h d", d=64),
                  in0=kt1[0:R, :].rearrange("p (h d) -> p h d", d=64),
                  in1=kg_bc[0:R, :].unsqueeze(1).broadcast_to([R, 4, 64]), op=ALU.mult),
                  reads=[("kt1",), ("tab",)], writes=[("Kn", j)])

          def A_v(t):
              R, j, g3, gt = t["R"], t["j"], t["g3"], t["gt"]
              S.add("act", lambda e, R=R, j=j: e.activation(out=Vf[0:R, j, :], in_=ps[0:R, 1280:1536], func=AF.Copy),
                    reads=[psk(2)], writes=[("Vf", j)])
              S.add("act", lambda e, R=R, g3=g3: e.activation(out=V3[0:R, g3, :], in_=ps[0:R, 1280:1536], func=AF.Copy),
                    reads=[psk(2)], writes=[("V", g3)])

          def A_out(t):
              j, gt = t["j"], t["gt"]
              if gt == 8:
                  S.add("sp", lambda e, j=j: [e.dma_start(out=okp[:, :], in_=Kn[:, j, :]),
                                              e.dma_start(out=ovp[:, :], in_=Vf[:, j, :])],
                        reads=[("Kn", j), ("Vf", j)], ndma=2, stream=("out", "kv8"))
              if gt == 9:
                  S.add("sp", lambda e, j=j: [e.dma_start(out=oks[s_, 124:128, :], in_=Kn[4 * s_:4 * s_ + 4, j, :]) for s_ in range(4)] +
                                             [e.dma_start(out=ovs[s_, 124:128, :], in_=Vf[4 * s_:4 * s_ + 4, j, :]) for s_ in range(4)],
                        reads=[("Kn", j), ("Vf", j)], ndma=8, stream=("out", "kv9"))

          psb3 = ps[:, 0 * 512:1 * 512].bitcast(BF16)
          psb4 = ps[:, 1 * 512:2 * 512].bitcast(BF16)

          def B(t):
              R, j, g3 = t["R"], t["j"], t["g3"]
              if t["full"]:
                  for h in range(16):
                      pb = psb3 if h < 8 else psb4
                      bk = 0 if h < 8 else 1
                      S.add("pe", lambda e, h=h, pb=pb, R=R, j=j: e.transpose(
                          out=pb[0:64, (h % 8) * 128:(h % 8) * 128 + R], in_=Qn[0:R, j, h * 64:(h + 1) * 64],
                          identity=identb[0:R, 0:R]),
                          reads=[("Qn", j), ("identb",)], writes=[psk(bk)])
                  S.add("act", lambda e, R=R, j=j: e.activation(
                      out=QTt[:, j, 0:8, 0:R], in_=psb3[0:64, :].rearrange("p (a c) -> p a c", c=128)[:, :, 0:R], func=AF.Copy),
                      reads=[psk(0)], writes=[("QT", j, 0)])
                  S.add("act", lambda e, R=R, j=j: e.activation(
                      out=QTt[:, j, 8:16, 0:R], in_=psb4[0:64, :].rearrange("p (a c) -> p a c", c=128)[:, :, 0:R], func=AF.Copy),
                      reads=[psk(1)], writes=[("QT", j, 1)])
              for kvh in range(4):
                  S.add("pe", lambda e, kvh=kvh, R=R, j=j: e.transpose(
                      out=ps[0:64, 2 * 512 + kvh * 128:2 * 512 + kvh * 128 + R], in_=Kn[0:R, j, kvh * 64:(kvh + 1) * 64],
                      identity=identf[0:R, 0:R]),
                      reads=[("Kn", j), ("tab",)], writes=[psk(2)])
              S.add("dve", lambda e, R=R, g3=g3: e.tensor_copy(
                  out=KT3[0:64, :, g3, 0:R],
                  in_=ps[0:64, 2 * 512:3 * 512].rearrange("p (a c) -> p a c", c=128)[:, :, 0:R]),
                  reads=[psk(2)], writes=[("KT", g3)])

          def C_main(t, hooks):
              lt, gt, j, g3 = t["lt"], t["gt"], t["j"], t["g3"]
              acol = t["col0"] - cb
              p3 = (gt - 1) % 3

              def S_(kvh):
                  e_i = kvh % 2
                  for kb, (k3, bk) in enumerate(((p3, 6), (g3, 7))):
                      for g in range(4):
                          S.add("pe", lambda e, kvh=kvh, g=g, k3=k3, bk=bk, j=j: e.matmul(
                              ps[:, bk * 512 + g * 128:bk * 512 + (g + 1) * 128],
                              lhsT=KT3[0:64, kvh, k3, :], rhs=QTt[:, j, 4 * kvh + g, :],
                              start=True, stop=True),
                              reads=[("KT", k3), ("QT", j, 0), ("QT", j, 1)], writes=[psk(bk)])
                      S.add("act", lambda e, bk=bk, e_i=e_i, kb=kb: e.activation(
                          out=E[:, e_i, kb, :], in_=bank(bk), func=AF.Exp), reads=[psk(bk)], writes=[("E", e_i, kb)])
                      S.add("dve" if kb == 0 else "pool", lambda e, kvh=kvh, e_i=e_i, kb=kb: e.tensor_tensor(
                          out=PT[:, e_i, kb, :], in0=E[:, e_i, kb, :], in1=maskv[:, kb, kvh * 512:(kvh + 1) * 512], op=ALU.mult),
                          reads=[("E", e_i, kb), ("R", 3)], writes=[("PT", e_i, kb)])

              def V_(kvh):
                  e_i = kvh % 2
                  ob = 3 if kvh % 2 == 0 else 4
                  for g in range(4):
                      hf, g2 = g % 2, g // 2
                      for kb, k3 in enumerate((p3, g3)):
                          S.add("pe", lambda e, kvh=kvh, g=g, hf=hf, g2=g2, kb=kb, k3=k3, ob=ob, e_i=e_i: e.matmul(
                              ps[hf * 64:hf * 64 + 64, ob * 512 + g2 * 128:ob * 512 + (g2 + 1) * 128],
                              lhsT=V3[:, k3, kvh * 64:(kvh + 1) * 64], rhs=PT[:, e_i, kb, g * 128:(g + 1) * 128],
                              start=(kb == 0), stop=(kb == 1)),
                              reads=[("V", k3), ("PT", e_i, kb)], writes=[psk(ob)])
                  for g in range(4):
                      hf, g2 = g % 2, g // 2
                      for kb in range(2):
                          lw = hvones if (gt == 1 and kb == 0) else ones
                          S.add("pe", lambda e, g=g, hf=hf, g2=g2, kb=kb, ob=ob, e_i=e_i, lw=lw: e.matmul(
                              ps[hf * 64:hf * 64 + 64, ob * 512 + 256 + g2 * 128:ob * 512 + 256 + (g2 + 1) * 128],
                              lhsT=lw[:, :], rhs=PT[:, e_i, kb, g * 128:(g + 1) * 128],
                              start=(kb == 0), stop=(kb == 1)),
                              reads=[("ones",), ("hvones",), ("PT", e_i, kb)], writes=[psk(ob)])
                  for g in range(4):
                      hf, g2 = g % 2, g // 2
                      hh = 4 * kvh + g
                      S.add("act", lambda e, hf=hf, g2=g2, hh=hh, ob=ob, e_i=e_i: e.activation(
                          out=dtmp[hf * 64:hf * 64 + 64, e_i, g2 * 128:(g2 + 1) * 128],
                          in_=ps[hf * 64:hf * 64 + 64, ob * 512 + 256 + g2 * 128:ob * 512 + 256 + (g2 + 1) * 128],
                          func=AF.Identity, bias=est[hf * 64:hf * 64 + 64, hh:hh + 1], scale=1.0),
                          reads=[psk(ob), ("est",), ("dtmp", e_i)], writes=[("dtmp", e_i)])
                  S.add("dve", lambda e, e_i=e_i: e.reciprocal(out=dtmp[:, e_i, 0:256], in_=dtmp[:, e_i, 0:256]),
                        reads=[("dtmp", e_i)], writes=[("dtmp", e_i)])
                  S.add("dve", lambda e, ob=ob, e_i=e_i, kvh=kvh, acol=acol: e.tensor_tensor(
                      out=AT[:, 2 * kvh:2 * kvh + 2, acol:acol + 128],
                      in0=ps[:, ob * 512:ob * 512 + 256].rearrange("p (g q) -> p g q", q=128),
                      in1=dtmp[:, e_i, 0:256].rearrange("p (g q) -> p g q", q=128), op=ALU.mult),
                      reads=[psk(ob), ("dtmp", e_i)], writes=[("AT", lt)])

              def H_(i):
                  for hk in hooks.get(i, []):
                      hk()

              S_(0); H_(0); S_(1); V_(0); H_(1); S_(2); V_(1); H_(2); S_(3); V_(2); H_(3); V_(3)

          def C_samp(t):
              lt, j, g3 = t["lt"], t["j"], t["g3"]
              acol = t["col0"] - cb
              for sbi in range(4):
                  bk = 3 + sbi // 2
                  S.add("pe", lambda e, sbi=sbi, bk=bk, g3=g3: e.matmul(
                      ps[0:4, bk * 512 + (sbi % 2) * 256:bk * 512 + (sbi % 2) * 256 + 256],
                      lhsT=identb[0:16, 4 * sbi:4 * sbi + 4], rhs=V3[0:16, g3, :], start=True, stop=True),
                      reads=[("identb",), ("V", g3)], writes=[psk(bk)])
              S.add("dve", lambda e: e.tensor_copy(out=Vn.rearrange("p s c -> p (s c)"), in_=ps[0:4, 3 * 512:5 * 512]),
                    reads=[psk(3), psk(4)], writes=[("Vn",)])
              for sbi in range(4):
                  for kvh in range(4):
                      c0 = sbi * 64 + kvh * 16
                      S.add("pe", lambda e, sbi=sbi, kvh=kvh, c0=c0, j=j: e.matmul(
                          ps[:, 6 * 512 + c0:6 * 512 + c0 + 16], lhsT=KTc[:, sbi, kvh, :],
                          rhs=QTt[:, j, 4 * kvh:4 * kvh + 4, 4 * sbi:4 * sbi + 4], start=True, stop=True),
                          reads=[("KTc",), ("QT", j, 0), ("QT", j, 1)], writes=[psk(6)])
                      S.add("pe", lambda e, sbi=sbi, kvh=kvh, c0=c0, j=j, g3=g3: e.matmul(
                          ps[0:4, 7 * 512 + c0:7 * 512 + c0 + 16], lhsT=KT3[0:64, kvh, g3, 4 * sbi:4 * sbi + 4],
                          rhs=QTt[:, j, 4 * kvh:4 * kvh + 4, 4 * sbi:4 * sbi + 4], start=True, stop=True),
                          reads=[("KT", g3), ("QT", j, 0), ("QT", j, 1)], writes=[psk(7)])
              S.add("act", lambda e: e.activation(out=E[:, 0, 0, 0:256], in_=ps[:, 6 * 512:6 * 512 + 256], func=AF.Exp),
                    reads=[psk(6)], writes=[("E", 0, 0)])
              S.add("act", lambda e: e.activation(out=E[0:4, 0, 1, 0:256], in_=ps[0:4, 7 * 512:7 * 512 + 256], func=AF.Exp),
                    reads=[psk(7)], writes=[("E", 0, 1)])
              S.add("dve", lambda e: e.tensor_tensor(
                  out=PT[:, 0, 0, 0:256].rearrange("p (s h q) -> p s h q", s=4, h=16),
                  in0=E[:, 0, 0, 0:256].rearrange("p (s h q) -> p s h q", s=4, h=16),
                  in1=maskv[:, 0, :].rearrange("p (h q) -> p h q", q=128)[:, :, 0:4].unsqueeze(1).broadcast_to([128, 4, 16, 4]),
                  op=ALU.mult), reads=[("E", 0, 0), ("R", 3)], writes=[("PT", 0, 0)])
              S.add("dve", lambda e: e.tensor_tensor(
                  out=PT[0:4, 0, 1, 0:256].rearrange("p (s h q) -> p s h q", s=4, h=16),
                  in0=E[0:4, 0, 1, 0:256].rearrange("p (s h q) -> p s h q", s=4, h=16),
                  in1=maskv[0:4, 1, :].rearrange("p (h q) -> p h q", q=128)[:, :, 0:4].unsqueeze(1).broadcast_to([4, 4, 16, 4]),
                  op=ALU.mult), reads=[("E", 0, 1), ("R", 3)], writes=[("PT", 0, 1)])
              for sbi in range(4):
                  for kvh in range(4):
                      for g in range(4):
                          h = 4 * kvh + g
                          hf = h % 2
                          pc = sbi * 64 + kvh * 16 + g * 4
                          oc = (h // 2) * 16 + sbi * 4
                          S.add("pe", lambda e, sbi=sbi, kvh=kvh, pc=pc, oc=oc, hf=hf: e.matmul(
                              ps[hf * 64:hf * 64 + 64, oc:oc + 4], lhsT=Vc[:, sbi, kvh * 64:(kvh + 1) * 64],
                              rhs=PT[:, 0, 0, pc:pc + 4], start=True, stop=False),
                              reads=[("Vc",), ("PT", 0, 0)], writes=[psk(0)])
                          S.add("pe", lambda e, sbi=sbi, kvh=kvh, pc=pc, oc=oc, hf=hf: e.matmul(
                              ps[hf * 64:hf * 64 + 64, oc:oc + 4], lhsT=Vn[:, sbi, kvh * 64:(kvh + 1) * 64],
                              rhs=PT[0:4, 0, 1, pc:pc + 4], start=False, stop=True),
                              reads=[("Vn",), ("PT", 0, 1)], writes=[psk(0)])
                          S.add("pe", lambda e, pc=pc, oc=oc, hf=hf: e.matmul(
                              ps[hf * 64:hf * 64 + 64, 512 + oc:512 + oc + 4], lhsT=ones[:, :],
                              rhs=PT[:, 0, 0, pc:pc + 4], start=True, stop=False),
                              reads=[("ones",), ("PT", 0, 0)], writes=[psk(1)])
                          S.add("pe", lambda e, pc=pc, oc=oc, hf=hf: e.matmul(
                              ps[hf * 64:hf * 64 + 64, 512 + oc:512 + oc + 4], lhsT=ones[0:4, :],
                              rhs=PT[0:4, 0, 1, pc:pc + 4], start=False, stop=True),
                              reads=[("ones",), ("PT", 0, 1)], writes=[psk(1)])
              for h in range(16):
                  hf, c = h % 2, h // 2
                  S.add("dve", lambda e, h=h, hf=hf, c=c: e.tensor_scalar(
                      out=dtmp[hf * 64:hf * 64 + 64, 0, c * 16:(c + 1) * 16],
                      in0=ps[hf * 64:hf * 64 + 64, 512 + c * 16:512 + (c + 1) * 16],
                      scalar1=est[hf * 64:hf * 64 + 64, h:h + 1], scalar2=None, op0=ALU.add),
                      reads=[psk(1), ("est",), ("dtmp", 0)], writes=[("dtmp", 0)])
              S.add("dve", lambda e: e.reciprocal(out=dtmp[:, 0, 0:128], in_=dtmp[:, 0, 0:128]),
                    reads=[("dtmp", 0)], writes=[("dtmp", 0)])
              S.add("dve", lambda e, acol=acol: e.tensor_tensor(
                  out=AT[:, :, acol:acol + 16],
                  in0=ps[:, 0:128].rearrange("p (c q) -> p c q", q=16),
                  in1=dtmp[:, 0, 0:128].rearrange("p (c q) -> p c q", q=16), op=ALU.mult),
                  reads=[psk(0), ("dtmp", 0)], writes=[("AT", lt)])

          def conv_w(cu):
              ss = cu % 2
              return Rflat[:, ss * 12288:(ss + 1) * 12288].rearrange("p (a k c) -> p a k c", a=3, k=16)

          def conv_keys(cu):
              return [("R", 0), ("R1a",)] if cu % 2 == 0 else [("R1b",), ("R", 2)]

          def ld_conv(cu):
              ss = cu % 2

              def f(e, cu=cu, ss=ss):
                  dst = Rflat[:, ss * 12288:(ss + 1) * 12288].rearrange("p (a n) -> p a n", n=2048)
                  src = w_convt[cu].rearrange("p (a n) -> p a n", n=2048)
                  return [e.dma_start(out=dst[:, 0:3, :], in_=src[:, 0:3, :]), e.dma_start(out=dst[:, 3:6, :], in_=src[:, 3:6, :])]
              S.add("pool", f, writes=conv_keys(cu), ndma=2, stream=("RS", ss), nobar=True)

          t0_ = tiles[0]
          for n in ([0, 1, 2] if t0_["full"] else [2]):
              A_mm(t0_, n)
          ld_masks()
          A_k(t0_); A_v(t0_)
          if t0_["full"]:
              A_q1(t0_); A_q2(t0_)
          A_out(t0_)
          B(t0_)
          for idx, t in enumerate(tiles):
              nxt = tiles[idx + 1] if idx + 1 < len(tiles) else None
              hooks = {}
              if nxt is not None:
                  last_mm = (idx + 2 == len(tiles))
                  hooks[0] = [lambda nxt=nxt: A_mm(nxt, 2), lambda nxt=nxt: A_mm(nxt, 0), lambda nxt=nxt: A_k(nxt),
                              lambda nxt=nxt: A_v(nxt)]
                  hooks[1] = [lambda nxt=nxt: A_mm(nxt, 1), lambda nxt=nxt: A_q1(nxt)]
                  if last_mm:
                      hooks[1].append(lambda: ld_conv(0))
                      hooks[1].append(lambda: ld_conv(1))
                  hooks[2] = [lambda nxt=nxt: A_q2(nxt), lambda nxt=nxt: A_out(nxt)]
                  hooks[3] = [lambda nxt=nxt: B(nxt)]
              if t["kind"] == "main":
                  C_main(t, hooks)
              elif t["kind"] == "samp":
                  C_samp(t)
              else:
                  for kv_ in range(4):
                      for hk in hooks.get(kv_, []):
                          hk()

          stop_at("attn")
          S.barrier()
          if P == 1:
              S.add("sp", lambda e: [e.dma_start(out=sct, in_=sconv[:, :])], writes=[("sct",)], ndma=1, stream="cache")
              for c in range(8):
                  S.add("pe", lambda e, c=c: e.transpose(
                      out=ps[:, 6 * 512 + c * 8:6 * 512 + c * 8 + 8], in_=sct[:, c * 128:(c + 1) * 128],
                      identity=identf[0:8, 0:8]),
                      reads=[("sct",), ("tab",)], writes=[psk(6)])
              S.add("dve", lambda e: e.tensor_copy(out=stT[:].rearrange("p a b -> p (a b)"), in_=ps[:, 6 * 512:6 * 512 + 64]),
                    reads=[psk(6)], writes=[("stT",)])

          nmain = 512
          clo = cb - 2 if P == 0 else 0
          cslices = split2(clo, ncols)
          for c in range(8):
              cu, cl = c // 2, c % 2
              ss = cu % 2
              Wc = conv_w(cu)
              rW = conv_keys(cu)
              i = c % 2
              if P == 1:
                  S.add("dve", lambda e, i=i, c=c: e.tensor_copy(out=ub[:, i, 0:2], in_=ucarry[:, c, :]),
                        reads=[("ucarry", c)], writes=[("ubh", i)])
              for si, (a, b) in enumerate(cslices):
                  n = b - a
                  pa = a - (cb - 2)
                  par = (c * 2 + si) % 2
                  bx, bc_, bb = (0, 1, 2) if par == 0 else (3, 4, 5)
                  tl = tiles_overlapping(tiles, a, b)
                  for part, bk in ((0, bx), (2, bc_), (1, bb)):
                      for k in range(16):
                          S.add("pe", lambda e, part=part, bk=bk, k=k, a=a, b=b, n=n, Wc=Wc, cl=cl: e.matmul(
                              bank(bk, n=n), lhsT=Wc[:, part, k, cl * 128:(cl + 1) * 128], rhs=hT[:, k, a:b],
                              start=(k == 0), stop=(k == 15)),
                              reads=rW + [("hT", t_["lt"], k) for t_ in tl],
                              writes=[psk(bk)] + ([("c3s",)] if (c == 6 and si == 0 and part == 0 and k == 0) else []),
                              nobar=True)
                  xi = par
                  S.add("act", lambda e, bx=bx, n=n, xi=xi: e.activation(out=xcs[:, xi, 0:n], in_=bank(bx, n=n), func=AF.Copy),
                        reads=[psk(bx)], writes=[("xcs", xi)])
                  S.add("dve", lambda e, bc_=bc_, n=n, xi=xi, i=i, pa=pa: e.tensor_tensor(
                      out=ub[:, i, pa:pa + n], in0=xcs[:, xi, 0:n], in1=bank(bc_, n=n), op=ALU.mult),
                      reads=[("xcs", xi), psk(bc_)], writes=[("ub", i, si)])
                  S.add("act", lambda e, bb=bb, n=n, i=i, pa=pa: e.activation(out=bgs[:, i, pa:pa + n], in_=bank(bb, n=n), func=AF.Copy),
                        reads=[psk(bb)], writes=[("bgs", i, si)])
              if cl == 1 and cu + 2 < 4:
                  ld_conv(cu + 2)
              if c == 7:
                  ld_gate(0); ld_gate(1, extra=[("c3s",)])
              ubk = [("ub", i, 0), ("ub", i, 1), ("ubh", i)]
              S.add("dve", lambda e, i=i, c=c: e.tensor_scalar(
                  out=yt[:, i, :], in0=ub[:, i, 2:2 + nmain], scalar1=cw[:, c * 3 + 2:c * 3 + 3], scalar2=None, op0=ALU.mult),
                  reads=ubk + [("tab",)], writes=[("yt", i)])
              for jj in (1, 0):
                  S.add("dve", lambda e, i=i, c=c, jj=jj: e.scalar_tensor_tensor(
                      out=yt[:, i, :], in0=ub[:, i, jj:jj + nmain], scalar=cw[:, c * 3 + jj:c * 3 + jj + 1], in1=yt[:, i, :],
                      op0=ALU.mult, op1=ALU.add),
                      reads=ubk + [("tab",), ("yt", i)], writes=[("yt", i)])
              S.add("pool", lambda e, i=i, c=c: e.tensor_tensor(
                  out=YB[:, c, 0:nmain], in0=yt[:, i, :], in1=bgs[:, i, 2:2 + nmain], op=ALU.mult),
                  reads=[("yt", i), ("bgs", i, 0), ("bgs", i, 1)], writes=[("YB", c)])
              if P == 0:
                  S.add("dve", lambda e, i=i, c=c: e.tensor_copy(out=ucarry[:, c, :], in_=ub[:, i, nmain:nmain + 2]),
                        reads=ubk, writes=[("ucarry", c)])
              else:
                  S.add("dve", lambda e, c=c: e.tensor_copy(out=us[:, :, 0:2], in_=stT[:, c, :].rearrange("p (s j) -> p s j", j=2)),
                        reads=[("stT",), ("us",)], writes=[("us",)])
                  S.add("dve", lambda e, i=i: e.tensor_copy(out=us[:, :, 2:6], in_=ub[:, i, 514:530].rearrange("p (s t) -> p s t", t=4)),
                        reads=ubk + [("us",)], writes=[("us",)])
                  S.add("dve", lambda e, c=c: e.tensor_scalar(
                      out=ysm[:, :, :], in0=us[:, :, 2:6], scalar1=cw[:, c * 3 + 2:c * 3 + 3], scalar2=None, op0=ALU.mult),
                      reads=[("us",), ("tab",), ("ysm",)], writes=[("ysm",)])
                  for jj in (1, 0):
                      S.add("dve", lambda e, c=c, jj=jj: e.scalar_tensor_tensor(
                          out=ysm[:, :, :], in0=us[:, :, jj:jj + 4], scalar=cw[:, c * 3 + jj:c * 3 + jj + 1], in1=ysm[:, :, :],
                          op0=ALU.mult, op1=ALU.add),
                          reads=[("us",), ("tab",), ("ysm",)], writes=[("ysm",)])
                  S.add("dve", lambda e, i=i, c=c: e.tensor_tensor(
                      out=YB[:, c, 512:528].rearrange("p (s t) -> p s t", t=4), in0=ysm[:, :, :],
                      in1=bgs[:, i, 514:530].rearrange("p (s t) -> p s t", t=4), op=ALU.mult),
                      reads=[("ysm",), ("bgs", i, 0), ("bgs", i, 1)], writes=[("YBs", c)])
                  S.add("dve", lambda e, c=c: e.tensor_copy(
                      out=ucat[:, c, 0:8].rearrange("p (s j) -> p s j", j=2), in_=us[:, :, 4:6]),
                      reads=[("us",)], writes=[("ucat", c)])
                  S.add("dve", lambda e, i=i, c=c: e.tensor_copy(out=ucat[:, c, 8:10], in_=ub[:, i, 512:514]),
                        reads=ubk, writes=[("ucat2", c)])
          if P == 1:
              for c in range(8):
                  bk = 6 + c // 4
                  S.add("pe", lambda e, c=c, bk=bk: e.transpose(
                      out=ps[0:10, bk * 512 + (c % 4) * 128:bk * 512 + (c % 4 + 1) * 128], in_=ucat[:, c, :], identity=identf[:, :]),
                      reads=[("ucat", c), ("ucat2", c), ("tab",)], writes=[psk(bk)])
              S.add("dve", lambda e: e.tensor_copy(out=uout, in_=ps[0:10, 6 * 512:8 * 512]),
                    reads=[psk(6), psk(7)], writes=[("uout",), ("yt", 0), ("yt", 1)])
              S.add("sp", lambda e: [e.dma_start(out=ocs[:, :], in_=uout[0:8, :]), e.dma_start(out=ocp[:, :], in_=uout[8:10, :])],
                    reads=[("uout",)], ndma=2, stream=("out", "conv"))

          stop_at("conv")
          fslices = split2(cb, ncols)

          eslot = lambda gi: 2 + (gi + 1) % 2
          load_wdn(w_o_v, 0, 4, eslot(0)); load_wdn(w_o_v, 4, 4, eslot(1))
          n2b_, n2a_ = norm_hooks(full, 2)
          for u in range(8):
              ws = u % 2
              Wg = ring[:, ws, :].rearrange("p (a k c) -> p a k c", a=2, k=16)
              co = wca[:, ws, 0:2048].rearrange("p (c n) -> p c n", n=256)
              ao = wca[:, ws, 2048:4096].rearrange("p (c n) -> p c n", n=256)
              for jl in range(2):
                  jx = 2 * u + jl
                  jg = jx % 4
                  for (a, b) in fslices:
                      n = b - a
                      aa = a - cb
                      tl = tiles_overlapping(full, a, b)
                      for c in range(8):
                          S.add("pe", lambda e, c=c, jl=jl, aa=aa, n=n, co=co: e.matmul(
                              bank(0, n=n), lhsT=co[:, c, jl * 128:(jl + 1) * 128], rhs=YB[:, c, aa:aa + n],
                              start=(c == 0), stop=(c == 7)),
                              reads=[("wca", ws), ("YB", c), ("YBs", c)], writes=[psk(0)])
                      for part, bk in ((0, 1), (1, 3)):
                          for k in range(16):
                              S.add("pe", lambda e, part=part, bk=bk, k=k, jl=jl, a=a, b=b, n=n, Wg=Wg: e.matmul(
                                  bank(bk, n=n), lhsT=Wg[:, part, k, jl * 128:(jl + 1) * 128], rhs=hT[:, k, a:b],
                                  start=(k == 0), stop=(k == 15)),
                                  reads=[("R", ws)] + [("hT", t["lt"], k) for t in tl], writes=[psk(bk)])
                      for c in range(8):
                          S.add("pe", lambda e, c=c, jl=jl, aa=aa, n=n, ao=ao: e.matmul(
                              bank(2, n=n), lhsT=ao[:, c, jl * 128:(jl + 1) * 128], rhs=AT[:, c, aa:aa + n],
                              start=(c == 0), stop=(c == 7)),
                              reads=[("wca", ws)] + [("AT", t["lt"]) for t in tl], writes=[psk(2)])
                      s0 = ctr["sg"] % 4
                      s1 = (ctr["sg"] + 1) % 4
                      ctr["sg"] += 2
                      S.add("act", lambda e, n=n, s0=s0: e.activation(out=sg[:, s0, 0:n], in_=bank(1, n=n), func=AF.Sigmoid),
                            reads=[psk(1)], writes=[("sg", s0)])
                      S.add("dve", lambda e, n=n, s0=s0: e.tensor_tensor(out=m12[:, 0, 0:n], in0=sg[:, s0, 0:n], in1=bank(0, n=n), op=ALU.mult),
                            reads=[("sg", s0), psk(0), ("m12", 0)], writes=[("m12", 0)])
                      S.add("act", lambda e, n=n, s1=s1: e.activation(out=sg[:, s1, 0:n], in_=bank(3, n=n), func=AF.Sigmoid),
                            reads=[psk(3)], writes=[("sg", s1)])
                      S.add("dve", lambda e, n=n, s1=s1: e.tensor_tensor(out=m12[:, 1, 0:n], in0=sg[:, s1, 0:n], in1=bank(2, n=n), op=ALU.mult),
                            reads=[("sg", s1), psk(2), ("m12", 1)], writes=[("m12", 1)])
                      S.add("pool", lambda e, n=n, jg=jg, a=a, b=b: e.tensor_tensor(
                          out=actT[:, jg, a:b], in0=m12[:, 0, 0:n], in1=m12[:, 1, 0:n], op=ALU.add),
                          reads=[("m12", 0), ("m12", 1)], writes=[("actT", jg, t["lt"]) for t in tl])
              if u + 2 < 8:
                  ld_gate(u + 2)
              if u == 7:
                  load_wpair(w_upt[1], 0, 0)
              if u % 2 == 1:
                  gi = u // 2
                  if gi == 3:
                      n2b_()
                  down_accum(full, 4, eslot(gi), 1.0, after=(n2a_ if gi == 3 else None))
                  if gi + 2 < 4:
                      load_wdn(w_o_v, 4 * (gi + 2), 4, eslot(gi))

          stop_at("stageE")
          st_ = (P == 0)
          emit_ffn(1, full, cb, ncols, stage_last=st_, first_loaded=True)
          for t in full:
              lt, gt, R = t["lt"], t["gt"], t["R"]
              dst = yp[(gt - 1) * 128:gt * 128, :] if t["kind"] == "main" else ys[:, :]
              srcb = xstage if st_ else x_sb
              rk = [("xs", lt, 0), ("xs", lt, 1)] if st_ else xk(lt)
              S.add("sp", lambda e, lt=lt, R=R, dst=dst, srcb=srcb: [e.dma_start(out=dst, in_=srcb[0:R, lt, :])],
                    reads=rk, ndma=1, stream=("out", P, lt))

    except _Stop:
        for lt in range(1, 5):
            S.add("sp", lambda e, lt=lt: [e.dma_start(out=yp[(lt - 1) * 128:lt * 128, :], in_=x_sb[:, lt, :])],
                  reads=xk(lt), ndma=1, stream=("out", "dbg", lt))
    S.emit(nc, final_streams=("out", "('out'"))
    es.close()
    return nc


_CACHE = {}


def _consts():
    h = np.arange(1, 17, dtype=np.float64)
    slopes = np.exp2(-8.0 * h / 16.0)
    ik = np.arange(128)[:, None, None]
    iq = np.arange(128)[None, None, :]
    sl = slopes[None, :, None]
    dP = 128 + iq - ik
    dC = iq - ik
    mP = np.where(ik >= iq, np.exp(-sl * dP), 0.0).astype(np.float32).reshape(128, 2048)
    mC = np.where(ik <= iq, np.exp(-sl * dC), 0.0).astype(np.float32).reshape(128, 2048)
    return mP, mC


def kernel(x_prompt, x_sample, state_conv, cache_k_win, cache_v_win, meta_tokens,
           ffn1_norm, ffn1_w_up, ffn1_w_down, mix_norm, w_in, q_norm, k_norm,
           conv_w, w_conv_out, attn_sinks, w_attn_out, w_o,
           ffn2_norm, ffn2_w_up, ffn2_w_down):
    if "nc" not in _CACHE:
        _CACHE["nc"] = build_program()
    nc = _CACHE["nc"]
    in_maps = make_in_maps(x_prompt, x_sample, state_conv, cache_k_win, cache_v_win, meta_tokens,
                           ffn1_norm, ffn1_w_up, ffn1_w_down, mix_norm, w_in, q_norm, k_norm,
                           conv_w, w_conv_out, attn_sinks, w_attn_out, w_o,
                           ffn2_norm, ffn2_w_up, ffn2_w_down)
    res = run_bass_kernel_spmd(nc, in_maps, core_ids=list(range(NCORES)))
    return assemble(res.results)


def make_in_maps(x_prompt, x_sample, state_conv, cache_k_win, cache_v_win, meta_tokens,
                 ffn1_norm, ffn1_w_up, ffn1_w_down, mix_norm, w_in, q_norm, k_norm,
                 conv_w, w_conv_out, attn_sinks, w_attn_out, w_o,
                 ffn2_norm, ffn2_w_up, ffn2_w_down):
    f = lambda a: np.ascontiguousarray(np.asarray(a, dtype=np.float32))
    x_prompt, x_sample, state_conv = f(x_prompt), f(x_sample), f(state_conv)
    cache_k_win, cache_v_win, meta_tokens = f(cache_k_win), f(cache_v_win), f(meta_tokens)
    mP, mC = _consts()
    gt = lambda g: f(g).reshape(16, 128).T
    gtab = np.ascontiguousarray(np.concatenate([gt(ffn1_norm[0]), gt(mix_norm[0]), gt(ffn2_norm[0])], axis=1))
    cw = np.ascontiguousarray(f(conv_w)[0].reshape(3, 8, 128).transpose(2, 1, 0).reshape(128, 24))
    wi = f(w_in)[0]

    def tile_up(w):
        w4 = f(w)[0].reshape(16, 128, 2, DFF)
        main = np.ascontiguousarray(w4[:, :, :, :21 * 256].reshape(16, 128, 2, 21, 256).transpose(3, 1, 2, 0, 4).reshape(21, 128, 8192))
        last = np.ascontiguousarray(w4[:, :, :, 21 * 256:].transpose(1, 2, 0, 3).reshape(128, 4096))
        return main, last

    w_convt = np.ascontiguousarray(wi[:, 0:3072].reshape(16, 128, 3, 4, 256).transpose(3, 1, 2, 0, 4).reshape(4, 128, 12288))
    w_gatet = np.ascontiguousarray(wi[:, 4608:8704].reshape(16, 128, 2, 8, 256).transpose(3, 1, 2, 0, 4).reshape(8, 128, 8192))
    tile_c = lambda w: np.ascontiguousarray(f(w)[0].reshape(8, 128, 8, 256).transpose(2, 1, 0, 3).reshape(8, 128, 2048))
    up1, up1L = tile_up(ffn1_w_up)
    up2, up2L = tile_up(ffn2_w_up)
    shared = {
        "w_up1t": up1, "w_up1L": up1L, "w_dn1": f(ffn1_w_down)[0], "w_up2t": up2, "w_up2L": up2L, "w_dn2": f(ffn2_w_down)[0],
        "w_qkv": np.ascontiguousarray(wi[:, 3072:4608]), "w_convt": w_convt, "w_gatet": w_gatet,
        "w_cot": tile_c(w_conv_out), "w_aot": tile_c(w_attn_out), "w_o": f(w_o)[0],
        "gtab": gtab, "gvec": np.ascontiguousarray(np.stack([f(ffn1_norm)[0], f(mix_norm)[0], f(ffn2_norm)[0]])), "qg": f(q_norm).reshape(1, 64), "kg": f(k_norm).reshape(1, 64), "cw": cw,
        "sinks": f(attn_sinks).reshape(1, 16), "maskP": mP, "maskC": mC, "ident": np.eye(128, dtype=np.float32),
    }
    in_maps = []
    for c in range(NCORES):
        b, half = c // 2, c % 2
        if half == 0:
            xh = np.zeros((128, D), np.float32)
            xh[112:] = meta_tokens
            hv = np.zeros((128, 64), np.float32)
            hv[112:] = 1.0
        else:
            xh = x_prompt[b, 896:1024]
            hv = np.ones((128, 64), np.float32)
        m = dict(shared)
        m.update({
            "xh": np.ascontiguousarray(xh), "hv": hv,
            "xm": np.ascontiguousarray(x_prompt[b, half * 1024:(half + 1) * 1024]),
            "xs": np.ascontiguousarray(x_sample[4 * c:4 * c + 4].reshape(16, D)),
            "sconv": np.ascontiguousarray(state_conv[0, 4 * c:4 * c + 4].reshape(8, 1024)),
            "ck": np.ascontiguousarray(cache_k_win[0, 4 * c:4 * c + 4].reshape(4, 128, 256)),
            "cv": np.ascontiguousarray(cache_v_win[0, 4 * c:4 * c + 4].reshape(4, 128, 256)),
        })
        in_maps.append(m)
    return in_maps


def assemble(R):
    y_prompt = np.zeros((4, 2048, D), np.float32)
    y_sample = np.zeros((32, 4, D), np.float32)
    ncp = np.zeros((1, 4, 2, 1024), np.float32)
    nkp = np.zeros((1, 4, 128, 4, 64), np.float32)
    nvp = np.zeros((1, 4, 128, 4, 64), np.float32)
    ncs = np.zeros((1, 32, 2, 1024), np.float32)
    nks = np.zeros((1, 32, 128, 4, 64), np.float32)
    nvs = np.zeros((1, 32, 128, 4, 64), np.float32)
    for c in range(NCORES):
        b, half = c // 2, c % 2
        r = R[c]
        y_prompt[b, half * 1024:(half + 1) * 1024] = r["yp"]
        y_sample[4 * c:4 * c + 4] = r["ys"].reshape(4, 4, D)
        ncs[0, 4 * c:4 * c + 4] = r["ocs"].reshape(4, 2, 1024)
        nks[0, 4 * c:4 * c + 4] = r["oks"].reshape(4, 128, 4, 64)
        nvs[0, 4 * c:4 * c + 4] = r["ovs"].reshape(4, 128, 4, 64)
        if half == 1:
            ncp[0, b] = r["ocp"]
            nkp[0, b] = r["okp"].reshape(128, 4, 64)
            nvp[0, b] = r["ovp"].reshape(128, 4, 64)
    return (y_prompt, y_sample, ncp, nkp, nvp, ncs, nks, nvs)
```

```python
import numpy as np
import concourse.bass as bass
import concourse.mybir as mybir
from concourse.bass_utils import run_bass_kernel_spmd

F32 = mybir.dt.float32
BF16 = mybir.dt.bfloat16
AF = mybir.ActivationFunctionType
ALU = mybir.AluOpType
AX = mybir.AxisListType

D = 2048
DFF = 5504
NCH = 43
DIN = 8704
EPS = 1e-6
NCORES = 8


class Op:
    __slots__ = ("eng", "fn", "reads", "writes", "ndma", "stream", "deps", "mile", "cum")

    def __init__(self, eng, fn, reads, writes, ndma, stream):
        self.eng, self.fn, self.reads, self.writes = eng, fn, reads, writes
        self.ndma, self.stream = ndma, stream
        self.deps = set()
        self.mile = 0
        self.cum = 0


class Sched:
    ENGS = ("pe", "act", "dve", "pool", "sp")

    def __init__(self):
        self.ops = []
        self.lw = {}
        self.rd = {}
        self.after = None
        self.last = {}
        self.dmas_since = []

    def barrier(self):
        deps = set(self.last.values()) | set(self.dmas_since)
        i = self.add("dve", self._bfn, nobar=True)
        self.ops[i].deps |= deps
        self.ops[i].deps.discard(i)
        self.after = i
        self.dmas_since = []

    def add(self, eng, fn, reads=(), writes=(), ndma=0, stream=None, nobar=False):
        reads = list(reads) + ([("R1a",), ("R1b",)] if ("R", 1) in reads else [])
        writes = list(writes) + ([("R1a",), ("R1b",)] if ("R", 1) in writes else [])
        op = Op(eng, fn, tuple(reads), tuple(writes), ndma, stream)
        i = len(self.ops)
        if self.after is not None and not nobar:
            op.deps.add(self.after)
        if ndma:
            self.dmas_since.append(i)
        else:
            self.last[eng] = i
        for k in op.reads:
            w = self.lw.get(k)
            if w is not None:
                op.deps.add(w)
            if k[0] == "ps":
                for e2, r in self.rd.get(k, {}).items():
                    if e2 != eng:
                        op.deps.add(r)
        for k in op.writes:
            w = self.lw.get(k)
            if w is not None:
                op.deps.add(w)
            for r in self.rd.get(k, {}).values():
                op.deps.add(r)
        for k in op.reads:
            self.rd.setdefault(k, {})[("dma", i) if ndma else eng] = i
        for k in op.writes:
            self.lw[k] = i
            self.rd[k] = {}
        op.deps.discard(i)
        self.ops.append(op)
        return i

    def emit(self, nc, final_streams):
        ops = self.ops
        need = [False] * len(ops)
        for i, op in enumerate(ops):
            for j in op.deps:
                pj = ops[j]
                if pj.ndma:
                    continue
                if pj.eng == "pe" and op.eng == "pe" and not op.ndma:
                    continue
                need[j] = True
        cnt = {e: 0 for e in self.ENGS}
        scum = {}
        for i, op in enumerate(ops):
            if op.ndma:
                scum[op.stream] = scum.get(op.stream, 0) + 16 * op.ndma
                op.cum = scum[op.stream]
            elif need[i]:
                cnt[op.eng] += 1
                op.mile = cnt[op.eng]
        streams = sorted(scum.keys(), key=str)
        import contextlib
        with contextlib.ExitStack() as es:
            esem = {e: es.enter_context(nc.semaphore("s_" + e)) for e in self.ENGS}
            ssem = {s: es.enter_context(nc.semaphore("d_" + str(n))) for n, s in enumerate(streams)}
            block = es.enter_context(nc.Block())
            per_eng = {e: [] for e in self.ENGS}
            for i, op in enumerate(ops):
                per_eng[op.eng].append(i)

            def run(e_name, eh):
                waited = {}
                for i in per_eng[e_name]:
                    op = ops[i]
                    reqs = {}
                    for j in op.deps:
                        pj = ops[j]
                        if pj.ndma:
                            key = ("s", pj.stream)
                            val = pj.cum
                        else:
                            if pj.eng == "pe" and e_name == "pe" and not op.ndma:
                                continue
                            key = ("e", pj.eng)
                            val = pj.mile
                        if val > reqs.get(key, 0):
                            reqs[key] = val
                    for key, val in reqs.items():
                        if waited.get(key, 0) >= val:
                            continue
                        waited[key] = val
                        sem = ssem[key[1]] if key[0] == "s" else esem[key[1]]
                        eh.wait_ge(sem, val)
                    r = op.fn(eh)
                    if op.ndma:
                        for ins in r:
                            ins.then_inc(ssem[op.stream], 16)
                    elif need[i]:
                        r.then_inc(esem[e_name], 1)
                if e_name == "sp":
                    for s in streams:
                        if str(s).startswith(final_streams):
                            eh.wait_ge(ssem[s], scum[s])

            @block.tensor
            def _(eh):
                run("pe", eh)

            @block.scalar
            def _(eh):
                run("act", eh)

            @block.vector
            def _(eh):
                run("dve", eh)

            @block.gpsimd
            def _(eh):
                run("pool", eh)

            @block.sync
            def _(eh):
                run("sp", eh)


KSTOP = None


class _Stop(Exception):
    pass


def build_program():
    nc = bass.Bass("TRN2", target_bir_lowering=False)
    S = Sched()

    def stop_at(name):
        if KSTOP == name:
            raise _Stop()

    def din(name, shape, dt=F32):
        return nc.dram_tensor(name, list(shape), dt, kind="ExternalInput").ap()

    def dout(name, shape, dt=F32):
        return nc.dram_tensor(name, list(shape), dt, kind="ExternalOutput").ap()

    xh = din("xh", [128, D]); xm = din("xm", [1024, D]); xs = din("xs", [16, D])
    sconv = din("sconv", [8, 1024]); ck = din("ck", [4, 128, 256]); cv = din("cv", [4, 128, 256])
    w_upt = [din("w_up1t", [21, 128, 8192]), din("w_up2t", [21, 128, 8192])]
    w_upL = [din("w_up1L", [128, 4096]), din("w_up2L", [128, 4096])]
    w_dn = [din("w_dn1", [DFF, D]), din("w_dn2", [DFF, D])]
    w_qkv = din("w_qkv", [D, 1536]); w_convt = din("w_convt", [4, 128, 12288]); w_gatet = din("w_gatet", [8, 128, 8192])
    w_cot = din("w_cot", [8, 128, 2048]); w_aot = din("w_aot", [8, 128, 2048])
    w_o = din("w_o", [D, D])
    gtab_d = din("gtab", [128, 48]); qg_d = din("qg", [1, 64]); kg_d = din("kg", [1, 64])
    cw_d = din("cw", [128, 24]); sinks_d = din("sinks", [1, 16])
    maskP_d = din("maskP", [128, 2048]); maskC_d = din("maskC", [128, 2048])
    hv_d = din("hv", [128, 64]); ident_d = din("ident", [128, 128])

    yp = dout("yp", [1024, D]); ys = dout("ys", [16, D])
    ocp = dout("ocp", [2, 1024]); okp = dout("okp", [128, 256]); ovp = dout("ovp", [128, 256])
    ocs = dout("ocs", [8, 1024]); oks = dout("oks", [4, 128, 256]); ovs = dout("ovs", [4, 128, 256])

    import contextlib
    es = contextlib.ExitStack()
    es.enter_context(nc.allow_low_precision("bf16 matmul operands, fp32 accumulation"))

    def sb(name, shape, dt):
        return es.enter_context(nc.sbuf_tensor(name, list(shape), dt))

    x_sb = sb("x_sb", [128, 5, D], F32)
    hT = sb("hT", [128, 16, 640], BF16)
    actT = sb("actT", [128, 4, 640], BF16)
    ring = sb("ring", [128, 4, 8192], BF16)
    sg = sb("sg", [128, 4, 320], F32)
    stat = sb("stat", [128, 768], F32)
    gtab = sb("gtab_s", [128, 48], F32)
    qg_bc = sb("qg_bc", [128, 64], F32); kg_bc = sb("kg_bc", [128, 64], F32)
    cw = sb("cw_s", [128, 24], F32)
    est = sb("est", [128, 16], F32)
    hv_f = sb("hv_f", [128, 64], F32); hvones = sb("hvones", [128, 64], BF16); ones = sb("ones", [128, 64], BF16)
    identf = sb("identf", [128, 128], F32); identb = sb("identb", [128, 128], BF16)
    bscr = sb("bscr", [128, 2], F32)
    KT3 = sb("KT3", [64, 4, 3, 128], BF16)
    V3 = sb("V3", [128, 3, 256], BF16)
    AT = sb("AT", [128, 8, 528], BF16)
    YB = sb("YB", [128, 8, 528], BF16)
    Kn = sb("Kn", [128, 2, 256], F32); Vf = sb("Vf", [128, 2, 256], F32)
    stT = sb("stT", [128, 8, 8], F32)
    us = sb("us", [128, 4, 6], F32); ysm = sb("ysm", [128, 4, 4], F32)
    ucarry = sb("ucarry", [128, 8, 2], F32)
    ucat = sb("ucat", [128, 8, 10], F32)
    T = sb("T", [128, 10496], F32)
    ps = es.enter_context(nc.psum_tensor("ps", [128, 4096], F32))
    S._bfn = lambda e: e.memset(bscr[:], 0.0)

    maskv = ring[:, 3, :].bitcast(F32).rearrange("p (m n) -> p m n", n=2048)
    E = T[:, 0:2048].rearrange("p (a b c) -> p a b c", a=2, b=2)
    sqb = T[:, 2048:3072]
    dtmp = T[:, 3072:4096].rearrange("p (a c) -> p a c", a=2)
    ckf = T[:, 0:1024].rearrange("p (s c) -> p s c", s=4)
    kt1 = T[:, 4096:4352]
    PT = T[:, 4352:5376].bitcast(BF16).rearrange("p (a b c) -> p a b c", a=2, b=2)
    Qn = T[:, 5376:6400].bitcast(BF16).rearrange("p (a c) -> p a c", a=2)
    QTt = T[0:64, 6400:8448].bitcast(BF16).rearrange("p (a h c) -> p a h c", a=2, h=16)
    KTc = T[0:64, 8448:9472].bitcast(BF16).rearrange("p (s k c) -> p s k c", s=4, k=4)
    Vc = T[:, 9472:9984].bitcast(BF16).rearrange("p (s c) -> p s c", s=4)
    Vn = T[0:4, 9984:10496].bitcast(BF16).rearrange("p (s c) -> p s c", s=4)
    sct = T[0:8, 0:1024]
    ub = T[:, 1024:2112].rearrange("p (i n) -> p i n", i=2)
    bgs = T[:, 2112:3200].rearrange("p (i n) -> p i n", i=2)
    xcs = T[:, 3200:3840].rearrange("p (i n) -> p i n", i=2)
    yt = T[:, 3840:4864].rearrange("p (i n) -> p i n", i=2)
    uout = T[0:10, 3840:4864]
    wca = T[:, 4864:8960].bitcast(BF16).rearrange("p (s n) -> p s n", s=2)
    m12 = T[:, 8960:9600].rearrange("p (i n) -> p i n", i=2)

    def slot_ap(ws):
        if ws < 4:
            return ring[:, ws, :]
        return T[:, 0:4096].bitcast(BF16)

    def bank(b, rows=128, n=512, r0=0):
        return ps[r0:r0 + rows, b * 512:b * 512 + n]

    def psk(b):
        return ("ps", b)

    def ld_tables(e):
        r = []
        r.append(e.dma_start(out=gtab[:], in_=gtab_d[:, :]))
        r.append(e.dma_start(out=qg_bc[:], in_=qg_d[0:1, :].partition_broadcast(128)))
        r.append(e.dma_start(out=kg_bc[:], in_=kg_d[0:1, :].partition_broadcast(128)))
        r.append(e.dma_start(out=cw[:], in_=cw_d[:, :]))
        r.append(e.dma_start(out=est[:], in_=sinks_d[0:1, :].partition_broadcast(128)))
        r.append(e.dma_start(out=hv_f[:], in_=hv_d[:, :]))
        r.append(e.dma_start(out=identf[:], in_=ident_d[:, :]))
        return r
    S.add("sp", ld_tables, writes=[("tab",)], ndma=7, stream="tab")
    S.add("dve", lambda e: e.tensor_copy(out=identb[:], in_=identf[:]), reads=[("tab",)], writes=[("identb",)])
    S.add("dve", lambda e: e.tensor_copy(out=hvones[:], in_=hv_f[:]), reads=[("tab",)], writes=[("hvones",)])
    S.add("dve", lambda e: e.memset(ones[:], 1.0), writes=[("ones",)])
    S.add("act", lambda e: e.activation(out=est[:], in_=est[:], func=AF.Exp), reads=[("tab",)], writes=[("est",)])

    w_dn_v = [w.rearrange("(c p) n -> p c n", p=128) for w in w_dn]
    w_qkv_v = w_qkv.rearrange("(k p) c -> p k c", p=128)
    w_o_v = w_o.rearrange("(c p) n -> p c n", p=128)

    def xk(lt):
        return [("x", lt, 0), ("x", lt, 1)]

    statc = [0]

    def newstat(n=1):
        c = statc[0]
        statc[0] += n
        if statc[0] > 768:
            c = 0
            statc[0] = n
        return c

    ctr = {"xn": 0, "sg": 0, "up": 0}

    def tiles_overlapping(tiles, a, b):
        return [t for t in tiles if t["col0"] < b and t["col0"] + t["R"] > a]

    gvec_d = din("gvec", [3, D])
    g_bc = ring[:, 3, 0:4096].bitcast(F32)
    hb = ring[:, 3, 4096:8192].rearrange("p (i n) -> p i n", i=2)
    psb0 = ps[:, 0:512].bitcast(BF16)
    psb1 = ps[:, 512:1024].bitcast(BF16)

    def norm_pipe(tiles, gcol):
        info = []
        for t in tiles:
            i = ctr["xn"] % 2
            ctr["xn"] += 1
            info.append((t, i, newstat(3)))

        def begin():
            S.add("sp", lambda e, gcol=gcol: [e.dma_start(out=g_bc, in_=gvec_d[gcol:gcol + 1, :].partition_broadcast(128))],
                  writes=[("R", 3)], ndma=1, stream="gbc", nobar=True)

        def stage1(n_):
            t, i, sc = info[n_]
            lt, R = t["lt"], t["R"]
            S.add("dve", lambda e, lt=lt, R=R, sc=sc, i=i: e.scalar_tensor_tensor(
                out=hb[0:R, i, :], in0=x_sb[0:R, lt, :], scalar=1.0, in1=x_sb[0:R, lt, :],
                op0=ALU.mult, op1=ALU.mult, accum_out=stat[0:R, sc:sc + 1]),
                reads=xk(lt) + [("R", 3)], writes=[("st", sc), ("hb", i)])
            S.add("act", lambda e, R=R, sc=sc: e.activation(
                out=stat[0:R, sc + 1:sc + 2], in_=stat[0:R, sc:sc + 1], func=AF.Sqrt, scale=1.0 / D, bias=EPS),
                reads=[("st", sc)], writes=[("st", sc + 1)])
            S.add("dve", lambda e, R=R, sc=sc: e.reciprocal(out=stat[0:R, sc + 2:sc + 3], in_=stat[0:R, sc + 1:sc + 2]),
                  reads=[("st", sc + 1)], writes=[("st", sc + 2)])
            S.add("dve", lambda e, lt=lt, R=R, sc=sc, i=i: e.scalar_tensor_tensor(
                out=hb[0:R, i, :], in0=x_sb[0:R, lt, :], scalar=stat[0:R, sc + 2:sc + 3], in1=g_bc[0:R, :],
                op0=ALU.mult, op1=ALU.mult),
                reads=xk(lt) + [("st", sc + 2), ("R", 3)], writes=[("hb", i)])

        def stage2(n_):
            t, i, sc = info[n_]
            lt, R, col0 = t["lt"], t["R"], t["col0"]
            for k in range(16):
                pb, b = (psb0, 0) if k < 8 else (psb1, 1)
                S.add("pe", lambda e, R=R, i=i, k=k, pb=pb: e.transpose(
                    out=pb[:, (k % 8) * 128:(k % 8) * 128 + R], in_=hb[0:R, i, k * 128:(k + 1) * 128],
                    identity=identb[0:R, 0:R]),
                    reads=[("hb", i), ("identb",), ("R", 3)], writes=[psk(b)])
            for b, pb in ((0, psb0), (1, psb1)):
                S.add("act", lambda e, R=R, b=b, pb=pb, col0=col0: e.activation(
                    out=hT[:, b * 8:b * 8 + 8, col0:col0 + R],
                    in_=pb.rearrange("p (a c) -> p a c", c=128)[:, :, 0:R], func=AF.Copy),
                    reads=[psk(b)], writes=[("hT", lt, b * 8 + kk) for kk in range(8)])

        return begin, stage1, stage2, len(info)

    def emit_norm(tiles, gcol):
        begin, stage1, stage2, n = norm_pipe(tiles, gcol)
        begin()
        stage1(0)
        for n_ in range(n):
            if n_ + 1 < n:
                stage1(n_ + 1)
            stage2(n_)

    def norm_hooks(tiles, gcol):
        begin, stage1, stage2, n = norm_pipe(tiles, gcol)

        def after(idx):
            stage1(idx)
            if idx >= 1:
                stage2(idx - 1)
            if idx == n - 1:
                stage2(idx)
        return begin, after

    def split2(a, b):
        m = a + ((b - a + 1) // 2)
        return [(a, m), (m, b)]

    xstage = T[:, 0:10240].rearrange("p (t n) -> p t n", n=D)

    def down_accum(tiles, nloc, wslot, scale, after=None, stage=False):
        Wd = slot_ap(wslot).rearrange("p (c n) -> p c n", n=D)
        for idx, t in enumerate(tiles):
            lt, R, col0 = t["lt"], t["R"], t["col0"]
            for half in range(2):
                b0 = 4 + 2 * half
                for cl in range(nloc):
                    for n in range(2):
                        S.add("pe", lambda e, R=R, col0=col0, cl=cl, n=n, half=half, b0=b0, Wd=Wd, nloc=nloc: e.matmul(
                            bank(b0 + n, rows=R), lhsT=actT[:, cl, col0:col0 + R],
                            rhs=Wd[:, cl, (2 * half + n) * 512:(2 * half + n + 1) * 512],
                            start=(cl == 0), stop=(cl == nloc - 1)),
                            reads=[("actT", cl, lt), ("R", wslot)], writes=[psk(b0 + n)])
                dst = xstage if stage else x_sb
                wk = [("xs", lt, half), ("wca", 0), ("wca", 1), ("m12", 0), ("m12", 1)] if stage else [("x", lt, half)]
                S.add("dve", lambda e, R=R, lt=lt, half=half, b0=b0, scale=scale, dst=dst: e.scalar_tensor_tensor(
                    out=dst[0:R, lt, half * 1024:(half + 1) * 1024], in0=ps[0:R, b0 * 512:b0 * 512 + 1024],
                    scalar=scale, in1=x_sb[0:R, lt, half * 1024:(half + 1) * 1024], op0=ALU.mult, op1=ALU.add),
                    reads=[psk(b0), psk(b0 + 1), ("x", lt, half)], writes=wk)
            if after is not None:
                after(idx)

    def load_wdn(src_v, c0, ng, wslot, extra=()):
        def f(e, c0=c0, ng=ng, wslot=wslot):
            Wd = slot_ap(wslot).rearrange("p (c n) -> p c n", n=D)
            return [e.dma_start(out=Wd[:, j:j + 1, :], in_=src_v[:, c0 + j:c0 + j + 1, :]) for j in range(ng)]
        wk = [("R", wslot)]
        if wslot == 4:
            wk += [("xs", lt_, h_) for lt_ in range(5) for h_ in range(2)] + [("uout",)]
        S.add("pool", f, reads=list(extra), writes=wk, ndma=ng, stream=("R", wslot), nobar=True)

    def load_wpair(src_t, u, wslot, extra=()):
        def f(e):
            dst = ring[:, wslot, :].rearrange("p (a n) -> p a n", n=2048)
            src = src_t[u].rearrange("p (a n) -> p a n", n=2048)
            return [e.dma_start(out=dst[:, 0:2, :], in_=src[:, 0:2, :]), e.dma_start(out=dst[:, 2:4, :], in_=src[:, 2:4, :])]
        S.add("pool", f, reads=list(extra), writes=[("R", wslot)], ndma=2, stream=("R", wslot), nobar=True)

    def load_wlast(src, wslot, extra=()):
        def f(e):
            dst = ring[:, wslot, 0:4096].rearrange("p (a n) -> p a n", n=2048)
            return [e.dma_start(out=dst, in_=src.rearrange("p (a n) -> p a n", n=2048))]
        S.add("pool", f, reads=list(extra), writes=[("R", wslot)], ndma=1, stream=("R", wslot), nobar=True)

    def emit_ffn(fi, tiles, col_lo, col_hi, last_after=None, last_pre=None, stage_last=False, first_loaded=False):
        slices = split2(col_lo, col_hi)
        units = [(u, 2 if 2 * u + 1 < NCH else 1) for u in range((NCH + 1) // 2)]
        ngroups = (NCH + 3) // 4

        def ld_unit(u, extra=()):
            if units[u][1] == 1:
                load_wlast(w_upL[fi], u % 2, extra=extra)
            else:
                load_wpair(w_upt[fi], u, u % 2, extra=extra)

        gslot = lambda gi: 4 if (fi == 0 and gi == ngroups - 1) else 2 + gi % 2

        def ld_group(gi, extra=()):
            ng = min(4, NCH - 4 * gi)
            load_wdn(w_dn_v[fi], 4 * gi, ng, gslot(gi), extra=extra)

        kst = ("ffn_started", ctr["up"])
        if not first_loaded:
            ld_unit(0)
        for u, nchk in units:
            wslot = u % 2
            if nchk == 2:
                Wv = ring[:, wslot, :].rearrange("p (a k c) -> p a k c", a=2, k=16)
            else:
                Wv = ring[:, wslot, 0:4096].rearrange("p (a k c) -> p a k c", a=2, k=16)
            for cl in range(nchk):
                c = 2 * u + cl
                cg = c % 4
                for (a, b) in slices:
                    n = b - a
                    par = ctr["up"] % 2
                    ctr["up"] += 1
                    gb, ubk = (0, 1) if par == 0 else (2, 3)
                    si = ctr["sg"] % 4
                    ctr["sg"] += 1
                    tl = tiles_overlapping(tiles, a, b)
                    for part, bk in ((0, gb), (1, ubk)):
                        for k in range(16):
                            first_mm = (u == 0 and cl == 0 and part == 0 and k == 0 and a == slices[0][0])
                            S.add("pe", lambda e, part=part, bk=bk, k=k, cl=cl, a=a, b=b, n=n, Wv=Wv: e.matmul(
                                bank(bk, n=n), lhsT=Wv[:, part, k, cl * 128:(cl + 1) * 128], rhs=hT[:, k, a:b],
                                start=(k == 0), stop=(k == 15)),
                                reads=[("R", wslot)] + [("hT", t["lt"], k) for t in tl],
                                writes=[psk(bk)] + ([kst] if first_mm else []))
                            if first_mm:
                                ld_unit(1, extra=[kst]); ld_group(0, extra=[kst]); ld_group(1, extra=[kst])
                    S.add("act", lambda e, gb=gb, n=n, si=si: e.activation(out=sg[:, si, 0:n], in_=bank(gb, n=n), func=AF.Silu),
                          reads=[psk(gb)], writes=[("sg", si)])
                    S.add("dve", lambda e, ubk=ubk, n=n, si=si, cg=cg, a=a, b=b: e.tensor_tensor(
                        out=actT[:, cg, a:b], in0=sg[:, si, 0:n], in1=bank(ubk, n=n), op=ALU.mult),
                        reads=[("sg", si), psk(ubk)], writes=[("actT", cg, t["lt"]) for t in tl])
            if u + 2 < len(units):
                ld_unit(u + 2)
            c_last = 2 * u + nchk - 1
            if c_last % 4 == 3 or c_last == NCH - 1:
                gi = c_last // 4
                is_last = (c_last == NCH - 1)
                if is_last and last_pre is not None:
                    last_pre()
                down_accum(tiles, c_last % 4 + 1, gslot(gi), 0.5,
                           after=(last_after if is_last else None), stage=(stage_last and is_last))
                if gi + 2 < ngroups:
                    ld_group(gi + 2)

    kv_ring = {"i": 0}
    try:
      for P in range(2):
          if P == 0:
              tiles = [dict(lt=0, gt=0, R=128, col0=0, kind="halo")] + \
                      [dict(lt=i, gt=i, R=128, col0=128 * i, kind="main") for i in range(1, 5)]
              cb = 128
          else:
              tiles = [dict(lt=i, gt=5 + i, R=128, col0=128 * i, kind="main") for i in range(4)] + \
                      [dict(lt=4, gt=9, R=16, col0=512, kind="samp")]
              cb = 0
          full = [t for t in tiles if t["kind"] != "halo"]
          ncols = tiles[-1]["col0"] + tiles[-1]["R"]

          for t in tiles:
              lt, gt, R = t["lt"], t["gt"], t["R"]
              if t["kind"] == "halo":
                  src = xh[:, :]
              elif t["kind"] == "main":
                  src = xm[(gt - 1) * 128:gt * 128, :]
              else:
                  src = xs[:, :]
              S.add("sp", lambda e, lt=lt, R=R, src=src: [e.dma_start(out=x_sb[0:R, lt, :], in_=src)],
                    writes=xk(lt), ndma=1, stream=("x", lt), nobar=True)

          if P == 1:
              S.add("sp", lambda e: [e.dma_start(out=oks[:, 0:124, :], in_=ck[:, 4:128, :]),
                                     e.dma_start(out=ovs[:, 0:124, :], in_=cv[:, 4:128, :])],
                    ndma=2, stream="out", nobar=True)

          emit_norm(tiles, 0)
          stop_at("norm1")
          nb_, na_ = norm_hooks(tiles, 1)
          emit_ffn(0, tiles, 0, ncols, last_after=na_, last_pre=nb_)
          stop_at("ffn1")

          def ld_gate(u, extra=()):
              load_wpair(w_gatet, u, u % 2, extra=extra)
              ws = u % 2

              def f(e, u=u, ws=ws):
                  return [e.dma_start(out=wca[:, ws, 0:2048], in_=w_cot[u]),
                          e.dma_start(out=wca[:, ws, 2048:4096], in_=w_aot[u])]
              S.add("pool", f, reads=list(extra), writes=[("wca", ws)], ndma=2, stream=("wca", ws))
          Rflat = ring[:].rearrange("p s n -> p (s n)")
          Wqkv = Rflat[:, 0:24576].rearrange("p (k c) -> p k c", c=1536)

          def ld_qkv(e):
              return [e.dma_start(out=Wqkv[:, kq * 4:kq * 4 + 4, :], in_=w_qkv_v[:, kq * 4:kq * 4 + 4, :])
                      for kq in range(4)]
          S.add("pool", ld_qkv, writes=[("R", 0), ("R", 1), ("R", 2)], ndma=4, stream="qkv", nobar=True)
          rqkv = [("R", 0), ("R", 1), ("R", 2)]
          S.barrier()
          def ld_masks():
              S.add("sp", lambda e: [e.dma_start(out=maskv[:, 0, :], in_=maskP_d[:, :]), e.dma_start(out=maskv[:, 1, :], in_=maskC_d[:, :])],
                    reads=[psk(2)], writes=[("R", 3)], ndma=2, stream="mask", nobar=True)

          if P == 1:
              S.add("sp", lambda e: [e.dma_start(out=ckf, in_=ck.rearrange("s p c -> p s c"))],
                    writes=[("E", 0, 0), ("E", 0, 1)], ndma=1, stream="cache")
              S.add("pool", lambda e: [e.dma_start(out=Vc, in_=cv.rearrange("s p c -> p s c"))],
                    writes=[("Vc",)], ndma=1, stream="cachev")
              for sbi in range(4):
                  for kvh in range(4):
                      S.add("pe", lambda e, sbi=sbi, kvh=kvh: e.transpose(
                          out=ps[0:64, 5 * 512 + kvh * 128:5 * 512 + (kvh + 1) * 128],
                          in_=ckf[:, sbi, kvh * 64:(kvh + 1) * 64], identity=identf[:, :]),
                          reads=[("E", 0, 0), ("E", 0, 1), ("tab",)], writes=[psk(5)])
                  S.add("dve", lambda e, sbi=sbi: e.tensor_copy(
                      out=KTc[:, sbi, :, :], in_=ps[0:64, 5 * 512:6 * 512].rearrange("p (a c) -> p a c", c=128)),
                      reads=[psk(5)], writes=[("KTc",)])

          for t in tiles:
              t["j"] = kv_ring["i"] % 2
              kv_ring["i"] += 1
              t["g3"] = t["gt"] % 3
              t["full"] = t["kind"] != "halo"
              if t["full"]:
                  t["sc"] = newstat(48)
              t["sk"] = newstat(12)

          def A_mm(t, n):
              lt, R, col0 = t["lt"], t["R"], t["col0"]
              for k in range(16):
                  S.add("pe", lambda e, n=n, k=k, R=R, col0=col0: e.matmul(
                      bank(n, rows=R), lhsT=hT[:, k, col0:col0 + R], rhs=Wqkv[:, k, n * 512:(n + 1) * 512],
                      start=(k == 0), stop=(k == 15)),
                      reads=rqkv + [("hT", lt, k)], writes=[psk(n)], nobar=True)

          def A_q1(t):
              R, sc = t["R"], t["sc"]
              S.add("act", lambda e, R=R: e.activation(out=sqb[0:R, :], in_=ps[0:R, 0:1024], func=AF.Square),
                    reads=[psk(0), psk(1)], writes=[("sqb",)])
              S.add("dve", lambda e, R=R, sc=sc: e.tensor_reduce(
                  out=stat[0:R, sc:sc + 16], in_=sqb[0:R, :].rearrange("p (h d) -> p h d", d=64), op=ALU.add, axis=AX.X),
                  reads=[("sqb",)], writes=[("st", sc)])
              S.add("act", lambda e, R=R, sc=sc: e.activation(
                  out=stat[0:R, sc + 16:sc + 32], in_=stat[0:R, sc:sc + 16], func=AF.Sqrt, scale=1.0, bias=64.0 * EPS),
                  reads=[("st", sc)], writes=[("st", sc + 16)])
              S.add("dve", lambda e, R=R, sc=sc: e.reciprocal(out=stat[0:R, sc + 32:sc + 48], in_=stat[0:R, sc + 16:sc + 32]),
                    reads=[("st", sc + 16)], writes=[("st", sc + 32)])

          def A_q2(t):
              R, sc, j = t["R"], t["sc"], t["j"]
              S.add("dve", lambda e, R=R, sc=sc: e.tensor_tensor(
                  out=sqb[0:R, :].rearrange("p (h d) -> p h d", d=64),
                  in0=ps[0:R, 0:1024].rearrange("p (h d) -> p h d", d=64),
                  in1=stat[0:R, sc + 32:sc + 48].unsqueeze(2).broadcast_to([R, 16, 64]), op=ALU.mult),
                  reads=[psk(0), psk(1), ("st", sc + 32), ("sqb",)], writes=[("sqb",)])
              S.add("pool", lambda e, R=R, j=j: e.tensor_tensor(
                  out=Qn[0:R, j, :].rearrange("p (h d) -> p h d", d=64),
                  in0=sqb[0:R, :].rearrange("p (h d) -> p h d", d=64),
                  in1=qg_bc[0:R, :].unsqueeze(1).broadcast_to([R, 16, 64]), op=ALU.mult),
                  reads=[("sqb",), ("tab",)], writes=[("Qn", j)])

          def A_k(t):
              R, sk, j = t["R"], t["sk"], t["j"]
              S.add("act", lambda e, R=R: e.activation(out=kt1[0:R, :], in_=ps[0:R, 1024:1280], func=AF.Square),
                    reads=[psk(2)], writes=[("kt1",)])
              S.add("dve", lambda e, R=R, sk=sk: e.tensor_reduce(
                  out=stat[0:R, sk:sk + 4], in_=kt1[0:R, :].rearrange("p (h d) -> p h d", d=64), op=ALU.add, axis=AX.X),
                  reads=[("kt1",)], writes=[("st", sk)])
              S.add("act", lambda e, R=R, sk=sk: e.activation(
                  out=stat[0:R, sk + 4:sk + 8], in_=stat[0:R, sk:sk + 4], func=AF.Sqrt, scale=1.0 / 64, bias=EPS),
                  reads=[("st", sk)], writes=[("st", sk + 4)])
              S.add("dve", lambda e, R=R, sk=sk: e.reciprocal(out=stat[0:R, sk + 8:sk + 12], in_=stat[0:R, sk + 4:sk + 8]),
                    reads=[("st", sk + 4)], writes=[("st", sk + 8)])
              S.add("dve", lambda e, R=R, sk=sk: e.tensor_tensor(
                  out=kt1[0:R, :].rearrange("p (h d) -> p h d", d=64),
                  in0=ps[0:R, 1024:1280].rearrange("p (h d) -> p h d", d=64),
                  in1=stat[0:R, sk + 8:sk + 12].unsqueeze(2).broadcast_to([R, 4, 64]), op=ALU.mult),
                  reads=[psk(2), ("st", sk + 8), ("kt1",)], writes=[("kt1",)])
              S.add("dve", lambda e, R=R, j=j: e.tensor_tensor(
                  out=Kn[0:R, j, :].rearrange("p (h d) -> p h d", d=64),
                  in0=kt1[0:R, :].rearrange("p (h d) -> p h d", d=64),
                  in1=kg_bc[0:R, :].unsqueeze(1).broadcast_to([R, 4, 64]), op=ALU.mult),
                  reads=[("kt1",), ("tab",)], writes=[("Kn", j)])

          def A_v(t):
              R, j, g3, gt = t["R"], t["j"], t["g3"], t["gt"]
              S.add("act", lambda e, R=R, j=j: e.activation(out=Vf[0:R, j, :], in_=ps[0:R, 1280:1536], func=AF.Copy),
                    reads=[psk(2)], writes=[("Vf", j)])
              S.add("act", lambda e, R=R, g3=g3: e.activation(out=V3[0:R, g3, :], in_=ps[0:R, 1280:1536], func=AF.Copy),
                    reads=[psk(2)], writes=[("V", g3)])

          def A_out(t):
              j, gt = t["j"], t["gt"]
              if gt == 8:
                  S.add("sp", lambda e, j=j: [e.dma_start(out=okp[:, :], in_=Kn[:, j, :]),
                                              e.dma_start(out=ovp[:, :], in_=Vf[:, j, :])],
                        reads=[("Kn", j), ("Vf", j)], ndma=2, stream=("out", "kv8"))
              if gt == 9:
                  S.add("sp", lambda e, j=j: [e.dma_start(out=oks[s_, 124:128, :], in_=Kn[4 * s_:4 * s_ + 4, j, :]) for s_ in range(4)] +
                                             [e.dma_start(out=ovs[s_, 124:128, :], in_=Vf[4 * s_:4 * s_ + 4, j, :]) for s_ in range(4)],
                        reads=[("Kn", j), ("Vf", j)], ndma=8, stream=("out", "kv9"))

          psb3 = ps[:, 0 * 512:1 * 512].bitcast(BF16)
          psb4 = ps[:, 1 * 512:2 * 512].bitcast(BF16)

          def B(t):
              R, j, g3 = t["R"], t["j"], t["g3"]
              if t["full"]:
                  for h in range(16):
                      pb = psb3 if h < 8 else psb4
                      bk = 0 if h < 8 else 1
                      S.add("pe", lambda e, h=h, pb=pb, R=R, j=j: e.transpose(
                          out=pb[0:64, (h % 8) * 128:(h % 8) * 128 + R], in_=Qn[0:R, j, h * 64:(h + 1) * 64],
                          identity=identb[0:R, 0:R]),
                          reads=[("Qn", j), ("identb",)], writes=[psk(bk)])
                  S.add("act", lambda e, R=R, j=j: e.activation(
                      out=QTt[:, j, 0:8, 0:R], in_=psb3[0:64, :].rearrange("p (a c) -> p a c", c=128)[:, :, 0:R], func=AF.Copy),
                      reads=[psk(0)], writes=[("QT", j, 0)])
                  S.add("act", lambda e, R=R, j=j: e.activation(
                      out=QTt[:, j, 8:16, 0:R], in_=psb4[0:64, :].rearrange("p (a c) -> p a c", c=128)[:, :, 0:R], func=AF.Copy),
                      reads=[psk(1)], writes=[("QT", j, 1)])
              for kvh in range(4):
                  S.add("pe", lambda e, kvh=kvh, R=R, j=j: e.transpose(
                      out=ps[0:64, 2 * 512 + kvh * 128:2 * 512 + kvh * 128 + R], in_=Kn[0:R, j, kvh * 64:(kvh + 1) * 64],
                      identity=identf[0:R, 0:R]),
                      reads=[("Kn", j), ("tab",)], writes=[psk(2)])
              S.add("dve", lambda e, R=R, g3=g3: e.tensor_copy(
                  out=KT3[0:64, :, g3, 0:R],
                  in_=ps[0:64, 2 * 512:3 * 512].rearrange("p (a c) -> p a c", c=128)[:, :, 0:R]),
                  reads=[psk(2)], writes=[("KT", g3)])

          def C_main(t, hooks):
              lt, gt, j, g3 = t["lt"], t["gt"], t["j"], t["g3"]
              acol = t["col0"] - cb
              p3 = (gt - 1) % 3

              def S_(kvh):
                  e_i = kvh % 2
                  for kb, (k3, bk) in enumerate(((p3, 6), (g3, 7))):
                      for g in range(4):
                          S.add("pe", lambda e, kvh=kvh, g=g, k3=k3, bk=bk, j=j: e.matmul(
                              ps[:, bk * 512 + g * 128:bk * 512 + (g + 1) * 128],
                              lhsT=KT3[0:64, kvh, k3, :], rhs=QTt[:, j, 4 * kvh + g, :],
                              start=True, stop=True),
                              reads=[("KT", k3), ("QT", j, 0), ("QT", j, 1)], writes=[psk(bk)])
                      S.add("act", lambda e, bk=bk, e_i=e_i, kb=kb: e.activation(
                          out=E[:, e_i, kb, :], in_=bank(bk), func=AF.Exp), reads=[psk(bk)], writes=[("E", e_i, kb)])
                      S.add("dve" if kb == 0 else "pool", lambda e, kvh=kvh, e_i=e_i, kb=kb: e.tensor_tensor(
                          out=PT[:, e_i, kb, :], in0=E[:, e_i, kb, :], in1=maskv[:, kb, kvh * 512:(kvh + 1) * 512], op=ALU.mult),
                          reads=[("E", e_i, kb), ("R", 3)], writes=[("PT", e_i, kb)])

              def V_(kvh):
                  e_i = kvh % 2
                  ob = 3 if kvh % 2 == 0 else 4
                  for g in range(4):
                      hf, g2 = g % 2, g // 2
                      for kb, k3 in enumerate((p3, g3)):
                          S.add("pe", lambda e, kvh=kvh, g=g, hf=hf, g2=g2, kb=kb, k3=k3, ob=ob, e_i=e_i: e.matmul(
                              ps[hf * 64:hf * 64 + 64, ob * 512 + g2 * 128:ob * 512 + (g2 + 1) * 128],
                              lhsT=V3[:, k3, kvh * 64:(kvh + 1) * 64], rhs=PT[:, e_i, kb, g * 128:(g + 1) * 128],
                              start=(kb == 0), stop=(kb == 1)),
                              reads=[("V", k3), ("PT", e_i, kb)], writes=[psk(ob)])
                  for g in range(4):
                      hf, g2 = g % 2, g // 2
                      for kb in range(2):
                          lw = hvones if (gt == 1 and kb == 0) else ones
                          S.add("pe", lambda e, g=g, hf=hf, g2=g2, kb=kb, ob=ob, e_i=e_i, lw=lw: e.matmul(
                              ps[hf * 64:hf * 64 + 64, ob * 512 + 256 + g2 * 128:ob * 512 + 256 + (g2 + 1) * 128],
                              lhsT=lw[:, :], rhs=PT[:, e_i, kb, g * 128:(g + 1) * 128],
                              start=(kb == 0), stop=(kb == 1)),
                              reads=[("ones",), ("hvones",), ("PT", e_i, kb)], writes=[psk(ob)])
                  for g in range(4):
                      hf, g2 = g % 2, g // 2
                      hh = 4 * kvh + g
                      S.add("act", lambda e, hf=hf, g2=g2, hh=hh, ob=ob, e_i=e_i: e.activation(
                          out=dtmp[hf * 64:hf * 64 + 64, e_i, g2 * 128:(g2 + 1) * 128],
                          in_=ps[hf * 64:hf * 64 + 64, ob * 512 + 256 + g2 * 128:ob * 512 + 256 + (g2 + 1) * 128],
                          func=AF.Identity, bias=est[hf * 64:hf * 64 + 64, hh:hh + 1], scale=1.0),
                          reads=[psk(ob), ("est",), ("dtmp", e_i)], writes=[("dtmp", e_i)])
                  S.add("dve", lambda e, e_i=e_i: e.reciprocal(out=dtmp[:, e_i, 0:256], in_=dtmp[:, e_i, 0:256]),
                        reads=[("dtmp", e_i)], writes=[("dtmp", e_i)])
                  S.add("dve", lambda e, ob=ob, e_i=e_i, kvh=kvh, acol=acol: e.tensor_tensor(
                      out=AT[:, 2 * kvh:2 * kvh + 2, acol:acol + 128],
                      in0=ps[:, ob * 512:ob * 512 + 256].rearrange("p (g q) -> p g q", q=128),
                      in1=dtmp[:, e_i, 0:256].rearrange("p (g q) -> p g q", q=128), op=ALU.mult),
                      reads=[psk(ob), ("dtmp", e_i)], writes=[("AT", lt)])

              def H_(i):
                  for hk in hooks.get(i, []):
                      hk()

              S_(0); H_(0); S_(1); V_(0); H_(1); S_(2); V_(1); H_(2); S_(3); V_(2); H_(3); V_(3)

          def C_samp(t):
              lt, j, g3 = t["lt"], t["j"], t["g3"]
              acol = t["col0"] - cb
              for sbi in range(4):
                  bk = 3 + sbi // 2
                  S.add("pe", lambda e, sbi=sbi, bk=bk, g3=g3: e.matmul(
                      ps[0:4, bk * 512 + (sbi % 2) * 256:bk * 512 + (sbi % 2) * 256 + 256],
                      lhsT=identb[0:16, 4 * sbi:4 * sbi + 4], rhs=V3[0:16, g3, :], start=True, stop=True),
                      reads=[("identb",), ("V", g3)], writes=[psk(bk)])
              S.add("dve", lambda e: e.tensor_copy(out=Vn.rearrange("p s c -> p (s c)"), in_=ps[0:4, 3 * 512:5 * 512]),
                    reads=[psk(3), psk(4)], writes=[("Vn",)])
              for sbi in range(4):
                  for kvh in range(4):
                      c0 = sbi * 64 + kvh * 16
                      S.add("pe", lambda e, sbi=sbi, kvh=kvh, c0=c0, j=j: e.matmul(
                          ps[:, 6 * 512 + c0:6 * 512 + c0 + 16], lhsT=KTc[:, sbi, kvh, :],
                          rhs=QTt[:, j, 4 * kvh:4 * kvh + 4, 4 * sbi:4 * sbi + 4], start=True, stop=True),
                          reads=[("KTc",), ("QT", j, 0), ("QT", j, 1)], writes=[psk(6)])
                      S.add("pe", lambda e, sbi=sbi, kvh=kvh, c0=c0, j=j, g3=g3: e.matmul(
                          ps[0:4, 7 * 512 + c0:7 * 512 + c0 + 16], lhsT=KT3[0:64, kvh, g3, 4 * sbi:4 * sbi + 4],
                          rhs=QTt[:, j, 4 * kvh:4 * kvh + 4, 4 * sbi:4 * sbi + 4], start=True, stop=True),
                          reads=[("KT", g3), ("QT", j, 0), ("QT", j, 1)], writes=[psk(7)])
              S.add("act", lambda e: e.activation(out=E[:, 0, 0, 0:256], in_=ps[:, 6 * 512:6 * 512 + 256], func=AF.Exp),
                    reads=[psk(6)], writes=[("E", 0, 0)])
              S.add("act", lambda e: e.activation(out=E[0:4, 0, 1, 0:256], in_=ps[0:4, 7 * 512:7 * 512 + 256], func=AF.Exp),
                    reads=[psk(7)], writes=[("E", 0, 1)])
              S.add("dve", lambda e: e.tensor_tensor(
                  out=PT[:, 0, 0, 0:256].rearrange("p (s h q) -> p s h q", s=4, h=16),
                  in0=E[:, 0, 0, 0:256].rearrange("p (s h q) -> p s h q", s=4, h=16),
                  in1=maskv[:, 0, :].rearrange("p (h q) -> p h q", q=128)[:, :, 0:4].unsqueeze(1).broadcast_to([128, 4, 16, 4]),
                  op=ALU.mult), reads=[("E", 0, 0), ("R", 3)], writes=[("PT", 0, 0)])
              S.add("dve", lambda e: e.tensor_tensor(
                  out=PT[0:4, 0, 1, 0:256].rearrange("p (s h q) -> p s h q", s=4, h=16),
                  in0=E[0:4, 0, 1, 0:256].rearrange("p (s h q) -> p s h q", s=4, h=16),
                  in1=maskv[0:4, 1, :].rearrange("p (h q) -> p h q", q=128)[:, :, 0:4].unsqueeze(1).broadcast_to([4, 4, 16, 4]),
                  op=ALU.mult), reads=[("E", 0, 1), ("R", 3)], writes=[("PT", 0, 1)])
              for sbi in range(4):
                  for kvh in range(4):
                      for g in range(4):
                          h = 4 * kvh + g
                          hf = h % 2
                          pc = sbi * 64 + kvh * 16 + g * 4
                          oc = (h // 2) * 16 + sbi * 4
                          S.add("pe", lambda e, sbi=sbi, kvh=kvh, pc=pc, oc=oc, hf=hf: e.matmul(
                              ps[hf * 64:hf * 64 + 64, oc:oc + 4], lhsT=Vc[:, sbi, kvh * 64:(kvh + 1) * 64],
                              rhs=PT[:, 0, 0, pc:pc + 4], start=True, stop=False),
                              reads=[("Vc",), ("PT", 0, 0)], writes=[psk(0)])
                          S.add("pe", lambda e, sbi=sbi, kvh=kvh, pc=pc, oc=oc, hf=hf: e.matmul(
                              ps[hf * 64:hf * 64 + 64, oc:oc + 4], lhsT=Vn[:, sbi, kvh * 64:(kvh + 1) * 64],
                              rhs=PT[0:4, 0, 1, pc:pc + 4], start=False, stop=True),
                              reads=[("Vn",), ("PT", 0, 1)], writes=[psk(0)])
                          S.add("pe", lambda e, pc=pc, oc=oc, hf=hf: e.matmul(
                              ps[hf * 64:hf * 64 + 64, 512 + oc:512 + oc + 4], lhsT=ones[:, :],
                              rhs=PT[:, 0, 0, pc:pc + 4], start=True, stop=False),
                              reads=[("ones",), ("PT", 0, 0)], writes=[psk(1)])
                          S.add("pe", lambda e, pc=pc, oc=oc, hf=hf: e.matmul(
                              ps[hf * 64:hf * 64 + 64, 512 + oc:512 + oc + 4], lhsT=ones[0:4, :],
                              rhs=PT[0:4, 0, 1, pc:pc + 4], start=False, stop=True),
                              reads=[("ones",), ("PT", 0, 1)], writes=[psk(1)])
              for h in range(16):
                  hf, c = h % 2, h // 2
                  S.add("dve", lambda e, h=h, hf=hf, c=c: e.tensor_scalar(
                      out=dtmp[hf * 64:hf * 64 + 64, 0, c * 16:(c + 1) * 16],
                      in0=ps[hf * 64:hf * 64 + 64, 512 + c * 16:512 + (c + 1) * 16],
                      scalar1=est[hf * 64:hf * 64 + 64, h:h + 1], scalar2=None, op0=ALU.add),
                      reads=[psk(1), ("est",), ("dtmp", 0)], writes=[("dtmp", 0)])
              S.add("dve", lambda e: e.reciprocal(out=dtmp[:, 0, 0:128], in_=dtmp[:, 0, 0:128]),
                    reads=[("dtmp", 0)], writes=[("dtmp", 0)])
              S.add("dve", lambda e, acol=acol: e.tensor_tensor(
                  out=AT[:, :, acol:acol + 16],
                  in0=ps[:, 0:128].rearrange("p (c q) -> p c q", q=16),
                  in1=dtmp[:, 0, 0:128].rearrange("p (c q) -> p c q", q=16), op=ALU.mult),
                  reads=[psk(0), ("dtmp", 0)], writes=[("AT", lt)])

          def conv_w(cu):
              ss = cu % 2
              return Rflat[:, ss * 12288:(ss + 1) * 12288].rearrange("p (a k c) -> p a k c", a=3, k=16)

          def conv_keys(cu):
              return [("R", 0), ("R1a",)] if cu % 2 == 0 else [("R1b",), ("R", 2)]

          def ld_conv(cu):
              ss = cu % 2

              def f(e, cu=cu, ss=ss):
                  dst = Rflat[:, ss * 12288:(ss + 1) * 12288].rearrange("p (a n) -> p a n", n=2048)
                  src = w_convt[cu].rearrange("p (a n) -> p a n", n=2048)
                  return [e.dma_start(out=dst[:, 0:3, :], in_=src[:, 0:3, :]), e.dma_start(out=dst[:, 3:6, :], in_=src[:, 3:6, :])]
              S.add("pool", f, writes=conv_keys(cu), ndma=2, stream=("RS", ss), nobar=True)

          t0_ = tiles[0]
          for n in ([0, 1, 2] if t0_["full"] else [2]):
              A_mm(t0_, n)
          ld_masks()
          A_k(t0_); A_v(t0_)
          if t0_["full"]:
              A_q1(t0_); A_q2(t0_)
          A_out(t0_)
          B(t0_)
          for idx, t in enumerate(tiles):
              nxt = tiles[idx + 1] if idx + 1 < len(tiles) else None
              hooks = {}
              if nxt is not None:
                  last_mm = (idx + 2 == len(tiles))
                  hooks[0] = [lambda nxt=nxt: A_mm(nxt, 2), lambda nxt=nxt: A_mm(nxt, 0), lambda nxt=nxt: A_k(nxt),
                              lambda nxt=nxt: A_v(nxt)]
                  hooks[1] = [lambda nxt=nxt: A_mm(nxt, 1), lambda nxt=nxt: A_q1(nxt)]
                  if last_mm:
                      hooks[1].append(lambda: ld_conv(0))
                      hooks[1].append(lambda: ld_conv(1))
                  hooks[2] = [lambda nxt=nxt: A_q2(nxt), lambda nxt=nxt: A_out(nxt)]
                  hooks[3] = [lambda nxt=nxt: B(nxt)]
              if t["kind"] == "main":
                  C_main(t, hooks)
              elif t["kind"] == "samp":
                  C_samp(t)
              else:
                  for kv_ in range(4):
                      for hk in hooks.get(kv_, []):
                          hk()

          stop_at("attn")
          S.barrier()
          if P == 1:
              S.add("sp", lambda e: [e.dma_start(out=sct, in_=sconv[:, :])], writes=[("sct",)], ndma=1, stream="cache")
              for c in range(8):
                  S.add("pe", lambda e, c=c: e.transpose(
                      out=ps[:, 6 * 512 + c * 8:6 * 512 + c * 8 + 8], in_=sct[:, c * 128:(c + 1) * 128],
                      identity=identf[0:8, 0:8]),
                      reads=[("sct",), ("tab",)], writes=[psk(6)])
              S.add("dve", lambda e: e.tensor_copy(out=stT[:].rearrange("p a b -> p (a b)"), in_=ps[:, 6 * 512:6 * 512 + 64]),
                    reads=[psk(6)], writes=[("stT",)])

          nmain = 512
          clo = cb - 2 if P == 0 else 0
          cslices = split2(clo, ncols)
          for c in range(8):
              cu, cl = c // 2, c % 2
              ss = cu % 2
              Wc = conv_w(cu)
              rW = conv_keys(cu)
              i = c % 2
              if P == 1:
                  S.add("dve", lambda e, i=i, c=c: e.tensor_copy(out=ub[:, i, 0:2], in_=ucarry[:, c, :]),
                        reads=[("ucarry", c)], writes=[("ubh", i)])
              for si, (a, b) in enumerate(cslices):
                  n = b - a
                  pa = a - (cb - 2)
                  par = (c * 2 + si) % 2
                  bx, bc_, bb = (0, 1, 2) if par == 0 else (3, 4, 5)
                  tl = tiles_overlapping(tiles, a, b)
                  for part, bk in ((0, bx), (2, bc_), (1, bb)):
                      for k in range(16):
                          S.add("pe", lambda e, part=part, bk=bk, k=k, a=a, b=b, n=n, Wc=Wc, cl=cl: e.matmul(
                              bank(bk, n=n), lhsT=Wc[:, part, k, cl * 128:(cl + 1) * 128], rhs=hT[:, k, a:b],
                              start=(k == 0), stop=(k == 15)),
                              reads=rW + [("hT", t_["lt"], k) for t_ in tl],
                              writes=[psk(bk)] + ([("c3s",)] if (c == 6 and si == 0 and part == 0 and k == 0) else []),
                              nobar=True)
                  xi = par
                  S.add("act", lambda e, bx=bx, n=n, xi=xi: e.activation(out=xcs[:, xi, 0:n], in_=bank(bx, n=n), func=AF.Copy),
                        reads=[psk(bx)], writes=[("xcs", xi)])
                  S.add("dve", lambda e, bc_=bc_, n=n, xi=xi, i=i, pa=pa: e.tensor_tensor(
                      out=ub[:, i, pa:pa + n], in0=xcs[:, xi, 0:n], in1=bank(bc_, n=n), op=ALU.mult),
                      reads=[("xcs", xi), psk(bc_)], writes=[("ub", i, si)])
                  S.add("act", lambda e, bb=bb, n=n, i=i, pa=pa: e.activation(out=bgs[:, i, pa:pa + n], in_=bank(bb, n=n), func=AF.Copy),
                        reads=[psk(bb)], writes=[("bgs", i, si)])
              if cl == 1 and cu + 2 < 4:
                  ld_conv(cu + 2)
              if c == 7:
                  ld_gate(0); ld_gate(1, extra=[("c3s",)])
              ubk = [("ub", i, 0), ("ub", i, 1), ("ubh", i)]
              S.add("dve", lambda e, i=i, c=c: e.tensor_scalar(
                  out=yt[:, i, :], in0=ub[:, i, 2:2 + nmain], scalar1=cw[:, c * 3 + 2:c * 3 + 3], scalar2=None, op0=ALU.mult),
                  reads=ubk + [("tab",)], writes=[("yt", i)])
              for jj in (1, 0):
                  S.add("dve", lambda e, i=i, c=c, jj=jj: e.scalar_tensor_tensor(
                      out=yt[:, i, :], in0=ub[:, i, jj:jj + nmain], scalar=cw[:, c * 3 + jj:c * 3 + jj + 1], in1=yt[:, i, :],
                      op0=ALU.mult, op1=ALU.add),
                      reads=ubk + [("tab",), ("yt", i)], writes=[("yt", i)])
              S.add("pool", lambda e, i=i, c=c: e.tensor_tensor(
                  out=YB[:, c, 0:nmain], in0=yt[:, i, :], in1=bgs[:, i, 2:2 + nmain], op=ALU.mult),
                  reads=[("yt", i), ("bgs", i, 0), ("bgs", i, 1)], writes=[("YB", c)])
              if P == 0:
                  S.add("dve", lambda e, i=i, c=c: e.tensor_copy(out=ucarry[:, c, :], in_=ub[:, i, nmain:nmain + 2]),
                        reads=ubk, writes=[("ucarry", c)])
              else:
                  S.add("dve", lambda e, c=c: e.tensor_copy(out=us[:, :, 0:2], in_=stT[:, c, :].rearrange("p (s j) -> p s j", j=2)),
                        reads=[("stT",), ("us",)], writes=[("us",)])
                  S.add("dve", lambda e, i=i: e.tensor_copy(out=us[:, :, 2:6], in_=ub[:, i, 514:530].rearrange("p (s t) -> p s t", t=4)),
                        reads=ubk + [("us",)], writes=[("us",)])
                  S.add("dve", lambda e, c=c: e.tensor_scalar(
                      out=ysm[:, :, :], in0=us[:, :, 2:6], scalar1=cw[:, c * 3 + 2:c * 3 + 3], scalar2=None, op0=ALU.mult),
                      reads=[("us",), ("tab",), ("ysm",)], writes=[("ysm",)])
                  for jj in (1, 0):
                      S.add("dve", lambda e, c=c, jj=jj: e.scalar_tensor_tensor(
                          out=ysm[:, :, :], in0=us[:, :, jj:jj + 4], scalar=cw[:, c * 3 + jj:c * 3 + jj + 1], in1=ysm[:, :, :],
                          op0=ALU.mult, op1=ALU.add),
                          reads=[("us",), ("tab",), ("ysm",)], writes=[("ysm",)])
                  S.add("dve", lambda e, i=i, c=c: e.tensor_tensor(
                      out=YB[:, c, 512:528].rearrange("p (s t) -> p s t", t=4), in0=ysm[:, :, :],
                      in1=bgs[:, i, 514:530].rearrange("p (s t) -> p s t", t=4), op=ALU.mult),
                      reads=[("ysm",), ("bgs", i, 0), ("bgs", i, 1)], writes=[("YBs", c)])
                  S.add("dve", lambda e, c=c: e.tensor_copy(
                      out=ucat[:, c, 0:8].rearrange("p (s j) -> p s j", j=2), in_=us[:, :, 4:6]),
                      reads=[("us",)], writes=[("ucat", c)])
                  S.add("dve", lambda e, i=i, c=c: e.tensor_copy(out=ucat[:, c, 8:10], in_=ub[:, i, 512:514]),
                        reads=ubk, writes=[("ucat2", c)])
          if P == 1:
              for c in range(8):
                  bk = 6 + c // 4
                  S.add("pe", lambda e, c=c, bk=bk: e.transpose(
                      out=ps[0:10, bk * 512 + (c % 4) * 128:bk * 512 + (c % 4 + 1) * 128], in_=ucat[:, c, :], identity=identf[:, :]),
                      reads=[("ucat", c), ("ucat2", c), ("tab",)], writes=[psk(bk)])
              S.add("dve", lambda e: e.tensor_copy(out=uout, in_=ps[0:10, 6 * 512:8 * 512]),
                    reads=[psk(6), psk(7)], writes=[("uout",), ("yt", 0), ("yt", 1)])
              S.add("sp", lambda e: [e.dma_start(out=ocs[:, :], in_=uout[0:8, :]), e.dma_start(out=ocp[:, :], in_=uout[8:10, :])],
                    reads=[("uout",)], ndma=2, stream=("out", "conv"))

          stop_at("conv")
          fslices = split2(cb, ncols)

          eslot = lambda gi: 2 + (gi + 1) % 2
          load_wdn(w_o_v, 0, 4, eslot(0)); load_wdn(w_o_v, 4, 4, eslot(1))
          n2b_, n2a_ = norm_hooks(full, 2)
          for u in range(8):
              ws = u % 2
              Wg = ring[:, ws, :].rearrange("p (a k c) -> p a k c", a=2, k=16)
              co = wca[:, ws, 0:2048].rearrange("p (c n) -> p c n", n=256)
              ao = wca[:, ws, 2048:4096].rearrange("p (c n) -> p c n", n=256)
              for jl in range(2):
                  jx = 2 * u + jl
                  jg = jx % 4
                  for (a, b) in fslices:
                      n = b - a
                      aa = a - cb
                      tl = tiles_overlapping(full, a, b)
                      for c in range(8):
                          S.add("pe", lambda e, c=c, jl=jl, aa=aa, n=n, co=co: e.matmul(
                              bank(0, n=n), lhsT=co[:, c, jl * 128:(jl + 1) * 128], rhs=YB[:, c, aa:aa + n],
                              start=(c == 0), stop=(c == 7)),
                              reads=[("wca", ws), ("YB", c), ("YBs", c)], writes=[psk(0)])
                      for part, bk in ((0, 1), (1, 3)):
                          for k in range(16):
                              S.add("pe", lambda e, part=part, bk=bk, k=k, jl=jl, a=a, b=b, n=n, Wg=Wg: e.matmul(
                                  bank(bk, n=n), lhsT=Wg[:, part, k, jl * 128:(jl + 1) * 128], rhs=hT[:, k, a:b],
                                  start=(k == 0), stop=(k == 15)),
                                  reads=[("R", ws)] + [("hT", t["lt"], k) for t in tl], writes=[psk(bk)])
                      for c in range(8):
                          S.add("pe", lambda e, c=c, jl=jl, aa=aa, n=n, ao=ao: e.matmul(
                              bank(2, n=n), lhsT=ao[:, c, jl * 128:(jl + 1) * 128], rhs=AT[:, c, aa:aa + n],
                              start=(c == 0), stop=(c == 7)),
                              reads=[("wca", ws)] + [("AT", t["lt"]) for t in tl], writes=[psk(2)])
                      s0 = ctr["sg"] % 4
                      s1 = (ctr["sg"] + 1) % 4
                      ctr["sg"] += 2
                      S.add("act", lambda e, n=n, s0=s0: e.activation(out=sg[:, s0, 0:n], in_=bank(1, n=n), func=AF.Sigmoid),
                            reads=[psk(1)], writes=[("sg", s0)])
                      S.add("dve", lambda e, n=n, s0=s0: e.tensor_tensor(out=m12[:, 0, 0:n], in0=sg[:, s0, 0:n], in1=bank(0, n=n), op=ALU.mult),
                            reads=[("sg", s0), psk(0), ("m12", 0)], writes=[("m12", 0)])
                      S.add("act", lambda e, n=n, s1=s1: e.activation(out=sg[:, s1, 0:n], in_=bank(3, n=n), func=AF.Sigmoid),
                            reads=[psk(3)], writes=[("sg", s1)])
                      S.add("dve", lambda e, n=n, s1=s1: e.tensor_tensor(out=m12[:, 1, 0:n], in0=sg[:, s1, 0:n], in1=bank(2, n=n), op=ALU.mult),
                            reads=[("sg", s1), psk(2), ("m12", 1)], writes=[("m12", 1)])
                      S.add("pool", lambda e, n=n, jg=jg, a=a, b=b: e.tensor_tensor(
                          out=actT[:, jg, a:b], in0=m12[:, 0, 0:n], in1=m12[:, 1, 0:n], op=ALU.add),
                          reads=[("m12", 0), ("m12", 1)], writes=[("actT", jg, t["lt"]) for t in tl])
              if u + 2 < 8:
                  ld_gate(u + 2)
              if u == 7:
                  load_wpair(w_upt[1], 0, 0)
              if u % 2 == 1:
                  gi = u // 2
                  if gi == 3:
                      n2b_()
                  down_accum(full, 4, eslot(gi), 1.0, after=(n2a_ if gi == 3 else None))
                  if gi + 2 < 4:
                      load_wdn(w_o_v, 4 * (gi + 2), 4, eslot(gi))

          stop_at("stageE")
          st_ = (P == 0)
          emit_ffn(1, full, cb, ncols, stage_last=st_, first_loaded=True)
          for t in full:
              lt, gt, R = t["lt"], t["gt"], t["R"]
              dst = yp[(gt - 1) * 128:gt * 128, :] if t["kind"] == "main" else ys[:, :]
              srcb = xstage if st_ else x_sb
              rk = [("xs", lt, 0), ("xs", lt, 1)] if st_ else xk(lt)
              S.add("sp", lambda e, lt=lt, R=R, dst=dst, srcb=srcb: [e.dma_start(out=dst, in_=srcb[0:R, lt, :])],
                    reads=rk, ndma=1, stream=("out", P, lt))

    except _Stop:
        for lt in range(1, 5):
            S.add("sp", lambda e, lt=lt: [e.dma_start(out=yp[(lt - 1) * 128:lt * 128, :], in_=x_sb[:, lt, :])],
                  reads=xk(lt), ndma=1, stream=("out", "dbg", lt))
    S.emit(nc, final_streams=("out", "('out'"))
    es.close()
    return nc


_CACHE = {}


def _consts():
    h = np.arange(1, 17, dtype=np.float64)
    slopes = np.exp2(-8.0 * h / 16.0)
    ik = np.arange(128)[:, None, None]
    iq = np.arange(128)[None, None, :]
    sl = slopes[None, :, None]
    dP = 128 + iq - ik
    dC = iq - ik
    mP = np.where(ik >= iq, np.exp(-sl * dP), 0.0).astype(np.float32).reshape(128, 2048)
    mC = np.where(ik <= iq, np.exp(-sl * dC), 0.0).astype(np.float32).reshape(128, 2048)
    return mP, mC


def kernel(x_prompt, x_sample, state_conv, cache_k_win, cache_v_win, meta_tokens,
           ffn1_norm, ffn1_w_up, ffn1_w_down, mix_norm, w_in, q_norm, k_norm,
           conv_w, w_conv_out, attn_sinks, w_attn_out, w_o,
           ffn2_norm, ffn2_w_up, ffn2_w_down):
    if "nc" not in _CACHE:
        _CACHE["nc"] = build_program()
    nc = _CACHE["nc"]
    in_maps = make_in_maps(x_prompt, x_sample, state_conv, cache_k_win, cache_v_win, meta_tokens,
                           ffn1_norm, ffn1_w_up, ffn1_w_down, mix_norm, w_in, q_norm, k_norm,
                           conv_w, w_conv_out, attn_sinks, w_attn_out, w_o,
                           ffn2_norm, ffn2_w_up, ffn2_w_down)
    res = run_bass_kernel_spmd(nc, in_maps, core_ids=list(range(NCORES)))
    return assemble(res.results)


def make_in_maps(x_prompt, x_sample, state_conv, cache_k_win, cache_v_win, meta_tokens,
                 ffn1_norm, ffn1_w_up, ffn1_w_down, mix_norm, w_in, q_norm, k_norm,
                 conv_w, w_conv_out, attn_sinks, w_attn_out, w_o,
                 ffn2_norm, ffn2_w_up, ffn2_w_down):
    f = lambda a: np.ascontiguousarray(np.asarray(a, dtype=np.float32))
    x_prompt, x_sample, state_conv = f(x_prompt), f(x_sample), f(state_conv)
    cache_k_win, cache_v_win, meta_tokens = f(cache_k_win), f(cache_v_win), f(meta_tokens)
    mP, mC = _consts()
    gt = lambda g: f(g).reshape(16, 128).T
    gtab = np.ascontiguousarray(np.concatenate([gt(ffn1_norm[0]), gt(mix_norm[0]), gt(ffn2_norm[0])], axis=1))
    cw = np.ascontiguousarray(f(conv_w)[0].reshape(3, 8, 128).transpose(2, 1, 0).reshape(128, 24))
    wi = f(w_in)[0]

    def tile_up(w):
        w4 = f(w)[0].reshape(16, 128, 2, DFF)
        main = np.ascontiguousarray(w4[:, :, :, :21 * 256].reshape(16, 128, 2, 21, 256).transpose(3, 1, 2, 0, 4).reshape(21, 128, 8192))
        last = np.ascontiguousarray(w4[:, :, :, 21 * 256:].transpose(1, 2, 0, 3).reshape(128, 4096))
        return main, last

    w_convt = np.ascontiguousarray(wi[:, 0:3072].reshape(16, 128, 3, 4, 256).transpose(3, 1, 2, 0, 4).reshape(4, 128, 12288))
    w_gatet = np.ascontiguousarray(wi[:, 4608:8704].reshape(16, 128, 2, 8, 256).transpose(3, 1, 2, 0, 4).reshape(8, 128, 8192))
    tile_c = lambda w: np.ascontiguousarray(f(w)[0].reshape(8, 128, 8, 256).transpose(2, 1, 0, 3).reshape(8, 128, 2048))
    up1, up1L = tile_up(ffn1_w_up)
    up2, up2L = tile_up(ffn2_w_up)
    shared = {
        "w_up1t": up1, "w_up1L": up1L, "w_dn1": f(ffn1_w_down)[0], "w_up2t": up2, "w_up2L": up2L, "w_dn2": f(ffn2_w_down)[0],
        "w_qkv": np.ascontiguousarray(wi[:, 3072:4608]), "w_convt": w_convt, "w_gatet": w_gatet,
        "w_cot": tile_c(w_conv_out), "w_aot": tile_c(w_attn_out), "w_o": f(w_o)[0],
        "gtab": gtab, "gvec": np.ascontiguousarray(np.stack([f(ffn1_norm)[0], f(mix_norm)[0], f(ffn2_norm)[0]])), "qg": f(q_norm).reshape(1, 64), "kg": f(k_norm).reshape(1, 64), "cw": cw,
        "sinks": f(attn_sinks).reshape(1, 16), "maskP": mP, "maskC": mC, "ident": np.eye(128, dtype=np.float32),
    }
    in_maps = []
    for c in range(NCORES):
        b, half = c // 2, c % 2
        if half == 0:
            xh = np.zeros((128, D), np.float32)
            xh[112:] = meta_tokens
            hv = np.zeros((128, 64), np.float32)
            hv[112:] = 1.0
        else:
            xh = x_prompt[b, 896:1024]
            hv = np.ones((128, 64), np.float32)
        m = dict(shared)
        m.update({
            "xh": np.ascontiguousarray(xh), "hv": hv,
            "xm": np.ascontiguousarray(x_prompt[b, half * 1024:(half + 1) * 1024]),
            "xs": np.ascontiguousarray(x_sample[4 * c:4 * c + 4].reshape(16, D)),
            "sconv": np.ascontiguousarray(state_conv[0, 4 * c:4 * c + 4].reshape(8, 1024)),
            "ck": np.ascontiguousarray(cache_k_win[0, 4 * c:4 * c + 4].reshape(4, 128, 256)),
            "cv": np.ascontiguousarray(cache_v_win[0, 4 * c:4 * c + 4].reshape(4, 128, 256)),
        })
        in_maps.append(m)
    return in_maps


def assemble(R):
    y_prompt = np.zeros((4, 2048, D), np.float32)
    y_sample = np.zeros((32, 4, D), np.float32)
    ncp = np.zeros((1, 4, 2, 1024), np.float32)
    nkp = np.zeros((1, 4, 128, 4, 64), np.float32)
    nvp = np.zeros((1, 4, 128, 4, 64), np.float32)
    ncs = np.zeros((1, 32, 2, 1024), np.float32)
    nks = np.zeros((1, 32, 128, 4, 64), np.float32)
    nvs = np.zeros((1, 32, 128, 4, 64), np.float32)
    for c in range(NCORES):
        b, half = c // 2, c % 2
        r = R[c]
        y_prompt[b, half * 1024:(half + 1) * 1024] = r["yp"]
        y_sample[4 * c:4 * c + 4] = r["ys"].reshape(4, 4, D)
        ncs[0, 4 * c:4 * c + 4] = r["ocs"].reshape(4, 2, 1024)
        nks[0, 4 * c:4 * c + 4] = r["oks"].reshape(4, 128, 4, 64)
        nvs[0, 4 * c:4 * c + 4] = r["ovs"].reshape(4, 128, 4, 64)
        if half == 1:
            ncp[0, b] = r["ocp"]
            nkp[0, b] = r["okp"].reshape(128, 4, 64)
            nvp[0, b] = r["ovp"].reshape(128, 4, 64)
    return (y_prompt, y_sample, ncp, nkp, nvp, ncs, nks, nvs)
```

```python
import numpy as np
import concourse.bass as bass
import concourse.mybir as mybir
from concourse.bass_utils import run_bass_kernel_spmd

F32 = mybir.dt.float32
BF16 = mybir.dt.bfloat16
AF = mybir.ActivationFunctionType
ALU = mybir.AluOpType
AX = mybir.AxisListType

D = 2048
DFF = 5504
NCH = 43
DIN = 8704
EPS = 1e-6
NCORES = 8


class Op:
    __slots__ = ("eng", "fn", "reads", "writes", "ndma", "stream", "deps", "mile", "cum")

    def __init__(self, eng, fn, reads, writes, ndma, stream):
        self.eng, self.fn, self.reads, self.writes = eng, fn, reads, writes
        self.ndma, self.stream = ndma, stream
        self.deps = set()
        self.mile = 0
        self.cum = 0


class Sched:
    ENGS = ("pe", "act", "dve", "pool", "sp")

    def __init__(self):
        self.ops = []
        self.lw = {}
        self.rd = {}
        self.after = None
        self.last = {}
        self.dmas_since = []

    def barrier(self):
        deps = set(self.last.values()) | set(self.dmas_since)
        i = self.add("dve", self._bfn, nobar=True)
        self.ops[i].deps |= deps
        self.ops[i].deps.discard(i)
        self.after = i
        self.dmas_since = []

    def add(self, eng, fn, reads=(), writes=(), ndma=0, stream=None, nobar=False):
        reads = list(reads) + ([("R1a",), ("R1b",)] if ("R", 1) in reads else [])
        writes = list(writes) + ([("R1a",), ("R1b",)] if ("R", 1) in writes else [])
        op = Op(eng, fn, tuple(reads), tuple(writes), ndma, stream)
        i = len(self.ops)
        if self.after is not None and not nobar:
            op.deps.add(self.after)
        if ndma:
            self.dmas_since.append(i)
        else:
            self.last[eng] = i
        for k in op.reads:
            w = self.lw.get(k)
            if w is not None:
                op.deps.add(w)
            if k[0] == "ps":
                for e2, r in self.rd.get(k, {}).items():
                    if e2 != eng:
                        op.deps.add(r)
        for k in op.writes:
            w = self.lw.get(k)
            if w is not None:
                op.deps.add(w)
            for r in self.rd.get(k, {}).values():
                op.deps.add(r)
        for k in op.reads:
            self.rd.setdefault(k, {})[("dma", i) if ndma else eng] = i
        for k in op.writes:
            self.lw[k] = i
            self.rd[k] = {}
        op.deps.discard(i)
        self.ops.append(op)
        return i

    def emit(self, nc, final_streams):
        ops = self.ops
        need = [False] * len(ops)
        for i, op in enumerate(ops):
            for j in op.deps:
                pj = ops[j]
                if pj.ndma:
                    continue
                if pj.eng == "pe" and op.eng == "pe" and not op.ndma:
                    continue
                need[j] = True
        cnt = {e: 0 for e in self.ENGS}
        scum = {}
        for i, op in enumerate(ops):
            if op.ndma:
                scum[op.stream] = scum.get(op.stream, 0) + 16 * op.ndma
                op.cum = scum[op.stream]
            elif need[i]:
                cnt[op.eng] += 1
                op.mile = cnt[op.eng]
        streams = sorted(scum.keys(), key=str)
        import contextlib
        with contextlib.ExitStack() as es:
            esem = {e: es.enter_context(nc.semaphore("s_" + e)) for e in self.ENGS}
            ssem = {s: es.enter_context(nc.semaphore("d_" + str(n))) for n, s in enumerate(streams)}
            block = es.enter_context(nc.Block())
            per_eng = {e: [] for e in self.ENGS}
            for i, op in enumerate(ops):
                per_eng[op.eng].append(i)

            def run(e_name, eh):
                waited = {}
                for i in per_eng[e_name]:
                    op = ops[i]
                    reqs = {}
                    for j in op.deps:
                        pj = ops[j]
                        if pj.ndma:
                            key = ("s", pj.stream)
                            val = pj.cum
                        else:
                            if pj.eng == "pe" and e_name == "pe" and not op.ndma:
                                continue
                            key = ("e", pj.eng)
                            val = pj.mile
                        if val > reqs.get(key, 0):
                            reqs[key] = val
                    for key, val in reqs.items():
                        if waited.get(key, 0) >= val:
                            continue
                        waited[key] = val
                        sem = ssem[key[1]] if key[0] == "s" else esem[key[1]]
                        eh.wait_ge(sem, val)
                    r = op.fn(eh)
                    if op.ndma:
                        for ins in r:
                            ins.then_inc(ssem[op.stream], 16)
                    elif need[i]:
                        r.then_inc(esem[e_name], 1)
                if e_name == "sp":
                    for s in streams:
                        if str(s).startswith(final_streams):
                            eh.wait_ge(ssem[s], scum[s])

            @block.tensor
            def _(eh):
                run("pe", eh)

            @block.scalar
            def _(eh):
                run("act", eh)

            @block.vector
            def _(eh):
                run("dve", eh)

            @block.gpsimd
            def _(eh):
                run("pool", eh)

            @block.sync
            def _(eh):
                run("sp", eh)


KSTOP = None


class _Stop(Exception):
    pass


def build_program():
    nc = bass.Bass("TRN2", target_bir_lowering=False)
    S = Sched()

    def stop_at(name):
        if KSTOP == name:
            raise _Stop()

    def din(name, shape, dt=F32):
        return nc.dram_tensor(name, list(shape), dt, kind="ExternalInput").ap()

    def dout(name, shape, dt=F32):
        return nc.dram_tensor(name, list(shape), dt, kind="ExternalOutput").ap()

    xh = din("xh", [128, D]); xm = din("xm", [1024, D]); xs = din("xs", [16, D])
    sconv = din("sconv", [8, 1024]); ck = din("ck", [4, 128, 256]); cv = din("cv", [4, 128, 256])
    w_upt = [din("w_up1t", [21, 128, 8192]), din("w_up2t", [21, 128, 8192])]
    w_upL = [din("w_up1L", [128, 4096]), din("w_up2L", [128, 4096])]
    w_dn = [din("w_dn1", [DFF, D]), din("w_dn2", [DFF, D])]
    w_qkv = din("w_qkv", [D, 1536]); w_convt = din("w_convt", [4, 128, 12288]); w_gatet = din("w_gatet", [8, 128, 8192])
    w_cot = din("w_cot", [8, 128, 2048]); w_aot = din("w_aot", [8, 128, 2048])
    w_o = din("w_o", [D, D])
    gtab_d = din("gtab", [128, 48]); qg_d = din("qg", [1, 64]); kg_d = din("kg", [1, 64])
    cw_d = din("cw", [128, 24]); sinks_d = din("sinks", [1, 16])
    maskP_d = din("maskP", [128, 2048]); maskC_d = din("maskC", [128, 2048])
    hv_d = din("hv", [128, 64]); ident_d = din("ident", [128, 128])

    yp = dout("yp", [1024, D]); ys = dout("ys", [16, D])
    ocp = dout("ocp", [2, 1024]); okp = dout("okp", [128, 256]); ovp = dout("ovp", [128, 256])
    ocs = dout("ocs", [8, 1024]); oks = dout("oks", [4, 128, 256]); ovs = dout("ovs", [4, 128, 256])

    import contextlib
    es = contextlib.ExitStack()
    es.enter_context(nc.allow_low_precision("bf16 matmul operands, fp32 accumulation"))

    def sb(name, shape, dt):
        return es.enter_context(nc.sbuf_tensor(name, list(shape), dt))

    x_sb = sb("x_sb", [128, 5, D], F32)
    hT = sb("hT", [128, 16, 640], BF16)
    actT = sb("actT", [128, 4, 640], BF16)
    ring = sb("ring", [128, 4, 8192], BF16)
    sg = sb("sg", [128, 4, 320], F32)
    stat = sb("stat", [128, 768], F32)
    gtab = sb("gtab_s", [128, 48], F32)
    qg_bc = sb("qg_bc", [128, 64], F32); kg_bc = sb("kg_bc", [128, 64], F32)
    cw = sb("cw_s", [128, 24], F32)
    est = sb("est", [128, 16], F32)
    hv_f = sb("hv_f", [128, 64], F32); hvones = sb("hvones", [128, 64], BF16); ones = sb("ones", [128, 64], BF16)
    identf = sb("identf", [128, 128], F32); identb = sb("identb", [128, 128], BF16)
    bscr = sb("bscr", [128, 2], F32)
    KT3 = sb("KT3", [64, 4, 3, 128], BF16)
    V3 = sb("V3", [128, 3, 256], BF16)
    AT = sb("AT", [128, 8, 528], BF16)
    YB = sb("YB", [128, 8, 528], BF16)
    Kn = sb("Kn", [128, 2, 256], F32); Vf = sb("Vf", [128, 2, 256], F32)
    stT = sb("stT", [128, 8, 8], F32)
    us = sb("us", [128, 4, 6], F32); ysm = sb("ysm", [128, 4, 4], F32)
    ucarry = sb("ucarry", [128, 8, 2], F32)
    ucat = sb("ucat", [128, 8, 10], F32)
    T = sb("T", [128, 10496], F32)
    ps = es.enter_context(nc.psum_tensor("ps", [128, 4096], F32))
    S._bfn = lambda e: e.memset(bscr[:], 0.0)

    maskv = ring[:, 3, :].bitcast(F32).rearrange("p (m n) -> p m n", n=2048)
    E = T[:, 0:2048].rearrange("p (a b c) -> p a b c", a=2, b=2)
    sqb = T[:, 2048:3072]
    dtmp = T[:, 3072:4096].rearrange("p (a c) -> p a c", a=2)
    ckf = T[:, 0:1024].rearrange("p (s c) -> p s c", s=4)
    kt1 = T[:, 4096:4352]
    PT = T[:, 4352:5376].bitcast(BF16).rearrange("p (a b c) -> p a b c", a=2, b=2)
    Qn = T[:, 5376:6400].bitcast(BF16).rearrange("p (a c) -> p a c", a=2)
    QTt = T[0:64, 6400:8448].bitcast(BF16).rearrange("p (a h c) -> p a h c", a=2, h=16)
    KTc = T[0:64, 8448:9472].bitcast(BF16).rearrange("p (s k c) -> p s k c", s=4, k=4)
    Vc = T[:, 9472:9984].bitcast(BF16).rearrange("p (s c) -> p s c", s=4)
    Vn = T[0:4, 9984:10496].bitcast(BF16).rearrange("p (s c) -> p s c", s=4)
    sct = T[0:8, 0:1024]
    ub = T[:, 1024:2112].rearrange("p (i n) -> p i n", i=2)
    bgs = T[:, 2112:3200].rearrange("p (i n) -> p i n", i=2)
    xcs = T[:, 3200:3840].rearrange("p (i n) -> p i n", i=2)
    yt = T[:, 3840:4864].rearrange("p (i n) -> p i n", i=2)
    uout = T[0:10, 3840:4864]
    wca = T[:, 4864:8960].bitcast(BF16).rearrange("p (s n) -> p s n", s=2)
    m12 = T[:, 8960:9600].rearrange("p (i n) -> p i n", i=2)

    def slot_ap(ws):
        if ws < 4:
            return ring[:, ws, :]
        return T[:, 0:4096].bitcast(BF16)

    def bank(b, rows=128, n=512, r0=0):
        return ps[r0:r0 + rows, b * 512:b * 512 + n]

    def psk(b):
        return ("ps", b)

    def ld_tables(e):
        r = []
        r.append(e.dma_start(out=gtab[:], in_=gtab_d[:, :]))
        r.append(e.dma_start(out=qg_bc[:], in_=qg_d[0:1, :].partition_broadcast(128)))
        r.append(e.dma_start(out=kg_bc[:], in_=kg_d[0:1, :].partition_broadcast(128)))
        r.append(e.dma_start(out=cw[:], in_=cw_d[:, :]))
        r.append(e.dma_start(out=est[:], in_=sinks_d[0:1, :].partition_broadcast(128)))
        r.append(e.dma_start(out=hv_f[:], in_=hv_d[:, :]))
        r.append(e.dma_start(out=identf[:], in_=ident_d[:, :]))
        return r
    S.add("sp", ld_tables, writes=[("tab",)], ndma=7, stream="tab")
    S.add("dve", lambda e: e.tensor_copy(out=identb[:], in_=identf[:]), reads=[("tab",)], writes=[("identb",)])
    S.add("dve", lambda e: e.tensor_copy(out=hvones[:], in_=hv_f[:]), reads=[("tab",)], writes=[("hvones",)])
    S.add("dve", lambda e: e.memset(ones[:], 1.0), writes=[("ones",)])
    S.add("act", lambda e: e.activation(out=est[:], in_=est[:], func=AF.Exp), reads=[("tab",)], writes=[("est",)])

    w_dn_v = [w.rearrange("(c p) n -> p c n", p=128) for w in w_dn]
    w_qkv_v = w_qkv.rearrange("(k p) c -> p k c", p=128)
    w_o_v = w_o.rearrange("(c p) n -> p c n", p=128)

    def xk(lt):
        return [("x", lt, 0), ("x", lt, 1)]

    statc = [0]

    def newstat(n=1):
        c = statc[0]
        statc[0] += n
        if statc[0] > 768:
            c = 0
            statc[0] = n
        return c

    ctr = {"xn": 0, "sg": 0, "up": 0}

    def tiles_overlapping(tiles, a, b):
        return [t for t in tiles if t["col0"] < b and t["col0"] + t["R"] > a]

    gvec_d = din("gvec", [3, D])
    g_bc = ring[:, 3, 0:4096].bitcast(F32)
    hb = ring[:, 3, 4096:8192].rearrange("p (i n) -> p i n", i=2)
    psb0 = ps[:, 0:512].bitcast(BF16)
    psb1 = ps[:, 512:1024].bitcast(BF16)

    def norm_pipe(tiles, gcol):
        info = []
        for t in tiles:
            i = ctr["xn"] % 2
            ctr["xn"] += 1
            info.append((t, i, newstat(3)))

        def begin():
            S.add("sp", lambda e, gcol=gcol: [e.dma_start(out=g_bc, in_=gvec_d[gcol:gcol + 1, :].partition_broadcast(128))],
                  writes=[("R", 3)], ndma=1, stream="gbc", nobar=True)

        def stage1(n_):
            t, i, sc = info[n_]
            lt, R = t["lt"], t["R"]
            S.add("dve", lambda e, lt=lt, R=R, sc=sc, i=i: e.scalar_tensor_tensor(
                out=hb[0:R, i, :], in0=x_sb[0:R, lt, :], scalar=1.0, in1=x_sb[0:R, lt, :],
                op0=ALU.mult, op1=ALU.mult, accum_out=stat[0:R, sc:sc + 1]),
                reads=xk(lt) + [("R", 3)], writes=[("st", sc), ("hb", i)])
            S.add("act", lambda e, R=R, sc=sc: e.activation(
                out=stat[0:R, sc + 1:sc + 2], in_=stat[0:R, sc:sc + 1], func=AF.Sqrt, scale=1.0 / D, bias=EPS),
                reads=[("st", sc)], writes=[("st", sc + 1)])
            S.add("dve", lambda e, R=R, sc=sc: e.reciprocal(out=stat[0:R, sc + 2:sc + 3], in_=stat[0:R, sc + 1:sc + 2]),
                  reads=[("st", sc + 1)], writes=[("st", sc + 2)])
            S.add("dve", lambda e, lt=lt, R=R, sc=sc, i=i: e.scalar_tensor_tensor(
                out=hb[0:R, i, :], in0=x_sb[0:R, lt, :], scalar=stat[0:R, sc + 2:sc + 3], in1=g_bc[0:R, :],
                op0=ALU.mult, op1=ALU.mult),
                reads=xk(lt) + [("st", sc + 2), ("R", 3)], writes=[("hb", i)])

        def stage2(n_):
            t, i, sc = info[n_]
            lt, R, col0 = t["lt"], t["R"], t["col0"]
            for k in range(16):
                pb, b = (psb0, 0) if k < 8 else (psb1, 1)
                S.add("pe", lambda e, R=R, i=i, k=k, pb=pb: e.transpose(
                    out=pb[:, (k % 8) * 128:(k % 8) * 128 + R], in_=hb[0:R, i, k * 128:(k + 1) * 128],
                    identity=identb[0:R, 0:R]),
                    reads=[("hb", i), ("identb",), ("R", 3)], writes=[psk(b)])
            for b, pb in ((0, psb0), (1, psb1)):
                S.add("act", lambda e, R=R, b=b, pb=pb, col0=col0: e.activation(
                    out=hT[:, b * 8:b * 8 + 8, col0:col0 + R],
                    in_=pb.rearrange("p (a c) -> p a c", c=128)[:, :, 0:R], func=AF.Copy),
                    reads=[psk(b)], writes=[("hT", lt, b * 8 + kk) for kk in range(8)])

        return begin, stage1, stage2, len(info)

    def emit_norm(tiles, gcol):
        begin, stage1, stage2, n = norm_pipe(tiles, gcol)
        begin()
        stage1(0)
        for n_ in range(n):
            if n_ + 1 < n:
                stage1(n_ + 1)
            stage2(n_)

    def norm_hooks(tiles, gcol):
        begin, stage1, stage2, n = norm_pipe(tiles, gcol)

        def after(idx):
            stage1(idx)
            if idx >= 1:
                stage2(idx - 1)
            if idx == n - 1:
                stage2(idx)
        return begin, after

    def split2(a, b):
        m = a + ((b - a + 1) // 2)
        return [(a, m), (m, b)]

    xstage = T[:, 0:10240].rearrange("p (t n) -> p t n", n=D)

    def down_accum(tiles, nloc, wslot, scale, after=None, stage=False):
        Wd = slot_ap(wslot).rearrange("p (c n) -> p c n", n=D)
        for idx, t in enumerate(tiles):
            lt, R, col0 = t["lt"], t["R"], t["col0"]
            for half in range(2):
                b0 = 4 + 2 * half
                for cl in range(nloc):
                    for n in range(2):
                        S.add("pe", lambda e, R=R, col0=col0, cl=cl, n=n, half=half, b0=b0, Wd=Wd, nloc=nloc: e.matmul(
                            bank(b0 + n, rows=R), lhsT=actT[:, cl, col0:col0 + R],
                            rhs=Wd[:, cl, (2 * half + n) * 512:(2 * half + n + 1) * 512],
                            start=(cl == 0), stop=(cl == nloc - 1)),
                            reads=[("actT", cl, lt), ("R", wslot)], writes=[psk(b0 + n)])
                dst = xstage if stage else x_sb
                wk = [("xs", lt, half), ("wca", 0), ("wca", 1), ("m12", 0), ("m12", 1)] if stage else [("x", lt, half)]
                S.add("dve", lambda e, R=R, lt=lt, half=half, b0=b0, scale=scale, dst=dst: e.scalar_tensor_tensor(
                    out=dst[0:R, lt, half * 1024:(half + 1) * 1024], in0=ps[0:R, b0 * 512:b0 * 512 + 1024],
                    scalar=scale, in1=x_sb[0:R, lt, half * 1024:(half + 1) * 1024], op0=ALU.mult, op1=ALU.add),
                    reads=[psk(b0), psk(b0 + 1), ("x", lt, half)], writes=wk)
            if after is not None:
                after(idx)

    def load_wdn(src_v, c0, ng, wslot, extra=()):
        def f(e, c0=c0, ng=ng, wslot=wslot):
            Wd = slot_ap(wslot).rearrange("p (c n) -> p c n", n=D)
            return [e.dma_start(out=Wd[:, j:j + 1, :], in_=src_v[:, c0 + j:c0 + j + 1, :]) for j in range(ng)]
        wk = [("R", wslot)]
        if wslot == 4:
            wk += [("xs", lt_, h_) for lt_ in range(5) for h_ in range(2)] + [("uout",)]
        S.add("pool", f, reads=list(extra), writes=wk, ndma=ng, stream=("R", wslot), nobar=True)

    def load_wpair(src_t, u, wslot, extra=()):
        def f(e):
            dst = ring[:, wslot, :].rearrange("p (a n) -> p a n", n=2048)
            src = src_t[u].rearrange("p (a n) -> p a n", n=2048)
            return [e.dma_start(out=dst[:, 0:2, :], in_=src[:, 0:2, :]), e.dma_start(out=dst[:, 2:4, :], in_=src[:, 2:4, :])]
        S.add("pool", f, reads=list(extra), writes=[("R", wslot)], ndma=2, stream=("R", wslot), nobar=True)

    def load_wlast(src, wslot, extra=()):
        def f(e):
            dst = ring[:, wslot, 0:4096].rearrange("p (a n) -> p a n", n=2048)
            return [e.dma_start(out=dst, in_=src.rearrange("p (a n) -> p a n", n=2048))]
        S.add("pool", f, reads=list(extra), writes=[("R", wslot)], ndma=1, stream=("R", wslot), nobar=True)

    def emit_ffn(fi, tiles, col_lo, col_hi, last_after=None, last_pre=None, stage_last=False, first_loaded=False):
        slices = split2(col_lo, col_hi)
        units = [(u, 2 if 2 * u + 1 < NCH else 1) for u in range((NCH + 1) // 2)]
        ngroups = (NCH + 3) // 4

        def ld_unit(u, extra=()):
            if units[u][1] == 1:
                load_wlast(w_upL[fi], u % 2, extra=extra)
            else:
                load_wpair(w_upt[fi], u, u % 2, extra=extra)

        gslot = lambda gi: 4 if (fi == 0 and gi == ngroups - 1) else 2 + gi % 2

        def ld_group(gi, extra=()):
            ng = min(4, NCH - 4 * gi)
            load_wdn(w_dn_v[fi], 4 * gi, ng, gslot(gi), extra=extra)

        kst = ("ffn_started", ctr["up"])
        if not first_loaded:
            ld_unit(0)
        for u, nchk in units:
            wslot = u % 2
            if nchk == 2:
                Wv = ring[:, wslot, :].rearrange("p (a k c) -> p a k c", a=2, k=16)
            else:
                Wv = ring[:, wslot, 0:4096].rearrange("p (a k c) -> p a k c", a=2, k=16)
            for cl in range(nchk):
                c = 2 * u + cl
                cg = c % 4
                for (a, b) in slices:
                    n = b - a
                    par = ctr["up"] % 2
                    ctr["up"] += 1
                    gb, ubk = (0, 1) if par == 0 else (2, 3)
                    si = ctr["sg"] % 4
                    ctr["sg"] += 1
                    tl = tiles_overlapping(tiles, a, b)
                    for part, bk in ((0, gb), (1, ubk)):
                        for k in range(16):
                            first_mm = (u == 0 and cl == 0 and part == 0 and k == 0 and a == slices[0][0])
                            S.add("pe", lambda e, part=part, bk=bk, k=k, cl=cl, a=a, b=b, n=n, Wv=Wv: e.matmul(
                                bank(bk, n=n), lhsT=Wv[:, part, k, cl * 128:(cl + 1) * 128], rhs=hT[:, k, a:b],
                                start=(k == 0), stop=(k == 15)),
                                reads=[("R", wslot)] + [("hT", t["lt"], k) for t in tl],
                                writes=[psk(bk)] + ([kst] if first_mm else []))
                            if first_mm:
                                ld_unit(1, extra=[kst]); ld_group(0, extra=[kst]); ld_group(1, extra=[kst])
                    S.add("act", lambda e, gb=gb, n=n, si=si: e.activation(out=sg[:, si, 0:n], in_=bank(gb, n=n), func=AF.Silu),
                          reads=[psk(gb)], writes=[("sg", si)])
                    S.add("dve", lambda e, ubk=ubk, n=n, si=si, cg=cg, a=a, b=b: e.tensor_tensor(
                        out=actT[:, cg, a:b], in0=sg[:, si, 0:n], in1=bank(ubk, n=n), op=ALU.mult),
                        reads=[("sg", si), psk(ubk)], writes=[("actT", cg, t["lt"]) for t in tl])
            if u + 2 < len(units):
                ld_unit(u + 2)
            c_last = 2 * u + nchk - 1
            if c_last % 4 == 3 or c_last == NCH - 1:
                gi = c_last // 4
                is_last = (c_last == NCH - 1)
                if is_last and last_pre is not None:
                    last_pre()
                down_accum(tiles, c_last % 4 + 1, gslot(gi), 0.5,
                           after=(last_after if is_last else None), stage=(stage_last and is_last))
                if gi + 2 < ngroups:
                    ld_group(gi + 2)

    kv_ring = {"i": 0}
    try:
      for P in range(2):
          if P == 0:
              tiles = [dict(lt=0, gt=0, R=128, col0=0, kind="halo")] + \
                      [dict(lt=i, gt=i, R=128, col0=128 * i, kind="main") for i in range(1, 5)]
              cb = 128
          else:
              tiles = [dict(lt=i, gt=5 + i, R=128, col0=128 * i, kind="main") for i in range(4)] + \
                      [dict(lt=4, gt=9, R=16, col0=512, kind="samp")]
              cb = 0
          full = [t for t in tiles if t["kind"] != "halo"]
          ncols = tiles[-1]["col0"] + tiles[-1]["R"]

          for t in tiles:
              lt, gt, R = t["lt"], t["gt"], t["R"]
              if t["kind"] == "halo":
                  src = xh[:, :]
              elif t["kind"] == "main":
                  src = xm[(gt - 1) * 128:gt * 128, :]
              else:
                  src = xs[:, :]
              S.add("sp", lambda e, lt=lt, R=R, src=src: [e.dma_start(out=x_sb[0:R, lt, :], in_=src)],
                    writes=xk(lt), ndma=1, stream=("x", lt), nobar=True)

          if P == 1:
              S.add("sp", lambda e: [e.dma_start(out=oks[:, 0:124, :], in_=ck[:, 4:128, :]),
                                     e.dma_start(out=ovs[:, 0:124, :], in_=cv[:, 4:128, :])],
                    ndma=2, stream="out", nobar=True)

          emit_norm(tiles, 0)
          stop_at("norm1")
          nb_, na_ = norm_hooks(tiles, 1)
          emit_ffn(0, tiles, 0, ncols, last_after=na_, last_pre=nb_)
          stop_at("ffn1")

          def ld_gate(u, extra=()):
              load_wpair(w_gatet, u, u % 2, extra=extra)
              ws = u % 2

              def f(e, u=u, ws=ws):
                  return [e.dma_start(out=wca[:, ws, 0:2048], in_=w_cot[u]),
                          e.dma_start(out=wca[:, ws, 2048:4096], in_=w_aot[u])]
              S.add("pool", f, reads=list(extra), writes=[("wca", ws)], ndma=2, stream=("wca", ws))
          Rflat = ring[:].rearrange("p s n -> p (s n)")
          Wqkv = Rflat[:, 0:24576].rearrange("p (k c) -> p k c", c=1536)

          def ld_qkv(e):
              return [e.dma_start(out=Wqkv[:, kq * 4:kq * 4 + 4, :], in_=w_qkv_v[:, kq * 4:kq * 4 + 4, :])
                      for kq in range(4)]
          S.add("pool", ld_qkv, writes=[("R", 0), ("R", 1), ("R", 2)], ndma=4, stream="qkv", nobar=True)
          rqkv = [("R", 0), ("R", 1), ("R", 2)]
          S.barrier()
          def ld_masks():
              S.add("sp", lambda e: [e.dma_start(out=maskv[:, 0, :], in_=maskP_d[:, :]), e.dma_start(out=maskv[:, 1, :], in_=maskC_d[:, :])],
                    reads=[psk(2)], writes=[("R", 3)], ndma=2, stream="mask", nobar=True)

          for t in tiles:
              t["j"] = kv_ring["i"] % 2
              kv_ring["i"] += 1
              t["g3"] = t["gt"] % 3
              t["full"] = t["kind"] != "halo"
              if t["full"]:
                  t["sc"] = newstat(48)
              t["sk"] = newstat(12)

          def A_mm(t, n):
              lt, R, col0 = t["lt"], t["R"], t["col0"]
              for k in range(16):
                  S.add("pe", lambda e, n=n, k=k, R=R, col0=col0: e.matmul(
                      bank(n, rows=R), lhsT=hT[:, k, col0:col0 + R], rhs=Wqkv[:, k, n * 512:(n + 1) * 512],
                      start=(k == 0), stop=(k == 15)),
                      reads=rqkv + [("hT", lt, k)], writes=[psk(n)])

          def A_q1(t):
              R, sc = t["R"], t["sc"]
              S.add("act", lambda e, R=R: e.activation(out=sqb[0:R, :], in_=ps[0:R, 0:1024], func=AF.Square),
                    reads=[psk(0), psk(1)], writes=[("sqb",)])
              S.add("dve", lambda e, R=R, sc=sc: e.tensor_reduce(
                  out=stat[0:R, sc:sc + 16], in_=sqb[0:R, :].rearrange("p (h d) -> p h d", d=64), op=ALU.add, axis=AX.X),
                  reads=[("sqb",)], writes=[("st", sc)])
              S.add("act", lambda e, R=R, sc=sc: e.activation(
                  out=stat[0:R, sc + 16:sc + 32], in_=stat[0:R, sc:sc + 16], func=AF.Sqrt, scale=1.0, bias=64.0 * EPS),
                  reads=[("st", sc)], writes=[("st", sc + 16)])
              S.add("dve", lambda e, R=R, sc=sc: e.reciprocal(out=stat[0:R, sc + 32:sc + 48], in_=stat[0:R, sc + 16:sc + 32]),
                    reads=[("st", sc + 16)], writes=[("st", sc + 32)])

          def A_q2(t):
              R, sc, j = t["R"], t["sc"], t["j"]
              S.add("dve", lambda e, R=R, sc=sc: e.tensor_tensor(
                  out=sqb[0:R, :].rearrange("p (h d) -> p h d", d=64),
                  in0=ps[0:R, 0:1024].rearrange("p (h d) -> p h d", d=64),
                  in1=stat[0:R, sc + 32:sc + 48].unsqueeze(2).broadcast_to([R, 16, 64]), op=ALU.mult),
                  reads=[psk(0), psk(1), ("st", sc + 32), ("sqb",)], writes=[("sqb",)])
              S.add("pool", lambda e, R=R, j=j: e.tensor_tensor(
                  out=Qn[0:R, j, :].rearrange("p (h d) -> p h d", d=64),
                  in0=sqb[0:R, :].rearrange("p (h d) -> p h d", d=64),
                  in1=qg_bc[0:R, :].unsqueeze(1).broadcast_to([R, 16, 64]), op=ALU.mult),
                  reads=[("sqb",), ("tab",)], writes=[("Qn", j)])

          def A_k(t):
              R, sk, j = t["R"], t["sk"], t["j"]
              S.add("act", lambda e, R=R: e.activation(out=kt1[0:R, :], in_=ps[0:R, 1024:1280], func=AF.Square),
                    reads=[psk(2)], writes=[("kt1",)])
              S.add("dve", lambda e, R=R, sk=sk: e.tensor_reduce(
                  out=stat[0:R, sk:sk + 4], in_=kt1[0:R, :].rearrange("p (h d) -> p h d", d=64), op=ALU.add, axis=AX.X),
                  reads=[("kt1",)], writes=[("st", sk)])
              S.add("act", lambda e, R=R, sk=sk: e.activation(
                  out=stat[0:R, sk + 4:sk + 8], in_=stat[0:R, sk:sk + 4], func=AF.Sqrt, scale=1.0 / 64, bias=EPS),
                  reads=[("st", sk)], writes=[("st", sk + 4)])
              S.add("dve", lambda e, R=R, sk=sk: e.reciprocal(out=stat[0:R, sk + 8:sk + 12], in_=stat[0:R, sk + 4:sk + 8]),
                    reads=[("st", sk + 4)], writes=[("st", sk + 8)])
              S.add("dve", lambda e, R=R, sk=sk: e.tensor_tensor(
                  out=kt1[0:R, :].rearrange("p (h d) -> p h d", d=64),
                  in0=ps[0:R, 1024:1280].rearrange("p (h d) -> p h d", d=64),
                  in1=stat[0:R, sk + 8:sk + 12].unsqueeze(2).broadcast_to([R, 4, 64]), op=ALU.mult),
                  reads=[psk(2), ("st", sk + 8), ("kt1",)], writes=[("kt1",)])
              S.add("dve", lambda e, R=R, j=j: e.tensor_tensor(
                  out=Kn[0:R, j, :].rearrange("p (h d) -> p h d", d=64),
                  in0=kt1[0:R, :].rearrange("p (h d) -> p h d", d=64),
                  in1=kg_bc[0:R, :].unsqueeze(1).broadcast_to([R, 4, 64]), op=ALU.mult),
                  reads=[("kt1",), ("tab",)], writes=[("Kn", j)])

          def A_v(t):
              R, j, g3, gt = t["R"], t["j"], t["g3"], t["gt"]
              S.add("act", lambda e, R=R, j=j: e.activation(out=Vf[0:R, j, :], in_=ps[0:R, 1280:1536], func=AF.Copy),
                    reads=[psk(2)], writes=[("Vf", j)])
              S.add("act", lambda e, R=R, g3=g3: e.activation(out=V3[0:R, g3, :], in_=ps[0:R, 1280:1536], func=AF.Copy),
                    reads=[psk(2)], writes=[("V", g3)])

          def A_out(t):
              j, gt = t["j"], t["gt"]
              if gt == 8:
                  S.add("sp", lambda e, j=j: [e.dma_start(out=okp[:, :], in_=Kn[:, j, :]),
                                              e.dma_start(out=ovp[:, :], in_=Vf[:, j, :])],
                        reads=[("Kn", j), ("Vf", j)], ndma=2, stream=("out", "kv8"))
              if gt == 9:
                  S.add("sp", lambda e, j=j: [e.dma_start(out=oks[s_, 124:128, :], in_=Kn[4 * s_:4 * s_ + 4, j, :]) for s_ in range(4)] +
                                             [e.dma_start(out=ovs[s_, 124:128, :], in_=Vf[4 * s_:4 * s_ + 4, j, :]) for s_ in range(4)],
                        reads=[("Kn", j), ("Vf", j)], ndma=8, stream=("out", "kv9"))

          psb3 = ps[:, 0 * 512:1 * 512].bitcast(BF16)
          psb4 = ps[:, 1 * 512:2 * 512].bitcast(BF16)

          def B(t):
              R, j, g3 = t["R"], t["j"], t["g3"]
              if t["full"]:
                  for h in range(16):
                      pb = psb3 if h < 8 else psb4
                      bk = 0 if h < 8 else 1
                      S.add("pe", lambda e, h=h, pb=pb, R=R, j=j: e.transpose(
                          out=pb[0:64, (h % 8) * 128:(h % 8) * 128 + R], in_=Qn[0:R, j, h * 64:(h + 1) * 64],
                          identity=identb[0:R, 0:R]),
                          reads=[("Qn", j), ("identb",)], writes=[psk(bk)])
                  S.add("act", lambda e, R=R, j=j: e.activation(
                      out=QTt[:, j, 0:8, 0:R], in_=psb3[0:64, :].rearrange("p (a c) -> p a c", c=128)[:, :, 0:R], func=AF.Copy),
                      reads=[psk(0)], writes=[("QT", j, 0)])
                  S.add("act", lambda e, R=R, j=j: e.activation(
                      out=QTt[:, j, 8:16, 0:R], in_=psb4[0:64, :].rearrange("p (a c) -> p a c", c=128)[:, :, 0:R], func=AF.Copy),
                      reads=[psk(1)], writes=[("QT", j, 1)])
              for kvh in range(4):
                  S.add("pe", lambda e, kvh=kvh, R=R, j=j: e.transpose(
                      out=ps[0:64, 2 * 512 + kvh * 128:2 * 512 + kvh * 128 + R], in_=Kn[0:R, j, kvh * 64:(kvh + 1) * 64],
                      identity=identf[0:R, 0:R]),
                      reads=[("Kn", j), ("tab",)], writes=[psk(2)])
              S.add("dve", lambda e, R=R, g3=g3: e.tensor_copy(
                  out=KT3[0:64, :, g3, 0:R],
                  in_=ps[0:64, 2 * 512:3 * 512].rearrange("p (a c) -> p a c", c=128)[:, :, 0:R]),
                  reads=[psk(2)], writes=[("KT", g3)])

          def C_main(t, hooks):
              lt, gt, j, g3 = t["lt"], t["gt"], t["j"], t["g3"]
              acol = t["col0"] - cb
              p3 = (gt - 1) % 3

              def S_(kvh):
                  e_i = kvh % 2
                  for kb, (k3, bk) in enumerate(((p3, 6), (g3, 7))):
                      for g in range(4):
                          S.add("pe", lambda e, kvh=kvh, g=g, k3=k3, bk=bk, j=j: e.matmul(
                              ps[:, bk * 512 + g * 128:bk * 512 + (g + 1) * 128],
                              lhsT=KT3[0:64, kvh, k3, :], rhs=QTt[:, j, 4 * kvh + g, :],
                              start=True, stop=True),
                              reads=[("KT", k3), ("QT", j, 0), ("QT", j, 1)], writes=[psk(bk)])
                      S.add("act", lambda e, bk=bk, e_i=e_i, kb=kb: e.activation(
                          out=E[:, e_i, kb, :], in_=bank(bk), func=AF.Exp), reads=[psk(bk)], writes=[("E", e_i, kb)])
                      S.add("dve" if kb == 0 else "pool", lambda e, kvh=kvh, e_i=e_i, kb=kb: e.tensor_tensor(
                          out=PT[:, e_i, kb, :], in0=E[:, e_i, kb, :], in1=maskv[:, kb, kvh * 512:(kvh + 1) * 512], op=ALU.mult),
                          reads=[("E", e_i, kb), ("R", 3)], writes=[("PT", e_i, kb)])

              def V_(kvh):
                  e_i = kvh % 2
                  ob = 3 if kvh % 2 == 0 else 4
                  for g in range(4):
                      hf, g2 = g % 2, g // 2
                      for kb, k3 in enumerate((p3, g3)):
                          S.add("pe", lambda e, kvh=kvh, g=g, hf=hf, g2=g2, kb=kb, k3=k3, ob=ob, e_i=e_i: e.matmul(
                              ps[hf * 64:hf * 64 + 64, ob * 512 + g2 * 128:ob * 512 + (g2 + 1) * 128],
                              lhsT=V3[:, k3, kvh * 64:(kvh + 1) * 64], rhs=PT[:, e_i, kb, g * 128:(g + 1) * 128],
                              start=(kb == 0), stop=(kb == 1)),
                              reads=[("V", k3), ("PT", e_i, kb)], writes=[psk(ob)])
                  for g in range(4):
                      hf, g2 = g % 2, g // 2
                      for kb in range(2):
                          lw = hvones if (gt == 1 and kb == 0) else ones
                          S.add("pe", lambda e, g=g, hf=hf, g2=g2, kb=kb, ob=ob, e_i=e_i, lw=lw: e.matmul(
                              ps[hf * 64:hf * 64 + 64, ob * 512 + 256 + g2 * 128:ob * 512 + 256 + (g2 + 1) * 128],
                              lhsT=lw[:, :], rhs=PT[:, e_i, kb, g * 128:(g + 1) * 128],
                              start=(kb == 0), stop=(kb == 1)),
                              reads=[("ones",), ("hvones",), ("PT", e_i, kb)], writes=[psk(ob)])
                  for g in range(4):
                      hf, g2 = g % 2, g // 2
                      hh = 4 * kvh + g
                      S.add("act", lambda e, hf=hf, g2=g2, hh=hh, ob=ob, e_i=e_i: e.activation(
                          out=dtmp[hf * 64:hf * 64 + 64, e_i, g2 * 128:(g2 + 1) * 128],
                          in_=ps[hf * 64:hf * 64 + 64, ob * 512 + 256 + g2 * 128:ob * 512 + 256 + (g2 + 1) * 128],
                          func=AF.Identity, bias=est[hf * 64:hf * 64 + 64, hh:hh + 1], scale=1.0),
                          reads=[psk(ob), ("est",), ("dtmp", e_i)], writes=[("dtmp", e_i)])
                  S.add("dve", lambda e, e_i=e_i: e.reciprocal(out=dtmp[:, e_i, 0:256], in_=dtmp[:, e_i, 0:256]),
                        reads=[("dtmp", e_i)], writes=[("dtmp", e_i)])
                  S.add("dve", lambda e, ob=ob, e_i=e_i, kvh=kvh, acol=acol: e.tensor_tensor(
                      out=AT[:, 2 * kvh:2 * kvh + 2, acol:acol + 128],
                      in0=ps[:, ob * 512:ob * 512 + 256].rearrange("p (g q) -> p g q", q=128),
                      in1=dtmp[:, e_i, 0:256].rearrange("p (g q) -> p g q", q=128), op=ALU.mult),
                      reads=[psk(ob), ("dtmp", e_i)], writes=[("AT", lt)])

              def H_(i):
                  for hk in hooks.get(i, []):
                      hk()

              S_(0); H_(0); S_(1); V_(0); H_(1); S_(2); V_(1); H_(2); S_(3); V_(2); H_(3); V_(3)

          def C_samp(t):
              lt, j, g3 = t["lt"], t["j"], t["g3"]
              acol = t["col0"] - cb
              for sbi in range(4):
                  bk = 3 + sbi // 2
                  S.add("pe", lambda e, sbi=sbi, bk=bk, g3=g3: e.matmul(
                      ps[0:4, bk * 512 + (sbi % 2) * 256:bk * 512 + (sbi % 2) * 256 + 256],
                      lhsT=identb[0:16, 4 * sbi:4 * sbi + 4], rhs=V3[0:16, g3, :], start=True, stop=True),
                      reads=[("identb",), ("V", g3)], writes=[psk(bk)])
              S.add("dve", lambda e: e.tensor_copy(out=Vn.rearrange("p s c -> p (s c)"), in_=ps[0:4, 3 * 512:5 * 512]),
                    reads=[psk(3), psk(4)], writes=[("Vn",)])
              for sbi in range(4):
                  for kvh in range(4):
                      c0 = sbi * 64 + kvh * 16
                      S.add("pe", lambda e, sbi=sbi, kvh=kvh, c0=c0, j=j: e.matmul(
                          ps[:, 6 * 512 + c0:6 * 512 + c0 + 16], lhsT=KTc[:, sbi, kvh, :],
                          rhs=QTt[:, j, 4 * kvh:4 * kvh + 4, 4 * sbi:4 * sbi + 4], start=True, stop=True),
                          reads=[("KTc",), ("QT", j, 0), ("QT", j, 1)], writes=[psk(6)])
                      S.add("pe", lambda e, sbi=sbi, kvh=kvh, c0=c0, j=j, g3=g3: e.matmul(
                          ps[0:4, 7 * 512 + c0:7 * 512 + c0 + 16], lhsT=KT3[0:64, kvh, g3, 4 * sbi:4 * sbi + 4],
                          rhs=QTt[:, j, 4 * kvh:4 * kvh + 4, 4 * sbi:4 * sbi + 4], start=True, stop=True),
                          reads=[("KT", g3), ("QT", j, 0), ("QT", j, 1)], writes=[psk(7)])
              S.add("act", lambda e: e.activation(out=E[:, 0, 0, 0:256], in_=ps[:, 6 * 512:6 * 512 + 256], func=AF.Exp),
                    reads=[psk(6)], writes=[("E", 0, 0)])
              S.add("act", lambda e: e.activation(out=E[0:4, 0, 1, 0:256], in_=ps[0:4, 7 * 512:7 * 512 + 256], func=AF.Exp),
                    reads=[psk(7)], writes=[("E", 0, 1)])
              S.add("dve", lambda e: e.tensor_tensor(
                  out=PT[:, 0, 0, 0:256].rearrange("p (s h q) -> p s h q", s=4, h=16),
                  in0=E[:, 0, 0, 0:256].rearrange("p (s h q) -> p s h q", s=4, h=16),
                  in1=maskv[:, 0, :].rearrange("p (h q) -> p h q", q=128)[:, :, 0:4].unsqueeze(1).broadcast_to([128, 4, 16, 4]),
                  op=ALU.mult), reads=[("E", 0, 0), ("R", 3)], writes=[("PT", 0, 0)])
              S.add("dve", lambda e: e.tensor_tensor(
                  out=PT[0:4, 0, 1, 0:256].rearrange("p (s h q) -> p s h q", s=4, h=16),
                  in0=E[0:4, 0, 1, 0:256].rearrange("p (s h q) -> p s h q", s=4, h=16),
                  in1=maskv[0:4, 1, :].rearrange("p (h q) -> p h q", q=128)[:, :, 0:4].unsqueeze(1).broadcast_to([4, 4, 16, 4]),
                  op=ALU.mult), reads=[("E", 0, 1), ("R", 3)], writes=[("PT", 0, 1)])
              for sbi in range(4):
                  for kvh in range(4):
                      for g in range(4):
                          h = 4 * kvh + g
                          hf = h % 2
                          pc = sbi * 64 + kvh * 16 + g * 4
                          oc = (h // 2) * 16 + sbi * 4
                          S.add("pe", lambda e, sbi=sbi, kvh=kvh, pc=pc, oc=oc, hf=hf: e.matmul(
                              ps[hf * 64:hf * 64 + 64, oc:oc + 4], lhsT=Vc[:, sbi, kvh * 64:(kvh + 1) * 64],
                              rhs=PT[:, 0, 0, pc:pc + 4], start=True, stop=False),
                              reads=[("Vc",), ("PT", 0, 0)], writes=[psk(0)])
                          S.add("pe", lambda e, sbi=sbi, kvh=kvh, pc=pc, oc=oc, hf=hf: e.matmul(
                              ps[hf * 64:hf * 64 + 64, oc:oc + 4], lhsT=Vn[:, sbi, kvh * 64:(kvh + 1) * 64],
                              rhs=PT[0:4, 0, 1, pc:pc + 4], start=False, stop=True),
                              reads=[("Vn",), ("PT", 0, 1)], writes=[psk(0)])
                          S.add("pe", lambda e, pc=pc, oc=oc, hf=hf: e.matmul(
                              ps[hf * 64:hf * 64 + 64, 512 + oc:512 + oc + 4], lhsT=ones[:, :],
                              rhs=PT[:, 0, 0, pc:pc + 4], start=True, stop=False),
                              reads=[("ones",), ("PT", 0, 0)], writes=[psk(1)])
                          S.add("pe", lambda e, pc=pc, oc=oc, hf=hf: e.matmul(
                              ps[hf * 64:hf * 64 + 64, 512 + oc:512 + oc + 4], lhsT=ones[0:4, :],
                              rhs=PT[0:4, 0, 1, pc:pc + 4], start=False, stop=True),
                              reads=[("ones",), ("PT", 0, 1)], writes=[psk(1)])
              for h in range(16):
                  hf, c = h % 2, h // 2
                  S.add("dve", lambda e, h=h, hf=hf, c=c: e.tensor_scalar(
                      out=dtmp[hf * 64:hf * 64 + 64, 0, c * 16:(c + 1) * 16],
                      in0=ps[hf * 64:hf * 64 + 64, 512 + c * 16:512 + (c + 1) * 16],
                      scalar1=est[hf * 64:hf * 64 + 64, h:h + 1], scalar2=None, op0=ALU.add),
                      reads=[psk(1), ("est",), ("dtmp", 0)], writes=[("dtmp", 0)])
              S.add("dve", lambda e: e.reciprocal(out=dtmp[:, 0, 0:128], in_=dtmp[:, 0, 0:128]),
                    reads=[("dtmp", 0)], writes=[("dtmp", 0)])
              S.add("dve", lambda e, acol=acol: e.tensor_tensor(
                  out=AT[:, :, acol:acol + 16],
                  in0=ps[:, 0:128].rearrange("p (c q) -> p c q", q=16),
                  in1=dtmp[:, 0, 0:128].rearrange("p (c q) -> p c q", q=16), op=ALU.mult),
                  reads=[psk(0), ("dtmp", 0)], writes=[("AT", lt)])

          def conv_w(cu):
              ss = cu % 2
              return Rflat[:, ss * 12288:(ss + 1) * 12288].rearrange("p (a k c) -> p a k c", a=3, k=16)

          def conv_keys(cu):
              return [("R", 0), ("R1a",)] if cu % 2 == 0 else [("R1b",), ("R", 2)]

          def ld_conv(cu):
              ss = cu % 2

              def f(e, cu=cu, ss=ss):
                  dst = Rflat[:, ss * 12288:(ss + 1) * 12288].rearrange("p (a n) -> p a n", n=2048)
                  src = w_convt[cu].rearrange("p (a n) -> p a n", n=2048)
                  return [e.dma_start(out=dst[:, 0:3, :], in_=src[:, 0:3, :]), e.dma_start(out=dst[:, 3:6, :], in_=src[:, 3:6, :])]
              S.add("pool", f, writes=conv_keys(cu), ndma=2, stream=("RS", ss), nobar=True)

          t0_ = tiles[0]
          for n in ([0, 1, 2] if t0_["full"] else [2]):
              A_mm(t0_, n)
          ld_masks()
          A_k(t0_); A_v(t0_)
          if t0_["full"]:
              A_q1(t0_); A_q2(t0_)
          A_out(t0_)
          if P == 1:
              S.add("sp", lambda e: [e.dma_start(out=ckf, in_=ck.rearrange("s p c -> p s c"))],
                    writes=[("E", 0, 0), ("E", 0, 1)], ndma=1, stream="cache")
              S.add("pool", lambda e: [e.dma_start(out=Vc, in_=cv.rearrange("s p c -> p s c"))],
                    writes=[("Vc",)], ndma=1, stream="cachev")
              for sbi in range(4):
                  for kvh in range(4):
                      S.add("pe", lambda e, sbi=sbi, kvh=kvh: e.transpose(
                          out=ps[0:64, 5 * 512 + kvh * 128:5 * 512 + (kvh + 1) * 128],
                          in_=ckf[:, sbi, kvh * 64:(kvh + 1) * 64], identity=identf[:, :]),
                          reads=[("E", 0, 0), ("E", 0, 1), ("tab",)], writes=[psk(5)])
                  S.add("dve", lambda e, sbi=sbi: e.tensor_copy(
                      out=KTc[:, sbi, :, :], in_=ps[0:64, 5 * 512:6 * 512].rearrange("p (a c) -> p a c", c=128)),
                      reads=[psk(5)], writes=[("KTc",)])

          B(t0_)
          for idx, t in enumerate(tiles):
              nxt = tiles[idx + 1] if idx + 1 < len(tiles) else None
              hooks = {}
              if nxt is not None:
                  last_mm = (idx + 2 == len(tiles))
                  hooks[0] = [lambda nxt=nxt: A_mm(nxt, 2), lambda nxt=nxt: A_mm(nxt, 0), lambda nxt=nxt: A_k(nxt),
                              lambda nxt=nxt: A_v(nxt)]
                  hooks[1] = [lambda nxt=nxt: A_mm(nxt, 1), lambda nxt=nxt: A_q1(nxt)]
                  if last_mm:
                      hooks[1].append(lambda: ld_conv(0))
                      hooks[1].append(lambda: ld_conv(1))
                  hooks[2] = [lambda nxt=nxt: A_q2(nxt), lambda nxt=nxt: A_out(nxt)]
                  hooks[3] = [lambda nxt=nxt: B(nxt)]
              if t["kind"] == "main":
                  C_main(t, hooks)
              elif t["kind"] == "samp":
                  C_samp(t)
              else:
                  for kv_ in range(4):
                      for hk in hooks.get(kv_, []):
                          hk()

          stop_at("attn")
          S.barrier()
          if P == 1:
              S.add("sp", lambda e: [e.dma_start(out=sct, in_=sconv[:, :])], writes=[("sct",)], ndma=1, stream="cache")
              for c in range(8):
                  S.add("pe", lambda e, c=c: e.transpose(
                      out=ps[:, 6 * 512 + c * 8:6 * 512 + c * 8 + 8], in_=sct[:, c * 128:(c + 1) * 128],
                      identity=identf[0:8, 0:8]),
                      reads=[("sct",), ("tab",)], writes=[psk(6)])
              S.add("dve", lambda e: e.tensor_copy(out=stT[:].rearrange("p a b -> p (a b)"), in_=ps[:, 6 * 512:6 * 512 + 64]),
                    reads=[psk(6)], writes=[("stT",)])

          nmain = 512
          clo = cb - 2 if P == 0 else 0
          cslices = split2(clo, ncols)
          for c in range(8):
              cu, cl = c // 2, c % 2
              ss = cu % 2
              Wc = conv_w(cu)
              rW = conv_keys(cu)
              i = c % 2
              if P == 1:
                  S.add("dve", lambda e, i=i, c=c: e.tensor_copy(out=ub[:, i, 0:2], in_=ucarry[:, c, :]),
                        reads=[("ucarry", c)], writes=[("ubh", i)])
              for si, (a, b) in enumerate(cslices):
                  n = b - a
                  pa = a - (cb - 2)
                  par = (c * 2 + si) % 2
                  bx, bc_, bb = (0, 1, 2) if par == 0 else (3, 4, 5)
                  tl = tiles_overlapping(tiles, a, b)
                  for part, bk in ((0, bx), (2, bc_), (1, bb)):
                      for k in range(16):
                          S.add("pe", lambda e, part=part, bk=bk, k=k, a=a, b=b, n=n, Wc=Wc, cl=cl: e.matmul(
                              bank(bk, n=n), lhsT=Wc[:, part, k, cl * 128:(cl + 1) * 128], rhs=hT[:, k, a:b],
                              start=(k == 0), stop=(k == 15)),
                              reads=rW + [("hT", t_["lt"], k) for t_ in tl],
                              writes=[psk(bk)] + ([("c3s",)] if (c == 6 and si == 0 and part == 0 and k == 0) else []),
                              nobar=True)
                  xi = par
                  S.add("act", lambda e, bx=bx, n=n, xi=xi: e.activation(out=xcs[:, xi, 0:n], in_=bank(bx, n=n), func=AF.Copy),
                        reads=[psk(bx)], writes=[("xcs", xi)])
                  S.add("dve", lambda e, bc_=bc_, n=n, xi=xi, i=i, pa=pa: e.tensor_tensor(
                      out=ub[:, i, pa:pa + n], in0=xcs[:, xi, 0:n], in1=bank(bc_, n=n), op=ALU.mult),
                      reads=[("xcs", xi), psk(bc_)], writes=[("ub", i, si)])
                  S.add("act", lambda e, bb=bb, n=n, i=i, pa=pa: e.activation(out=bgs[:, i, pa:pa + n], in_=bank(bb, n=n), func=AF.Copy),
                        reads=[psk(bb)], writes=[("bgs", i, si)])
              if cl == 1 and cu + 2 < 4:
                  ld_conv(cu + 2)
              if c == 7:
                  ld_gate(0); ld_gate(1, extra=[("c3s",)])
              ubk = [("ub", i, 0), ("ub", i, 1), ("ubh", i)]
              S.add("dve", lambda e, i=i, c=c: e.tensor_scalar(
                  out=yt[:, i, :], in0=ub[:, i, 2:2 + nmain], scalar1=cw[:, c * 3 + 2:c * 3 + 3], scalar2=None, op0=ALU.mult),
                  reads=ubk + [("tab",)], writes=[("yt", i)])
              for jj in (1, 0):
                  S.add("dve", lambda e, i=i, c=c, jj=jj: e.scalar_tensor_tensor(
                      out=yt[:, i, :], in0=ub[:, i, jj:jj + nmain], scalar=cw[:, c * 3 + jj:c * 3 + jj + 1], in1=yt[:, i, :],
                      op0=ALU.mult, op1=ALU.add),
                      reads=ubk + [("tab",), ("yt", i)], writes=[("yt", i)])
              S.add("pool", lambda e, i=i, c=c: e.tensor_tensor(
                  out=YB[:, c, 0:nmain], in0=yt[:, i, :], in1=bgs[:, i, 2:2 + nmain], op=ALU.mult),
                  reads=[("yt", i), ("bgs", i, 0), ("bgs", i, 1)], writes=[("YB", c)])
              if P == 0:
                  S.add("dve", lambda e, i=i, c=c: e.tensor_copy(out=ucarry[:, c, :], in_=ub[:, i, nmain:nmain + 2]),
                        reads=ubk, writes=[("ucarry", c)])
              else:
                  S.add("dve", lambda e, c=c: e.tensor_copy(out=us[:, :, 0:2], in_=stT[:, c, :].rearrange("p (s j) -> p s j", j=2)),
                        reads=[("stT",), ("us",)], writes=[("us",)])
                  S.add("dve", lambda e, i=i: e.tensor_copy(out=us[:, :, 2:6], in_=ub[:, i, 514:530].rearrange("p (s t) -> p s t", t=4)),
                        reads=ubk + [("us",)], writes=[("us",)])
                  S.add("dve", lambda e, c=c: e.tensor_scalar(
                      out=ysm[:, :, :], in0=us[:, :, 2:6], scalar1=cw[:, c * 3 + 2:c * 3 + 3], scalar2=None, op0=ALU.mult),
                      reads=[("us",), ("tab",), ("ysm",)], writes=[("ysm",)])
                  for jj in (1, 0):
                      S.add("dve", lambda e, c=c, jj=jj: e.scalar_tensor_tensor(
                          out=ysm[:, :, :], in0=us[:, :, jj:jj + 4], scalar=cw[:, c * 3 + jj:c * 3 + jj + 1], in1=ysm[:, :, :],
                          op0=ALU.mult, op1=ALU.add),
                          reads=[("us",), ("tab",), ("ysm",)], writes=[("ysm",)])
                  S.add("dve", lambda e, i=i, c=c: e.tensor_tensor(
                      out=YB[:, c, 512:528].rearrange("p (s t) -> p s t", t=4), in0=ysm[:, :, :],
                      in1=bgs[:, i, 514:530].rearrange("p (s t) -> p s t", t=4), op=ALU.mult),
                      reads=[("ysm",), ("bgs", i, 0), ("bgs", i, 1)], writes=[("YBs", c)])
                  S.add("dve", lambda e, c=c: e.tensor_copy(
                      out=ucat[:, c, 0:8].rearrange("p (s j) -> p s j", j=2), in_=us[:, :, 4:6]),
                      reads=[("us",)], writes=[("ucat", c)])
                  S.add("dve", lambda e, i=i, c=c: e.tensor_copy(out=ucat[:, c, 8:10], in_=ub[:, i, 512:514]),
                        reads=ubk, writes=[("ucat2", c)])
          if P == 1:
              for c in range(8):
                  bk = 6 + c // 4
                  S.add("pe", lambda e, c=c, bk=bk: e.transpose(
                      out=ps[0:10, bk * 512 + (c % 4) * 128:bk * 512 + (c % 4 + 1) * 128], in_=ucat[:, c, :], identity=identf[:, :]),
                      reads=[("ucat", c), ("ucat2", c), ("tab",)], writes=[psk(bk)])
              S.add("dve", lambda e: e.tensor_copy(out=uout, in_=ps[0:10, 6 * 512:8 * 512]),
                    reads=[psk(6), psk(7)], writes=[("uout",), ("yt", 0), ("yt", 1)])
              S.add("sp", lambda e: [e.dma_start(out=ocs[:, :], in_=uout[0:8, :]), e.dma_start(out=ocp[:, :], in_=uout[8:10, :])],
                    reads=[("uout",)], ndma=2, stream=("out", "conv"))

          stop_at("conv")
          fslices = split2(cb, ncols)

          eslot = lambda gi: 2 + (gi + 1) % 2
          load_wdn(w_o_v, 0, 4, eslot(0)); load_wdn(w_o_v, 4, 4, eslot(1))
          n2b_, n2a_ = norm_hooks(full, 2)
          for u in range(8):
              ws = u % 2
              Wg = ring[:, ws, :].rearrange("p (a k c) -> p a k c", a=2, k=16)
              co = wca[:, ws, 0:2048].rearrange("p (c n) -> p c n", n=256)
              ao = wca[:, ws, 2048:4096].rearrange("p (c n) -> p c n", n=256)
              for jl in range(2):
                  jx = 2 * u + jl
                  jg = jx % 4
                  for (a, b) in fslices:
                      n = b - a
                      aa = a - cb
                      tl = tiles_overlapping(full, a, b)
                      for c in range(8):
                          S.add("pe", lambda e, c=c, jl=jl, aa=aa, n=n, co=co: e.matmul(
                              bank(0, n=n), lhsT=co[:, c, jl * 128:(jl + 1) * 128], rhs=YB[:, c, aa:aa + n],
                              start=(c == 0), stop=(c == 7)),
                              reads=[("wca", ws), ("YB", c), ("YBs", c)], writes=[psk(0)])
                      for part, bk in ((0, 1), (1, 3)):
                          for k in range(16):
                              S.add("pe", lambda e, part=part, bk=bk, k=k, jl=jl, a=a, b=b, n=n, Wg=Wg: e.matmul(
                                  bank(bk, n=n), lhsT=Wg[:, part, k, jl * 128:(jl + 1) * 128], rhs=hT[:, k, a:b],
                                  start=(k == 0), stop=(k == 15)),
                                  reads=[("R", ws)] + [("hT", t["lt"], k) for t in tl], writes=[psk(bk)])
                      for c in range(8):
                          S.add("pe", lambda e, c=c, jl=jl, aa=aa, n=n, ao=ao: e.matmul(
                              bank(2, n=n), lhsT=ao[:, c, jl * 128:(jl + 1) * 128], rhs=AT[:, c, aa:aa + n],
                              start=(c == 0), stop=(c == 7)),
                              reads=[("wca", ws)] + [("AT", t["lt"]) for t in tl], writes=[psk(2)])
                      s0 = ctr["sg"] % 4
                      s1 = (ctr["sg"] + 1) % 4
                      ctr["sg"] += 2
                      S.add("act", lambda e, n=n, s0=s0: e.activation(out=sg[:, s0, 0:n], in_=bank(1, n=n), func=AF.Sigmoid),
                            reads=[psk(1)], writes=[("sg", s0)])
                      S.add("dve", lambda e, n=n, s0=s0: e.tensor_tensor(out=m12[:, 0, 0:n], in0=sg[:, s0, 0:n], in1=bank(0, n=n), op=ALU.mult),
                            reads=[("sg", s0), psk(0), ("m12", 0)], writes=[("m12", 0)])
                      S.add("act", lambda e, n=n, s1=s1: e.activation(out=sg[:, s1, 0:n], in_=bank(3, n=n), func=AF.Sigmoid),
                            reads=[psk(3)], writes=[("sg", s1)])
                      S.add("dve", lambda e, n=n, s1=s1: e.tensor_tensor(out=m12[:, 1, 0:n], in0=sg[:, s1, 0:n], in1=bank(2, n=n), op=ALU.mult),
                            reads=[("sg", s1), psk(2), ("m12", 1)], writes=[("m12", 1)])
                      S.add("pool", lambda e, n=n, jg=jg, a=a, b=b: e.tensor_tensor(
                          out=actT[:, jg, a:b], in0=m12[:, 0, 0:n], in1=m12[:, 1, 0:n], op=ALU.add),
                          reads=[("m12", 0), ("m12", 1)], writes=[("actT", jg, t["lt"]) for t in tl])
              if u + 2 < 8:
                  ld_gate(u + 2)
              if u == 7:
                  load_wpair(w_upt[1], 0, 0)
              if u % 2 == 1:
                  gi = u // 2
                  if gi == 3:
                      n2b_()
                  down_accum(full, 4, eslot(gi), 1.0, after=(n2a_ if gi == 3 else None))
                  if gi + 2 < 4:
                      load_wdn(w_o_v, 4 * (gi + 2), 4, eslot(gi))

          stop_at("stageE")
          st_ = (P == 0)
          emit_ffn(1, full, cb, ncols, stage_last=st_, first_loaded=True)
          for t in full:
              lt, gt, R = t["lt"], t["gt"], t["R"]
              dst = yp[(gt - 1) * 128:gt * 128, :] if t["kind"] == "main" else ys[:, :]
              srcb = xstage if st_ else x_sb
              rk = [("xs", lt, 0), ("xs", lt, 1)] if st_ else xk(lt)
              S.add("sp", lambda e, lt=lt, R=R, dst=dst, srcb=srcb: [e.dma_start(out=dst, in_=srcb[0:R, lt, :])],
                    reads=rk, ndma=1, stream=("out", P, lt))

    except _Stop:
        for lt in range(1, 5):
            S.add("sp", lambda e, lt=lt: [e.dma_start(out=yp[(lt - 1) * 128:lt * 128, :], in_=x_sb[:, lt, :])],
                  reads=xk(lt), ndma=1, stream=("out", "dbg", lt))
    S.emit(nc, final_streams=("out", "('out'"))
    es.close()
    return nc


_CACHE = {}


def _consts():
    h = np.arange(1, 17, dtype=np.float64)
    slopes = np.exp2(-8.0 * h / 16.0)
    ik = np.arange(128)[:, None, None]
    iq = np.arange(128)[None, None, :]
    sl = slopes[None, :, None]
    dP = 128 + iq - ik
    dC = iq - ik
    mP = np.where(ik >= iq, np.exp(-sl * dP), 0.0).astype(np.float32).reshape(128, 2048)
    mC = np.where(ik <= iq, np.exp(-sl * dC), 0.0).astype(np.float32).reshape(128, 2048)
    return mP, mC


def kernel(x_prompt, x_sample, state_conv, cache_k_win, cache_v_win, meta_tokens,
           ffn1_norm, ffn1_w_up, ffn1_w_down, mix_norm, w_in, q_norm, k_norm,
           conv_w, w_conv_out, attn_sinks, w_attn_out, w_o,
           ffn2_norm, ffn2_w_up, ffn2_w_down):
    if "nc" not in _CACHE:
        _CACHE["nc"] = build_program()
    nc = _CACHE["nc"]
    in_maps = make_in_maps(x_prompt, x_sample, state_conv, cache_k_win, cache_v_win, meta_tokens,
                           ffn1_norm, ffn1_w_up, ffn1_w_down, mix_norm, w_in, q_norm, k_norm,
                           conv_w, w_conv_out, attn_sinks, w_attn_out, w_o,
                           ffn2_norm, ffn2_w_up, ffn2_w_down)
    res = run_bass_kernel_spmd(nc, in_maps, core_ids=list(range(NCORES)))
    return assemble(res.results)


def make_in_maps(x_prompt, x_sample, state_conv, cache_k_win, cache_v_win, meta_tokens,
                 ffn1_norm, ffn1_w_up, ffn1_w_down, mix_norm, w_in, q_norm, k_norm,
                 conv_w, w_conv_out, attn_sinks, w_attn_out, w_o,
                 ffn2_norm, ffn2_w_up, ffn2_w_down):
    f = lambda a: np.ascontiguousarray(np.asarray(a, dtype=np.float32))
    x_prompt, x_sample, state_conv = f(x_prompt), f(x_sample), f(state_conv)
    cache_k_win, cache_v_win, meta_tokens = f(cache_k_win), f(cache_v_win), f(meta_tokens)
    mP, mC = _consts()
    gt = lambda g: f(g).reshape(16, 128).T
    gtab = np.ascontiguousarray(np.concatenate([gt(ffn1_norm[0]), gt(mix_norm[0]), gt(ffn2_norm[0])], axis=1))
    cw = np.ascontiguousarray(f(conv_w)[0].reshape(3, 8, 128).transpose(2, 1, 0).reshape(128, 24))
    wi = f(w_in)[0]

    def tile_up(w):
        w4 = f(w)[0].reshape(16, 128, 2, DFF)
        main = np.ascontiguousarray(w4[:, :, :, :21 * 256].reshape(16, 128, 2, 21, 256).transpose(3, 1, 2, 0, 4).reshape(21, 128, 8192))
        last = np.ascontiguousarray(w4[:, :, :, 21 * 256:].transpose(1, 2, 0, 3).reshape(128, 4096))
        return main, last

    w_convt = np.ascontiguousarray(wi[:, 0:3072].reshape(16, 128, 3, 4, 256).transpose(3, 1, 2, 0, 4).reshape(4, 128, 12288))
    w_gatet = np.ascontiguousarray(wi[:, 4608:8704].reshape(16, 128, 2, 8, 256).transpose(3, 1, 2, 0, 4).reshape(8, 128, 8192))
    tile_c = lambda w: np.ascontiguousarray(f(w)[0].reshape(8, 128, 8, 256).transpose(2, 1, 0, 3).reshape(8, 128, 2048))
    up1, up1L = tile_up(ffn1_w_up)
    up2, up2L = tile_up(ffn2_w_up)
    shared = {
        "w_up1t": up1, "w_up1L": up1L, "w_dn1": f(ffn1_w_down)[0], "w_up2t": up2, "w_up2L": up2L, "w_dn2": f(ffn2_w_down)[0],
        "w_qkv": np.ascontiguousarray(wi[:, 3072:4608]), "w_convt": w_convt, "w_gatet": w_gatet,
        "w_cot": tile_c(w_conv_out), "w_aot": tile_c(w_attn_out), "w_o": f(w_o)[0],
        "gtab": gtab, "gvec": np.ascontiguousarray(np.stack([f(ffn1_norm)[0], f(mix_norm)[0], f(ffn2_norm)[0]])), "qg": f(q_norm).reshape(1, 64), "kg": f(k_norm).reshape(1, 64), "cw": cw,
        "sinks": f(attn_sinks).reshape(1, 16), "maskP": mP, "maskC": mC, "ident": np.eye(128, dtype=np.float32),
    }
    in_maps = []
    for c in range(NCORES):
        b, half = c // 2, c % 2
        if half == 0:
            xh = np.zeros((128, D), np.float32)
            xh[112:] = meta_tokens
            hv = np.zeros((128, 64), np.float32)
            hv[112:] = 1.0
        else:
            xh = x_prompt[b, 896:1024]
            hv = np.ones((128, 64), np.float32)
        m = dict(shared)
        m.update({
            "xh": np.ascontiguousarray(xh), "hv": hv,
            "xm": np.ascontiguousarray(x_prompt[b, half * 1024:(half + 1) * 1024]),
            "xs": np.ascontiguousarray(x_sample[4 * c:4 * c + 4].reshape(16, D)),
            "sconv": np.ascontiguousarray(state_conv[0, 4 * c:4 * c + 4].reshape(8, 1024)),
            "ck": np.ascontiguousarray(cache_k_win[0, 4 * c:4 * c + 4].reshape(4, 128, 256)),
            "cv": np.ascontiguousarray(cache_v_win[0, 4 * c:4 * c + 4].reshape(4, 128, 256)),
        })
        in_maps.append(m)
    return in_maps


def assemble(R):
    y_prompt = np.zeros((4, 2048, D), np.float32)
    y_sample = np.zeros((32, 4, D), np.float32)
    ncp = np.zeros((1, 4, 2, 1024), np.float32)
    nkp = np.zeros((1, 4, 128, 4, 64), np.float32)
    nvp = np.zeros((1, 4, 128, 4, 64), np.float32)
    ncs = np.zeros((1, 32, 2, 1024), np.float32)
    nks = np.zeros((1, 32, 128, 4, 64), np.float32)
    nvs = np.zeros((1, 32, 128, 4, 64), np.float32)
    for c in range(NCORES):
        b, half = c // 2, c % 2
        r = R[c]
        y_prompt[b, half * 1024:(half + 1) * 1024] = r["yp"]
        y_sample[4 * c:4 * c + 4] = r["ys"].reshape(4, 4, D)
        ncs[0, 4 * c:4 * c + 4] = r["ocs"].reshape(4, 2, 1024)
        nks[0, 4 * c:4 * c + 4] = r["oks"].reshape(4, 128, 4, 64)
        nvs[0, 4 * c:4 * c + 4] = r["ovs"].reshape(4, 128, 4, 64)
        if half == 1:
            ncp[0, b] = r["ocp"]
            nkp[0, b] = r["okp"].reshape(128, 4, 64)
            nvp[0, b] = r["ovp"].reshape(128, 4, 64)
    return (y_prompt, y_sample, ncp, nkp, nvp, ncs, nks, nvs)
```
